# Optimizing a Trainium2 kernel written in Bass

```python
import jax, jax.numpy as jnp
from jax import lax
import numpy as np

D_MODEL = 1024
BATCH = 2
SEQ = 8192
DEPTH = 2
DEC_BATCH = 8
DEC_SEQ = 64
PAST_LEN = 4096

CHUNK = 64
Q_BLOCK = 128
ROPE_THETA = 500000.0
RMS_EPS = 1e-6
NEG_INF = -1e30

N_AB_LAYERS = (DEPTH + 1) // 2
N_C_LAYERS = DEPTH // 2

A_HEADS = 8
A_NOPE = 64
A_ROPE = 32
A_QK = A_NOPE + A_ROPE
A_V = 64
A_Q_RANK = 384
A_KV_RANK = 256
A_WIDTH = A_HEADS * A_V
A_SCALE = A_QK ** -0.5

B_HEADS = 8
B_HEAD_DIM = 64
B_WIDTH = B_HEADS * B_HEAD_DIM
B_PAST_CHUNKS = 8
B_BAND = B_PAST_CHUNKS * CHUNK
B_MAX_REL = 128
B_SCALE = B_HEAD_DIM ** -0.5

C_HEADS = 16
C_KV_HEADS = 2
C_GROUP = C_HEADS // C_KV_HEADS
C_HEAD_DIM = 64
C_WIDTH = C_HEADS * C_HEAD_DIM
C_WINDOW = 128
C_PAST_CHUNKS = C_WINDOW // CHUNK
C_ROT = C_HEAD_DIM // 4
C_SCALE = C_HEAD_DIM ** -0.5

AB_SIZES = (A_Q_RANK, A_KV_RANK, A_ROPE, A_WIDTH, B_WIDTH, B_WIDTH, B_WIDTH, B_WIDTH)
AB_IN = A_Q_RANK + A_KV_RANK + A_ROPE + A_WIDTH + 4 * B_WIDTH
C_SIZES = (C_WIDTH, C_KV_HEADS * C_HEAD_DIM, C_KV_HEADS * C_HEAD_DIM, C_WIDTH)
C_IN = 2 * C_WIDTH + 2 * C_KV_HEADS * C_HEAD_DIM

kernel_name = 'hybrid_chunk_stream_encoder_step'


def _split(z, sizes):
    idx, acc = [], 0
    for n in sizes[:-1]:
        acc += n
        idx.append(acc)
    return jnp.split(z, idx, axis=-1)


def _rms(x, g):
    xf = x.astype(jnp.float32)
    y = xf * lax.rsqrt(jnp.mean(xf * xf, axis=-1, keepdims=True) + RMS_EPS)
    return (y * g.astype(jnp.float32)).astype(x.dtype)


def _rope(x, pos, rot):
    half = rot // 2
    inv = jnp.power(ROPE_THETA, -jnp.arange(half, dtype=jnp.float32) * 2.0 / rot)
    ang = pos.astype(jnp.float32)[:, None] * inv[None, :]
    cos = jnp.cos(ang)[:, None, :].astype(x.dtype)
    sin = jnp.sin(ang)[:, None, :].astype(x.dtype)
    x1, x2 = x[..., :half], x[..., half:rot]
    return jnp.concatenate([x1 * cos - x2 * sin, x2 * cos + x1 * sin, x[..., rot:]], axis=-1)


def _grouped_attend(q, k, v, valid, scale, bias=None, sinks=None):
    s = jnp.einsum('...qhgd,...khd->...hgqk', q, k).astype(jnp.float32) * scale
    if bias is not None:
        s = s + bias.astype(jnp.float32)
    s = jnp.where(valid, s, NEG_INF)
    if sinks is not None:
        hk, g = q.shape[-3], q.shape[-2]
        sk = jnp.broadcast_to(sinks.astype(jnp.float32).reshape(hk, g, 1, 1), s.shape[:-1] + (1,))
        p = jax.nn.softmax(jnp.concatenate([s, sk], axis=-1), axis=-1)[..., :-1]
    else:
        p = jax.nn.softmax(s, axis=-1)
    return jnp.einsum('...hgqk,...khd->...qhgd', p.astype(v.dtype), v)


def _rel_bias(table, rel):
    idx = jnp.clip(rel, -B_MAX_REL, B_MAX_REL) + B_MAX_REL
    return jnp.expand_dims(jnp.moveaxis(table[:, idx], 0, -3), -3)


def _band_gather(k, n_past):
    b, s = k.shape[:2]
    nc = s // CHUNK
    kc = k.reshape((b, nc, CHUNK) + k.shape[2:])
    kp = jnp.pad(kc, [(0, 0), (n_past, 0)] + [(0, 0)] * (kc.ndim - 2))
    idx = jnp.arange(nc)[:, None] + jnp.arange(n_past + 1)[None, :]
    band = kp[:, idx]
    return band.reshape((b, nc, (n_past + 1) * CHUNK) + k.shape[2:])


def _band_prompt(q, k, v, n_past, scale, rel_table=None, sinks=None):
    b, s = q.shape[:2]
    nc = s // CHUNK
    band = (n_past + 1) * CHUNK
    qc = q.reshape((b, nc, CHUNK) + q.shape[2:])
    kb, vb = _band_gather(k, n_past), _band_gather(v, n_past)
    kpos = (jnp.arange(nc)[:, None] - n_past) * CHUNK + jnp.arange(band)[None, :]
    qpos = jnp.arange(nc)[:, None] * CHUNK + jnp.arange(CHUNK)[None, :]
    valid = (kpos >= 0)[:, None, None, None, :]
    bias = None
    if rel_table is not None:
        bias = _rel_bias(rel_table, qpos[:, :, None] - kpos[:, None, :])
    o = _grouped_attend(qc, kb, vb, valid, scale, bias, sinks)
    return o.reshape(b, s, -1)


def _band_sample(q, k_new, v_new, k_buf, v_buf, pos, scale, rel_table=None, sinks=None):
    b, t = q.shape[:2]
    w = k_buf.shape[1]
    k = jnp.concatenate([k_buf, k_new], axis=1)
    v = jnp.concatenate([v_buf, v_new], axis=1)
    kpos = jnp.concatenate([pos[0] - w + jnp.arange(w), pos])
    bias = None
    if rel_table is not None:
        bias = _rel_bias(rel_table, pos[:, None] - kpos[None, :])
    o = _grouped_attend(q, k, v, jnp.asarray(True), scale, bias, sinks)
    return o.reshape(b, t, -1), k[:, -w:], v[:, -w:]


def _mla_keys(c, kr, w_ukv):
    kv = jnp.einsum('bkr,rhe->bkhe', c, w_ukv.reshape(A_KV_RANK, A_HEADS, A_NOPE + A_V))
    k = jnp.concatenate([kv[..., :A_NOPE], jnp.broadcast_to(kr[:, :, None, :], kv.shape[:3] + (A_ROPE,))], axis=-1)
    return k, kv[..., A_NOPE:]


def _mla_prompt(q, k, v):
    b, s = q.shape[:2]
    kchunk = jnp.arange(s) // CHUNK

    def blk(i):
        qs = i * Q_BLOCK
        qb = lax.dynamic_slice_in_dim(q, qs, Q_BLOCK, axis=1)[:, :, :, None, :]
        qchunk = (qs + jnp.arange(Q_BLOCK)) // CHUNK
        mask = kchunk[None, :] <= qchunk[:, None]
        return _grouped_attend(qb, k, v, mask, A_SCALE)

    o = lax.map(blk, jnp.arange(s // Q_BLOCK))
    return jnp.moveaxis(o, 0, 1).reshape(b, s, A_WIDTH)


def _ab_layer(h, pos, cache, pre_g, post_g, w_in, q_norm, kv_norm, w_uq, w_ukv, rel_table, w_out):
    b, s, _ = h.shape
    xn = _rms(h, pre_g)
    q_lat, c_kv, k_r, g_a, q_b, k_b, v_b, g_b = _split(xn @ w_in, AB_SIZES)
    qa = (_rms(q_lat, q_norm) @ w_uq).reshape(b, s, A_HEADS, A_QK)
    qa = jnp.concatenate([qa[..., :A_NOPE], _rope(qa[..., A_NOPE:], pos, A_ROPE)], axis=-1)
    c_new = _rms(c_kv, kv_norm)
    kr_new = _rope(k_r[:, :, None, :], pos, A_ROPE)[:, :, 0]
    q_b = q_b.reshape(b, s, B_HEADS, 1, B_HEAD_DIM)
    k_b = k_b.reshape(b, s, B_HEADS, B_HEAD_DIM)
    v_b = v_b.reshape(b, s, B_HEADS, B_HEAD_DIM)
    if cache is None:
        ka, va = _mla_keys(c_new, kr_new, w_ukv)
        o_a = _mla_prompt(qa, ka, va)
        o_b = _band_prompt(q_b, k_b, v_b, B_PAST_CHUNKS, B_SCALE, rel_table=rel_table)
        wb = min(B_BAND, s)
        state = (c_new, kr_new, k_b[:, s - wb:], v_b[:, s - wb:])
    else:
        ckv_c, kr_c, bk_c, bv_c = cache
        ka, va = _mla_keys(jnp.concatenate([ckv_c, c_new], axis=1), jnp.concatenate([kr_c, kr_new], axis=1), w_ukv)
        o_a = _grouped_attend(qa[:, :, :, None, :], ka, va, jnp.asarray(True), A_SCALE).reshape(b, s, A_WIDTH)
        o_b, bk, bv = _band_sample(q_b, k_b, v_b, bk_c, bv_c, pos, B_SCALE, rel_table=rel_table)
        state = (c_new, kr_new, bk, bv)
    mixed = jnp.concatenate([o_a * jax.nn.silu(g_a), o_b * jax.nn.silu(g_b)], axis=-1)
    return h + _rms(mixed @ w_out, post_g), state


def _c_layer(h, pos, cache, pre_g, post_g, w_in, sinks, w_out):
    b, s, _ = h.shape
    xn = _rms(h, pre_g)
    q, k, v, g = _split(xn @ w_in, C_SIZES)
    q = _rope(q.reshape(b, s, C_HEADS, C_HEAD_DIM), pos, C_ROT).reshape(b, s, C_KV_HEADS, C_GROUP, C_HEAD_DIM)
    k = _rope(k.reshape(b, s, C_KV_HEADS, C_HEAD_DIM), pos, C_ROT)
    v = v.reshape(b, s, C_KV_HEADS, C_HEAD_DIM)
    if cache is None:
        o = _band_prompt(q, k, v, C_PAST_CHUNKS, C_SCALE, sinks=sinks)
        w = min(C_WINDOW, s)
        state = (k[:, s - w:], v[:, s - w:])
    else:
        o, kc, vc = _band_sample(q, k, v, cache[0], cache[1], pos, C_SCALE, sinks=sinks)
        state = (kc, vc)
    return h + _rms((o * jax.nn.silu(g)) @ w_out, post_g), state


def setup_inputs(seed: int = 0) -> dict:
    key = jax.random.key(seed)
    ks = jax.random.split(key, 22)
    f32 = jnp.float32

    def nrm(k, shape, scale=1.0):
        return jax.random.normal(k, shape, f32) * scale

    def gain(k, shape):
        return 1.0 + 0.02 * jax.random.normal(k, shape, f32)

    wb = min(B_BAND, PAST_LEN)
    wc = min(C_WINDOW, PAST_LEN)
    return {
        'x_prompt': nrm(ks[0], (BATCH, SEQ, D_MODEL)),
        'x_sample': nrm(ks[1], (DEC_BATCH, DEC_SEQ, D_MODEL)),
        'cache_a_ckv': nrm(ks[2], (N_AB_LAYERS, DEC_BATCH, PAST_LEN, A_KV_RANK)),
        'cache_a_krope': nrm(ks[3], (N_AB_LAYERS, DEC_BATCH, PAST_LEN, A_ROPE)),
        'cache_b_k': nrm(ks[4], (N_AB_LAYERS, DEC_BATCH, wb, B_HEADS, B_HEAD_DIM)),
        'cache_b_v': nrm(ks[5], (N_AB_LAYERS, DEC_BATCH, wb, B_HEADS, B_HEAD_DIM)),
        'cache_c_k': nrm(ks[6], (N_C_LAYERS, DEC_BATCH, wc, C_KV_HEADS, C_HEAD_DIM)),
        'cache_c_v': nrm(ks[7], (N_C_LAYERS, DEC_BATCH, wc, C_KV_HEADS, C_HEAD_DIM)),
        'ab_pre_norm': gain(ks[8], (N_AB_LAYERS, D_MODEL)),
        'ab_post_norm': gain(ks[9], (N_AB_LAYERS, D_MODEL)),
        'ab_w_in': nrm(ks[10], (N_AB_LAYERS, D_MODEL, AB_IN), D_MODEL ** -0.5),
        'ab_q_norm': gain(ks[11], (N_AB_LAYERS, A_Q_RANK)),
        'ab_kv_norm': gain(ks[12], (N_AB_LAYERS, A_KV_RANK)),
        'ab_w_uq': nrm(ks[13], (N_AB_LAYERS, A_Q_RANK, A_HEADS * A_QK), A_Q_RANK ** -0.5),
        'ab_w_ukv': nrm(ks[14], (N_AB_LAYERS, A_KV_RANK, A_HEADS * (A_NOPE + A_V)), A_KV_RANK ** -0.5),
        'ab_rel_bias': nrm(ks[15], (N_AB_LAYERS, B_HEADS, 2 * B_MAX_REL + 1), 0.1),
        'ab_w_out': nrm(ks[16], (N_AB_LAYERS, A_WIDTH + B_WIDTH, D_MODEL), (A_WIDTH + B_WIDTH) ** -0.5),
        'c_pre_norm': gain(ks[17], (N_C_LAYERS, D_MODEL)),
        'c_post_norm': gain(ks[18], (N_C_LAYERS, D_MODEL)),
        'c_w_in': nrm(ks[19], (N_C_LAYERS, D_MODEL, C_IN), D_MODEL ** -0.5),
        'c_sinks': nrm(ks[20], (N_C_LAYERS, C_HEADS), 0.5),
        'c_w_out': nrm(ks[21], (N_C_LAYERS, C_WIDTH, D_MODEL), C_WIDTH ** -0.5),
    }


def reference(x_prompt, x_sample, cache_a_ckv, cache_a_krope, cache_b_k, cache_b_v, cache_c_k, cache_c_v,
              ab_pre_norm, ab_post_norm, ab_w_in, ab_q_norm, ab_kv_norm, ab_w_uq, ab_w_ukv, ab_rel_bias, ab_w_out,
              c_pre_norm, c_post_norm, c_w_in, c_sinks, c_w_out):
    past_len = cache_a_ckv.shape[2]
    pos_p = jnp.arange(x_prompt.shape[1], dtype=jnp.int32)
    pos_s = past_len + jnp.arange(x_sample.shape[1], dtype=jnp.int32)
    hp, hs = x_prompt, x_sample
    ab_p, ab_s, c_p, c_s = [], [], [], []
    for layer in range(DEPTH):
        i = layer // 2
        if layer % 2 == 0:
            w = (ab_pre_norm[i], ab_post_norm[i], ab_w_in[i], ab_q_norm[i], ab_kv_norm[i],
                 ab_w_uq[i], ab_w_ukv[i], ab_rel_bias[i], ab_w_out[i])
            hp, st = _ab_layer(hp, pos_p, None, *w)
            ab_p.append(st)
            hs, st = _ab_layer(hs, pos_s, (cache_a_ckv[i], cache_a_krope[i], cache_b_k[i], cache_b_v[i]), *w)
            ab_s.append(st)
        else:
            w = (c_pre_norm[i], c_post_norm[i], c_w_in[i], c_sinks[i], c_w_out[i])
            hp, st = _c_layer(hp, pos_p, None, *w)
            c_p.append(st)
            hs, st = _c_layer(hs, pos_s, (cache_c_k[i], cache_c_v[i]), *w)
            c_s.append(st)

    def stk(states, j):
        return jnp.stack([st[j] for st in states])

    return (hp, hs,
            stk(ab_p, 0), stk(ab_p, 1), stk(ab_p, 2), stk(ab_p, 3), stk(c_p, 0), stk(c_p, 1),
            stk(ab_s, 0), stk(ab_s, 1), stk(ab_s, 2), stk(ab_s, 3), stk(c_s, 0), stk(c_s, 1))
```

```python
import os
import numpy as np
import concourse.bass as bass
import concourse.mybir as mybir
from concourse.bass_utils import run_bass_kernel_spmd

F32, BF16 = mybir.dt.float32, mybir.dt.bfloat16
AF = mybir.ActivationFunctionType
ALU = mybir.AluOpType

D = 1024
SEQ = 8192
PAST = 4096
CH = 64
EPS = 1e-6
THETA = 500000.0
A_SCALE = 96 ** -0.5
B_SCALE = 64 ** -0.5
C_SCALE = 64 ** -0.5
C_QLAT, C_CKV, C_KR, C_GA, C_QB, C_KB, C_VB, C_GB = 0, 384, 640, 672, 1184, 1696, 2208, 2720
C1_Q, C1_K, C1_V, C1_G = 0, 1024, 1152, 1280
NT = 17
T0 = 47
NQC = NT * 128


class Buf:
    __slots__ = ("name", "psum")

    def __init__(self, name, psum=False):
        self.name = name
        self.psum = psum


class Rec:
    ENGS = ("pe", "act", "dve", "pool", "sp")

    def __init__(self, nc, sems, dsems):
        self.nc = nc
        self.eng = {"pe": nc.tensor, "act": nc.scalar, "dve": nc.vector, "pool": nc.gpsimd, "sp": nc.sync}
        self.sem = sems
        self.tick = {e: 0 for e in self.ENGS}
        self.known = {e: {} for e in self.ENGS}
        self.dsems = dsems
        self.dcount = {q: 0 for q in dsems}
        self.items = []
        self.n_ins = 0
        self.reorder = os.environ.get("KREORDER", "1") == "1"

    def op(self, eng, fn, r=(), w=(), cost=300.0):
        self.items.append((eng, fn, tuple(r), tuple(w), False, cost))

    def dma(self, q, fn, r=(), w=(), cost=3000.0):
        self.items.append((q, fn, tuple(r), tuple(w), True, cost))

    def _schedule(self, items, alld, W=640):
        n = len(items)
        succ = [[] for _ in range(n)]
        left = [0] * n
        for i in range(n):
            left[i] = len(alld[i])
            for j in alld[i]:
                succ[j].append(i)
        rdy = [0.0] * n
        fin = [0.0] * n
        engfree = {}
        done = [False] * n
        order = []
        ready = [i for i in range(min(n, W)) if left[i] == 0]
        hi = min(n, W)
        lo = 0
        while len(order) < n:
            best, bt = -1, None
            for i in ready:
                t = max(rdy[i], engfree.get(items[i][0], 0.0))
                if bt is None or t < bt - 1e-9 or (abs(t - bt) <= 1e-9 and i < best):
                    best, bt = i, t
            i = best
            ready.remove(i)
            eng, isd, cost = items[i][0], items[i][4], items[i][5]
            if isd:
                engfree[eng] = bt + 60.0
                fin[i] = bt + cost
            else:
                engfree[eng] = bt + cost
                fin[i] = bt + cost
            done[i] = True
            order.append(i)
            for k in succ[i]:
                left[k] -= 1
                if fin[i] > rdy[k]:
                    rdy[k] = fin[i]
                if left[k] == 0 and k < hi:
                    ready.append(k)
            while lo < n and done[lo]:
                lo += 1
            nh = min(n, lo + W)
            while hi < nh:
                if left[hi] == 0 and not done[hi]:
                    ready.append(hi)
                hi += 1
        return order

    def _ensure(self, e, sem, val):
        k = self.known[e]
        if k.get(sem, 0) < val:
            self.eng[e].wait_ge(sem, val)
            k[sem] = val

    def flush(self):
        items = self.items
        n = len(items)
        last_w = {}
        readers = {}
        need = [None] * n
        inc = [False] * n
        alld = [None] * n
        for i, (eng, fn, R, W, isd, _c) in enumerate(items):
            d = {}
            for b in R:
                j = last_w.get(b)
                if j is not None:
                    d[j] = "raw"
                if b.psum:
                    for r_ in readers.get(b, ()):
                        if items[r_][0] != eng and r_ not in d:
                            d[r_] = "rar"
            for b in W:
                j = last_w.get(b)
                if j is not None and j not in d:
                    d[j] = "waw"
                for r_ in readers.get(b, ()):
                    if r_ != i and r_ not in d:
                        d[r_] = "war"
            lst = []
            for j, kind in d.items():
                ej, jd = items[j][0], items[j][4]
                if (not jd) and (not isd) and ej == eng:
                    if eng != "pool" and (kind != "raw" or eng == "pe"):
                        continue
                lst.append(j)
                inc[j] = True
            need[i] = lst
            alld[i] = list(d.keys())
            for b in R:
                readers.setdefault(b, []).append(i)
            for b in W:
                last_w[b] = i
                readers[b] = []
        order = self._schedule(items, alld) if (self.reorder and n > 2) else list(range(n))
        last_eng = {}
        for i in order:
            if not items[i][4]:
                last_eng[items[i][0]] = i
        for e, i in last_eng.items():
            inc[i] = True
        ev = [None] * n
        for i in order:
            eng, fn, R, W, isd, _c = items[i]
            for j in need[i]:
                s, v = ev[j]
                self._ensure(eng, s, v)
            if isd:
                m = self.dcount[eng]
                K = len(self.dsems[eng])
                s = self.dsems[eng][m % K]
                if m >= K:
                    self._ensure(eng, s, 16 * (m // K))
                ins = fn()
                ins.then_inc(s, 16)
                ev[i] = (s, 16 * (m // K + 1))
                self.dcount[eng] = m + 1
            else:
                ins = fn()
                if inc[i]:
                    self.tick[eng] += 1
                    ins.then_inc(self.sem[eng], 1)
                    ev[i] = (self.sem[eng], self.tick[eng])
            self.n_ins += 1
        self.items = []
        self.barrier()

    def barrier(self):
        for e in self.ENGS:
            for f in self.ENGS:
                if f != e and self.tick[f] > 0:
                    self._ensure(e, self.sem[f], self.tick[f])
            for q, lst in self.dsems.items():
                m = self.dcount[q]
                K = len(lst)
                for k_, s in enumerate(lst):
                    cnt = (m - k_ + K - 1) // K if m > k_ else 0
                    if cnt > 0:
                        self._ensure(e, s, 16 * cnt)


class Ring:
    def __init__(self, tiles, name, bufs=None, psum=False):
        self.tiles = tiles
        self.bufs = bufs if bufs is not None else [Buf(f"{name}{i}", psum) for i in range(len(tiles))]
        self.i = 0

    def next(self):
        k = self.i % len(self.tiles)
        self.i += 1
        return self.tiles[k], self.bufs[k]


def _rope_tables_fm(pos, rot, rows, row0s, scale):
    half = rot // 2
    inv = np.power(np.float32(THETA), -np.arange(half, dtype=np.float32) * np.float32(2.0) / np.float32(rot)).astype(np.float32)
    ang = pos.astype(np.float32)[None, :] * inv[:, None]
    cos = np.cos(ang).astype(np.float32)
    sin = np.sin(ang).astype(np.float32)
    n = len(pos)
    ct = np.full((rows, n), scale, np.float32)
    st = np.zeros((rows, n), np.float32)
    for row0 in row0s:
        ct[row0:row0 + half] = cos * scale
        ct[row0 + half:row0 + rot] = cos * scale
        st[row0:row0 + half] = sin * scale
        st[row0 + half:row0 + rot] = sin * scale
    return ct, st


def _perm_lhsT(rows, blocks):
    P = np.zeros((rows, rows), np.float32)
    for row0, rot in blocks:
        half = rot // 2
        for j in range(half):
            P[row0 + j + half, row0 + j] = -1.0
            P[row0 + j, row0 + j + half] = 1.0
    return P


def _pk(v, k):
    return np.ascontiguousarray(np.asarray(v, np.float32).reshape(k, 128).T)


def _wl(w, k):
    w = np.asarray(w, np.float32)
    return np.ascontiguousarray(w.reshape(k, 128, w.shape[1]).transpose(1, 0, 2))


def build_program(stages=99):
    from contextlib import ExitStack
    nc = bass.Bass("TRN2", target_bir_lowering=False)

    def din(name, shape):
        return nc.dram_tensor(name, list(shape), F32, kind="ExternalInput").ap()

    def dout(name, shape):
        return nc.dram_tensor(name, list(shape), F32, kind="ExternalOutput").ap()

    xw = din("xw", [8192, D]); xs_in = din("xs_in", [64, D])
    w_in = din("w_in", [128, 8, 3232]); g_pre = din("g_pre", [128, 8])
    w_uq = din("w_uq", [128, 3, 768]); g_q = din("g_q", [128, 3])
    w_ukv = din("w_ukv", [128, 2, 1024]); g_kv = din("g_kv", [128, 2])
    w_out0 = din("w_out0", [128, 8, D]); gb_post0 = din("gb_post0", [128, D])
    c_w_in = din("c_w_in", [128, 8, 2304]); g_pre1 = din("g_pre1", [128, 8])
    c_w_out = din("c_w_out", [128, 8, D]); gb_post1 = din("gb_post1", [128, D])
    cosk = din("cosk", [96, 8192]); sink = din("sink", [96, 8192])
    cosq = din("cosq", [96, 2560]); sinq = din("sinq", [96, 2560])
    cosk_s = din("cosk_s", [96, 64]); sink_s = din("sink_s", [96, 64])
    cosq_s = din("cosq_s", [96, 64]); sinq_s = din("sinq_s", [96, 64])
    cos1q = din("cos1q", [128, NQC]); sin1q = din("sin1q", [128, NQC])
    cos1k = din("cos1k", [128, NQC]); sin1k = din("sin1k", [128, NQC])
    cos1q_s = din("cos1q_s", [128, 64]); sin1q_s = din("sin1q_s", [128, 64])
    cos1k_s = din("cos1k_s", [128, 64]); sin1k_s = din("sin1k_s", [128, 64])
    pm96_d = din("pm96", [96, 96]); pm128_d = din("pm128", [128, 128]); ident_d = din("ident", [128, 128])
    valid_d = din("valid", [128, 64])
    bd_d = din("bd", [128, 2, 8, 128]); cb_d = din("cb", [128, 8])
    sinks_d = din("sinks", [128, 16])
    cache_ckv = din("cache_ckv", [PAST, 256]); cache_kr = din("cache_kr", [PAST, 32])
    cache_bk = din("cache_bk", [512, 512]); cache_bv = din("cache_bv", [512, 512])
    cache_ck = din("cache_ck", [128, 128]); cache_cv = din("cache_cv", [128, 128])

    y_p = dout("y_p", [2048, D]); y_s = dout("y_s", [64, D])
    ckv_p = dout("ckv_p", [2048, 256]); kr_p = dout("kr_p", [2048, 32])
    bk_p = dout("bk_p", [512, 512]); bv_p = dout("bv_p", [512, 512])
    ck_p = dout("ck_p", [128, 128]); cv_p = dout("cv_p", [128, 128])
    ckv_s = dout("ckv_s", [64, 256]); kr_s = dout("kr_s", [64, 32])
    bk_s = dout("bk_s", [512, 512]); bv_s = dout("bv_s", [512, 512])
    ck_s = dout("ck_s", [128, 128]); cv_s = dout("cv_s", [128, 128])

    h1d = nc.dram_tensor("h1_scratch", [NQC + 64, D], F32, kind="Internal").ap()
    top = ExitStack()
    sems = {e: top.enter_context(nc.semaphore("sem_" + e)) for e in Rec.ENGS}
    dsems = {q: [top.enter_context(nc.semaphore(f"dsem_{q}{i}")) for i in range(8)] for q in ("sp", "pool")}
    rec = Rec(nc, sems, dsems)
    E = rec.eng

    uid = {"n": 0}

    def sb(es, name, shape, dt):
        uid["n"] += 1
        return es.enter_context(nc.sbuf_tensor(f"s{uid['n']}_{name}", list(shape), dt))

    def psum(es, name, shape, dt):
        uid["n"] += 1
        return es.enter_context(nc.psum_tensor(f"p{uid['n']}_{name}", list(shape), dt))

    def fsz(ap):
        n = 1
        for d_ in list(ap.shape)[1:]:
            n *= int(d_)
        return n

    def ecost(eng, ap):
        f = fsz(ap)
        if eng == "act":
            return f / 1.2 + 200.0
        if eng == "dve":
            return f / 0.96 + 120.0
        return f / 0.45 + 150.0

    def mm(out, lhsT, rhs, start, stop, r, w, sgc=False):
        f32 = rhs.dtype == F32
        rec.op("pe", lambda: nc.tensor.matmul(out, lhsT=lhsT, rhs=rhs, start=start, stop=stop, skip_group_check=sgc), r, w,
               cost=(max(64, fsz(rhs)) / 2.4) * (4 if f32 else 1) + 70.0)

    def tr(out, in_, ident, r, w):
        rec.op("pe", lambda: nc.tensor.transpose(out, in_, ident), r, w, cost=120.0)

    def act(out, in_, func, r, w, **kw):
        rec.op("act", lambda: nc.scalar.activation(out=out, in_=in_, func=func, **kw), r, w, cost=ecost("act", out))

    def cp(eng, out, in_, r, w):
        if eng == "act":
            rec.op("act", lambda: nc.scalar.copy(out=out, in_=in_), r, w, cost=ecost("act", out))
        else:
            rec.op(eng, lambda: E[eng].tensor_copy(out=out, in_=in_), r, w, cost=ecost(eng, out))

    def tt(eng, out, in0, in1, op, r, w):
        rec.op(eng, lambda: E[eng].tensor_tensor(out=out, in0=in0, in1=in1, op=op), r, w, cost=ecost(eng, out))

    def ts(eng, out, in0, s1, op0, r, w):
        rec.op(eng, lambda: E[eng].tensor_scalar(out=out, in0=in0, scalar1=s1, scalar2=None, op0=op0), r, w, cost=ecost(eng, out))

    def stt(eng, out, in0, scalar, in1, op0, op1, r, w):
        rec.op(eng, lambda: E[eng].scalar_tensor_tensor(out=out, in0=in0, scalar=scalar, in1=in1, op0=op0, op1=op1), r, w, cost=ecost(eng, out))

    def recip(out, in_, r, w):
        rec.op("dve", lambda: nc.vector.reciprocal(out=out, in_=in_), r, w, cost=5 * ecost("dve", out))

    def memset(eng, ap, val, w):
        rec.op(eng, lambda: E[eng].memset(ap, val), [], w, cost=ecost(eng, ap))

    def dma(q, out, in_, r, w):
        nb = 1
        for d_ in list(out.shape):
            nb *= int(d_)
        rec.dma(q, lambda: E[q].dma_start(out=out, in_=in_), r, w, cost=2500.0 + nb * 4 / 120.0)

    alt = {"i": 0}

    def alt_eng(choices=("dve", "act")):
        alt["i"] += 1
        return choices[alt["i"] % len(choices)]

    identb = sb(top, "identb", [128, 128], BF16); identf = sb(top, "identf", [128, 128], F32)
    onesb = sb(top, "onesb", [128, 128], BF16); onesf = sb(top, "onesf", [128, 128], F32)
    pm96b = sb(top, "pm96b", [128, 96], BF16); pm128b = sb(top, "pm128b", [128, 128], BF16)
    valid = sb(top, "valid", [128, 64], F32); vone = sb(top, "vone", [128, 64], F32)
    QT_s = sb(top, "QT_s", [128, 8, 64], BF16)
    sga_s = sb(top, "sga_s", [128, 4, 64], BF16)
    mixb_s = sb(top, "mixb_s", [128, 4, 64], BF16)
    cs_new = sb(top, "cs_new", [128, 2, 64], BF16)
    krs_new = sb(top, "krs_new", [128, 64], BF16)
    CONST = Buf("const")
    bQT, bsga, bmixb = Buf("QT"), Buf("sga"), Buf("mixb")
    bQTs, bsgas, bmixbs, bcsn, bkrsn = Buf("QTs"), Buf("sgas"), Buf("mixbs"), Buf("csn"), Buf("krsn")

    with ExitStack() as es:
        stg = sb(es, "cstg", [128, 3, 128], F32)
        dma("sp", identf[:], ident_d[:, :], [], [CONST])
        dma("sp", stg[0:96, 0, 0:96], pm96_d[:, :], [], [CONST])
        dma("sp", stg[:, 1, :], pm128_d[:, :], [], [CONST])
        dma("sp", valid[:], valid_d[:, :], [], [CONST])
        cp("dve", identb[:], identf[:], [CONST], [CONST])
        cp("dve", pm96b[0:96, :], stg[0:96, 0, 0:96], [CONST], [CONST])
        cp("dve", pm128b[:], stg[:, 1, :], [CONST], [CONST])
        memset("dve", onesb[:], 1.0, [CONST]); memset("dve", onesf[:], 1.0, [CONST]); memset("dve", vone[:], 1.0, [CONST])
        rec.flush()

    def prep_w(es_ring, dst, src, K, C, gain, wbuf, c_lo=0, c_hi=None):
        c_hi = C if c_hi is None else c_hi
        CP = 1664
        for k in range(K):
            c = c_lo
            while c < c_hi:
                n = min(CP, c_hi - c)
                st_, sbf = es_ring.next()
                dma("sp" if alt["i"] % 4 < 2 else "pool", st_[:, :n], src[:, k, c:c + n], [], [sbf])
                e = alt_eng(("dve", "act"))
                o = dst[:, k, c - c_lo:c - c_lo + n]
                if gain is None:
                    cp(e, o, st_[:, :n], [sbf], [wbuf])
                elif e == "act":
                    act(o, st_[:, :n], AF.Copy, [sbf, CONST], [wbuf], scale=gain[:, k:k + 1])
                else:
                    ts(e, o, st_[:, :n], gain[:, k:k + 1], ALU.mult, [sbf, CONST], [wbuf])
                c += n

    def frontend(F, tiles, xnT, xnb):
        for ti, (src, rows, srcb) in enumerate(tiles):
            if srcb is None:
                xt, xb = F["xring"].next()
                dma("sp", xt[:rows, :], src, [], [xb])
            else:
                xt, xb = src, srcb
            st_, stb = F["string"].next()
            xs, xsb = F["xsring"].next()
            if F.get("junk") is None:
                act(xs[:rows, :], xt[:rows, :], AF.Square, [xb], [xsb, stb], accum_out=st_[:rows, 0:1])
            else:
                act(F["junk"][:rows, :], xt[:rows, :], AF.Square, [xb], [F["junkb"], stb], accum_out=st_[:rows, 0:1])
            act(st_[:rows, 1:2], st_[:rows, 0:1], AF.Ln, [stb], [stb], scale=1.0 / D, bias=EPS)
            act(st_[:rows, 2:3], st_[:rows, 1:2], AF.Exp, [stb], [stb], scale=-0.5)
            ts("dve", xs[:rows, :], xt[:rows, :], st_[:rows, 2:3], ALU.mult, [xb, stb], [xsb])
            pT, pTb = F["pT"].next()
            for k in range(8):
                tr(pT[:, k, :rows], xs[:rows, k * 128:(k + 1) * 128], identb[:rows, :rows], [xsb, CONST], [pTb])
            cp(alt_eng(("dve", "act")), xnT[:, :, ti * 128:ti * 128 + rows], pT[:, :, :rows], [pTb], [xnb])

    def proj(ps, W, Wb, c0, ncols, xnT, xnb, t0, n, psb):
        for k in range(8):
            mm(ps[0:ncols, 0:n], W[:, k, c0:c0 + ncols], xnT[:, k, t0:t0 + n], k == 0, k == 7, [Wb, xnb], [psb])

    def rms_fm(F, chunks, n, nfeat):
        sq, sqb = F["sqring"].next()
        for c, (ps, psb) in enumerate(chunks):
            act(sq[:, c, :n], ps[:, :n], AF.Square, [psb], [sqb])
        pss, pssb = F["psring"].next()
        for c in range(len(chunks)):
            mm(pss[:, :n], onesb[:, :], sq[:, c, :n], c == 0, c == len(chunks) - 1, [sqb, CONST], [pssb])
        rs, rsb = F["rsring"].next()
        act(rs[:, :n], pss[:, :n], AF.Ln, [pssb], [rsb], scale=1.0 / nfeat, bias=EPS)
        act(rs[:, :n], rs[:, :n], AF.Exp, [rsb], [rsb], scale=-0.5)
        return rs, rsb

    def rope_fm(F, ps, psb, R, p0, p1, n, pm, cosT, sinT, tb, out_bf, outb, out_f32=None, outfb=None):
        qr, qrb = F["qrring"].next()
        cp("act", qr[0:R, :n], ps[0:R, :n], [psb], [qrb])
        ps2, ps2b = F["psring"].next()
        mm(ps2[0:R, :n], pm[0:R, 0:R], qr[0:R, :n], True, True, [qrb, CONST], [ps2b])
        t1, t1b = F["t1ring"].next()
        t2, t2b = F["t2ring"].next()
        tt("dve", t1[p0:p1, :n], ps[p0:p1, :n], cosT, ALU.mult, [psb, tb], [t1b])
        tt("dve", t2[p0:p1, :n], ps2[p0:p1, :n], sinT, ALU.mult, [ps2b, tb], [t2b])
        if out_f32 is not None:
            tt("pool", out_f32, t1[p0:p1, :n], t2[p0:p1, :n], ALU.add, [t1b, t2b], [outfb])
            cp("pool", out_bf, out_f32, [outfb], [outb])
        else:
            tt("pool", out_bf, t1[p0:p1, :n], t2[p0:p1, :n], ALU.add, [t1b, t2b], [outb])

    def silu_fm(F, ps, psb, n, dst, dstb):
        e1, e1b = F["t1ring"].next()
        act(e1[:, :n], ps[:, :n], AF.Exp, [psb], [e1b], scale=-1.0)
        act(e1[:, :n], e1[:, :n], AF.Ln, [e1b], [e1b], bias=1.0)
        act(e1[:, :n], e1[:, :n], AF.Exp, [e1b], [e1b], scale=-1.0)
        tt("dve", dst, ps[:, :n], e1[:, :n], ALU.mult, [psb, e1b], [dstb])

    def norm_out(F, po_views, pob, nq, sg_fn, out_fn, extra_l=None):
        for hf, (po, pb_) in enumerate(zip(po_views, pob)):
            for par in range(2):
                p0 = 64 * par
                Rs, Rsb_ = F["Rring"].next()
                Rv = Rs[p0:p0 + 64, :].rearrange("p (a q) -> p a q", a=2)[:, :, :nq]
                act(Rv, po[64:128, :, par, :nq], AF.Ln, [pb_], [Rsb_], bias=1e-30)
                act(Rv, Rv, AF.Exp, [Rsb_], [Rsb_], scale=-1.0)
                sg, sgb_ = sg_fn(hf, par)
                o, ob = out_fn(hf, par)
                tm, tmb = F["tmring"].next()
                tv = tm[p0:p0 + 64, :].rearrange("p (a q) -> p a q", a=2)[:, :, :nq]
                tt("dve", tv, Rv, sg, ALU.mult, [Rsb_, sgb_], [tmb])
                tt("dve", o, po[0:64, :, par, :nq], tv, ALU.mult, [pb_, tmb], [ob])

    def stage_b2():
        with ExitStack() as es:
            Wp = sb(es, "Wb2", [128, 8, 2048], BF16); Wb = Buf("Wb2")
            gp = sb(es, "gp", [128, 8], F32)
            dma("sp", gp[:], g_pre[:, :], [], [CONST])
            bdb = sb(es, "bdb", [128, 2, 8, 128], BF16); bbd = Buf("bd")
            with ExitStack() as es2:
                ring = Ring([sb(es2, f"stgw{i}", [128, 1664], F32) for i in range(4)], "stgw")
                prep_w(ring, Wp, w_in, 8, 3232, gp, Wb, c_lo=C_QB, c_hi=3232)
                bdf = sb(es2, "bdf", [128, 2, 8, 128], F32); cbt = sb(es2, "cbt", [128, 8], F32)
                dma("sp", bdf[:], bd_d[:, :, :, :], [], [bbd]); dma("sp", cbt[:], cb_d[:, :], [], [bbd])
                rec.op("dve", lambda: nc.vector.tensor_scalar(out=cbt[:], in0=cbt[:], scalar1=-1.0, scalar2=None, op0=ALU.mult), [bbd], [bbd])
                for d_ in range(2):
                    for h in range(8):
                        act(bdb[:, d_, h, :], bdf[:, d_, h, :], AF.Exp, [bbd], [bbd], bias=cbt[:, h:h + 1])
                rec.flush()
            O_QB, O_KB, O_VB, O_GB = 0, 512, 1024, 1536
            F = {}
            F["xring"] = Ring([sb(es, f"xr{i}", [128, D], F32) for i in range(3)], "xr")
            F["xsring"] = Ring([sb(es, f"xs{i}", [128, D], BF16) for i in range(2)], "xs")
            F["junk"] = sb(es, "junk", [128, D], BF16); F["junkb"] = Buf("junk")
            F["string"] = Ring([sb(es, f"st{i}", [128, 4], F32) for i in range(4)], "st")
            F["pT"] = Ring([psum(es, "pT0", [128, 8, 128], BF16)], "pT", psum=True)
            ps2 = [psum(es, f"ps2_{i}", [128, 1024], F32) for i in range(2)]
            ps1 = [psum(es, f"ps1_{i}", [128, 512], F32) for i in range(3)]
            views = [ps2[0][:, 0:512], ps2[0][:, 512:1024], ps2[1][:, 0:512], ps2[1][:, 512:1024]] + [p[:, :] for p in ps1]
            F["psring"] = Ring(views, "psv", psum=True)
            vb_ = F["psring"].bufs
            F["prring"] = Ring([ps1[2][:, :]], "pr", [vb_[6]])
            F["t1ring"] = Ring([sb(es, f"t1_{i}", [128, 512], F32) for i in range(2)], "t1")
            F["Rring"] = Ring([sb(es, f"Rs{i}", [128, 512], F32) for i in range(2)], "Rs")
            F["tmring"] = Ring([sb(es, f"tm{i}", [128, 256], F32) for i in range(2)], "tm")
            xnr = Ring([sb(es, f"xnT{i}", [128, 8, 512], BF16) for i in range(2)], "xnT")
            sgbq = sb(es, "sgbq", [128, 4, 512], BF16); bsgbq = Buf("sgbq")
            qbq = sb(es, "qbq", [128, 4, 512], BF16); bqbq = Buf("qbq")
            kbT = sb(es, "kbT", [128, 4, 12 * 128], BF16)
            vb = sb(es, "vb", [128, 12, 8, 128], BF16)
            bslot = [Buf(f"kvslot{i}") for i in range(12)]
            ptr = Ring([sb(es, f"pTs{i}", [128, 640], BF16) for i in range(4)], "pTs")
            ostg = Ring([sb(es, f"ostg{i}", [128, 512], F32) for i in range(2)], "ostg")
            kbT_s = sb(es, "kbT_s", [128, 4, 640], BF16); vb_s = sb(es, "vb_s", [128, 5, 8, 128], BF16); bkvs = Buf("kvs")

            def b_attention(nq, qT, qTb, qc0, keyT, vaug, kbufs, nks, sg_fn, out_fn):
                po = [ps1[0], ps1[1]]
                pob = [vb_[4], vb_[5]]
                for hp in range(4):
                    for j in range(5):
                        for h in (2 * hp, 2 * hp + 1):
                            c, pb = h // 2, (h % 2) * 64
                            psS = ps2[h % 2]; psSb = [vb_[2 * (h % 2)], vb_[2 * (h % 2) + 1]]
                            nk = nks[j]
                            o = psS[0:nk, j * 128:j * 128 + nq]
                            wb_ = [psSb[0] if j < 4 else psSb[1]]
                            mm(o, keyT(j, c, pb), qT[pb:pb + 64, c, qc0:qc0 + nq], True, True, [kbufs[j], qTb], wb_)
                    for h in (2 * hp, 2 * hp + 1):
                        psS = ps2[h % 2]; psSb = [vb_[2 * (h % 2)], vb_[2 * (h % 2) + 1]]
                        pt, ptb = ptr.next()
                        act(pt[:, 0:512].rearrange("p (j q) -> p j q", j=4)[:, :, 0:nq], psS[:, 0:512].rearrange("p (j q) -> p j q", j=4)[:, :, 0:nq], AF.Exp, [psSb[0]], [ptb])
                        act(pt[0:nks[4], 512:512 + nq], psS[0:nks[4], 512:512 + nq], AF.Exp, [psSb[1]], [ptb])
                        ov = po[h // 4][0:128, :].rearrange("p (a b q) -> p a b q", a=2, b=2)[:, (h % 4) // 2, h % 2, :]
                        ob = [pob[h // 4]]
                        tt("pool", pt[:, 384:384 + nq], pt[:, 384:384 + nq], bdb[:, 1, h, 0:nq], ALU.mult, [ptb, bbd], [ptb])
                        tt("pool", pt[0:nks[4], 512:512 + nq], pt[0:nks[4], 512:512 + nq], bdb[0:nks[4], 0, h, 0:nq], ALU.mult, [ptb, bbd], [ptb])
                        if nq > 64:
                            memset("pool", pt[0:64, 64:128], 0.0, [ptb])
                            memset("pool", pt[64:128, 512:576], 0.0, [ptb])
                        for j in range(5):
                            mm(ov[:, 0:nq], vaug(j, h, 0, nks[j]), pt[0:nks[j], j * 128:j * 128 + nq], j == 0, j == 4, [kbufs[j], ptb], ob)
                pov = [p[:, :].rearrange("p (a b q) -> p a b q", a=2, b=2) for p in po]
                norm_out(F, pov, pob, nq, sg_fn, out_fn)

            def quad(tiles, wt0, full_from, is_sample):
                N = sum(r for _, r, _ in tiles)
                xnT, xnb = xnr.next()
                frontend(F, tiles, xnT, xnb)
                for c in range(4):
                    ps, psb = F["psring"].next()
                    proj(ps, Wp, Wb, O_KB + c * 128, 128, xnT, xnb, 0, N, psb)
                    if is_sample:
                        cp(alt_eng(), kbT_s[:, c, 512:512 + N], ps[:, :N], [psb], [bkvs])
                    else:
                        s0 = (wt0 - 40) % 12
                        cp(alt_eng(), kbT[:, c, s0 * 128:s0 * 128 + N], ps[:, :N], [psb], bslot[s0:s0 + 4])
                want_out = is_sample or wt0 == 60
                for ti, (_, rows, _) in enumerate(tiles):
                    ps, psb = F["psring"].next()
                    for k in range(8):
                        mm(ps[0:rows, :], xnT[:, k, ti * 128:ti * 128 + rows], Wp[:, k, O_VB:O_VB + 512], k == 0, k == 7, [xnb, Wb], [psb])
                    pv = ps[0:rows, :].rearrange("p (h e) -> p h e", h=8)
                    if is_sample:
                        cp("dve", vb_s[0:rows, 4, :, 0:64], pv, [psb], [bkvs])
                    else:
                        s = (wt0 + ti - 40) % 12
                        rec.op("dve", lambda s=s, pv=pv, t=wt0 + ti: nc.vector.tensor_scalar(out=vb[:, s, :, 0:64], in0=pv, scalar1=valid[:, t:t + 1], scalar2=None, op0=ALU.mult), [psb, CONST], [bslot[s]])
                        cp("act", vb[:, s, :, 64:128], valid[:, wt0 + ti:wt0 + ti + 1].unsqueeze(2).to_broadcast([128, 8, 64]), [CONST], [bslot[s]])
                    if want_out:
                        og, ogb = ostg.next()
                        cp("act", og[0:rows, :], ps[0:rows, :], [psb], [ogb])
                        dst = bv_s[448:512, :] if is_sample else bv_p[ti * 128:ti * 128 + 128, :]
                        dma("pool", dst, og[0:rows, :], [ogb], [])
                        ps, psb = F["psring"].next()
                        for k in range(8):
                            mm(ps[0:rows, :], xnT[:, k, ti * 128:ti * 128 + rows], Wp[:, k, O_KB:O_KB + 512], k == 0, k == 7, [xnb, Wb], [psb])
                        og, ogb = ostg.next()
                        cp("act", og[0:rows, :], ps[0:rows, :], [psb], [ogb])
                        dst = bk_s[448:512, :] if is_sample else bk_p[ti * 128:ti * 128 + 128, :]
                        dma("pool", dst, og[0:rows, :], [ogb], [])
                if full_from is None:
                    return
                f0 = full_from
                n = N - f0
                for c in range(4):
                    ps, psb = F["psring"].next()
                    proj(ps, Wp, Wb, O_QB + c * 128, 128, xnT, xnb, f0, n, psb)
                    act(qbq[:, c, 0:n], ps[:, :n], AF.Copy, [psb], [bqbq], scale=B_SCALE)
                    ps, psb = F["psring"].next()
                    proj(ps, Wp, Wb, O_GB + c * 128, 128, xnT, xnb, f0, n, psb)
                    silu_fm(F, ps, psb, n, sgbq[:, c, 0:n], bsgbq)
                if is_sample:
                    def keyT(j, c, pb):
                        return kbT_s[pb:pb + 64, c, j * 128:j * 128 + (128 if j < 4 else 64)]

                    def vaug(j, h, k0, k1):
                        return vb_s[k0:k1, j, h, :]
                    b_attention(64, qbq, bqbq, 0, keyT, vaug, [bkvs] * 5, [128, 128, 128, 128, 64],
                                lambda hf, par: (sgbq[64 * par:64 * par + 64, 2 * hf:2 * hf + 2, 0:64], bsgbq),
                                lambda hf, par: (mixb_s[64 * par:64 * par + 64, 2 * hf:2 * hf + 2, 0:64], bmixbs))
                else:
                    for ti in range(f0 // 128, len(tiles)):
                        t = wt0 + ti
                        qc = ti * 128 - f0
                        oc = (t - T0) * 128
                        slots = [(t - 4 + j - 40) % 12 for j in range(5)]

                        def keyT(j, c, pb, slots=slots):
                            return kbT[pb:pb + 64, c, slots[j] * 128:slots[j] * 128 + 128]

                        def vaug(j, h, k0, k1, slots=slots):
                            return vb[k0:k1, slots[j], h, :]
                        b_attention(128, qbq, bqbq, qc, keyT, vaug, [bslot[s] for s in slots], [128] * 5,
                                    lambda hf, par, qc=qc: (sgbq[64 * par:64 * par + 64, 2 * hf:2 * hf + 2, qc:qc + 128], bsgbq),
                                    lambda hf, par, oc=oc: (mixb[64 * par:64 * par + 64, 2 * hf:2 * hf + 2, oc:oc + 128], bmixb))

            with ExitStack() as es2:
                cst = Ring([sb(es2, f"cst{i}", [128, 512], F32) for i in range(2)], "cst")
                cbf = Ring([sb(es2, f"cbf{i}", [128, 512], BF16) for i in range(2)], "cbf")
                for j in range(4):
                    st_, stb = cst.next()
                    dma("sp", st_[:], cache_bk[j * 128:(j + 1) * 128, :], [], [stb])
                    bf, bfb = cbf.next()
                    cp("dve", bf[:], st_[:], [stb], [bfb])
                    pT, pTb = F["pT"].next()
                    for c in range(4):
                        tr(pT[:, c, :], bf[:, c * 128:(c + 1) * 128], identb[:, :], [bfb, CONST], [pTb])
                    cp("act", kbT_s[:, :, j * 128:(j + 1) * 128], pT[:, 0:4, :], [pTb], [bkvs])
                    st_, stb = cst.next()
                    dma("sp", st_[:], cache_bv[j * 128:(j + 1) * 128, :], [], [stb])
                    cp("dve", vb_s[:, j, :, 0:64], st_[:].rearrange("p (h e) -> p h e", h=8), [stb], [bkvs])
                memset("pool", vb_s[:, :, :, 64:128], 1.0, [bkvs])
                dma("pool", bk_s[0:448, :], cache_bk[64:512, :], [], [])
                dma("pool", bv_s[0:448, :], cache_bv[64:512, :], [], [])
                quad([(xs_in[:, :], 64, None)], None, 0, True)
                rec.flush()
            for q in range(10, 16):
                wt0 = q * 4
                tiles = [(xw[(wt0 + i) * 128:(wt0 + i + 1) * 128, :], 128, None) for i in range(4)]
                full_from = None if q == 10 else (384 if q == 11 else 0)
                quad(tiles, wt0, full_from, False)
            rec.flush()

    def stage_b1():
        with ExitStack() as es:
            Wp = sb(es, "Wb1", [128, 8, 896], BF16); Wb = Buf("Wb1")
            wuq = sb(es, "wuq", [128, 3, 768], BF16); wuqb = Buf("wuq")
            gp = sb(es, "gp1", [128, 8], F32); gq = sb(es, "gq1", [128, 3], F32)
            dma("sp", gp[:], g_pre[:, :], [], [CONST]); dma("sp", gq[:], g_q[:, :], [], [CONST])
            with ExitStack() as es2:
                ring = Ring([sb(es2, f"stgw{i}", [128, 1664], F32) for i in range(4)], "stgw")
                prep_w(ring, Wp[:, :, 0:384], w_in, 8, 3232, gp, Wb, c_lo=0, c_hi=384)
                prep_w(ring, Wp[:, :, 384:896], w_in, 8, 3232, gp, Wb, c_lo=C_GA, c_hi=C_GA + 512)
                prep_w(ring, wuq, w_uq, 3, 768, gq, wuqb)
                rec.flush()
            F = {}
            F["xring"] = Ring([sb(es, f"xr{i}", [128, D], F32) for i in range(4)], "xr")
            F["xsring"] = Ring([sb(es, f"xs{i}", [128, D], BF16) for i in range(3)], "xs")
            F["junk"] = sb(es, "junk", [128, D], BF16); F["junkb"] = Buf("junk")
            F["string"] = Ring([sb(es, f"st{i}", [128, 4], F32) for i in range(4)], "st")
            F["pT"] = Ring([psum(es, f"pT{i}", [128, 8, 128], BF16) for i in range(2)], "pT", psum=True)
            F["psring"] = Ring([psum(es, f"psb1_{i}", [128, 512], F32) for i in range(6)], "psv", psum=True)
            F["t1ring"] = Ring([sb(es, f"t1_{i}", [128, 512], F32) for i in range(3)], "t1")
            F["t2ring"] = Ring([sb(es, f"t2_{i}", [128, 512], F32) for i in range(2)], "t2")
            F["sqring"] = Ring([sb(es, f"sq{i}", [128, 3, 512], BF16) for i in range(2)], "sq")
            F["rsring"] = Ring([sb(es, f"rs{i}", [128, 512], F32) for i in range(2)], "rs")
            F["qrring"] = Ring([sb(es, f"qr{i}", [128, 512], BF16) for i in range(2)], "qr")
            xnr = Ring([sb(es, f"xnT{i}", [128, 8, 512], BF16) for i in range(2)], "xnT")
            qln = sb(es, "qln", [128, 3, 512], BF16); qlnb = Buf("qln")
            tab = Ring([sb(es, f"tab{i}", [128, 2, 512], F32) for i in range(2)], "tab")

            def quad(tiles, f0, cs_ap, sn_ap, QTd, QTb_, qc, sgd, sgb_):
                N = sum(r for _, r, _ in tiles)
                n = N - f0
                xnT, xnb = xnr.next()
                frontend(F, tiles, xnT, xnb)
                tb_, tbb = tab.next()
                dma("pool", tb_[0:96, 0, 0:n], cs_ap, [], [tbb]); dma("pool", tb_[0:96, 1, 0:n], sn_ap, [], [tbb])
                chunks = []
                for c in range(3):
                    ps, psb = F["psring"].next()
                    proj(ps, Wp, Wb, c * 128, 128, xnT, xnb, f0, n, psb)
                    chunks.append((ps, psb))
                rs, rsb = rms_fm(F, chunks, n, 384)
                for c, (ps, psb) in enumerate(chunks):
                    tt("dve", qln[:, c, 0:n], ps[:, :n], rs[:, :n], ALU.mult, [psb, rsb], [qlnb])
                for h in range(8):
                    ps, psb = F["psring"].next()
                    for c in range(3):
                        mm(ps[0:96, 0:n], wuq[:, c, h * 96:(h + 1) * 96], qln[:, c, 0:n], c == 0, c == 2, [wuqb, qlnb], [psb])
                    rope_fm(F, ps, psb, 96, 0, 96, n, pm96b, tb_[0:96, 0, 0:n], tb_[0:96, 1, 0:n], tbb,
                            QTd[0:96, h, qc:qc + n], QTb_)
                for c in range(4):
                    ps, psb = F["psring"].next()
                    proj(ps, Wp, Wb, 384 + c * 128, 128, xnT, xnb, f0, n, psb)
                    silu_fm(F, ps, psb, n, sgd[:, c, qc:qc + n], sgb_)

            quad([(xs_in[:, :], 64, None)], 0, cosq_s[:, :], sinq_s[:, :], QT_s, bQTs, 0, sga_s, bsgas)
            for q in range(11, 16):
                wt0 = q * 4
                tiles = [(xw[(wt0 + i) * 128:(wt0 + i + 1) * 128, :], 128, None) for i in range(4)]
                f0 = 384 if q == 11 else 0
                tc0 = (wt0 - 44) * 128 + f0
                qc = (wt0 - T0) * 128 + f0
                quad(tiles, f0, cosq[:, tc0:tc0 + 512 - f0], sinq[:, tc0:tc0 + 512 - f0], QT, bQT, qc, sga, bsga)
            rec.flush()

    def stage_am():
        with ExitStack() as es:
            cT = sb(es, "cT", [128, 2, 8192], BF16)
            KT = sb(es, "KT", [128, 8192], BF16)
            bcT = [Buf(f"cT{i}") for i in range(16)]
            bKr = [Buf(f"KTr{i}") for i in range(16)]
            bKn = [Buf(f"KTn{i}") for i in range(16)]
            wukv = sb(es, "wukv", [128, 2, 1024], BF16); wukvb = Buf("wukv")
            gkv = sb(es, "gkv", [128, 2], F32)
            dma("sp", gkv[:], g_kv[:, :], [], [CONST])
            with ExitStack() as esA:
                Wl = sb(esA, "Wl", [128, 8, 256], BF16); Wlb = Buf("Wl")
                Wk = sb(esA, "Wk", [128, 8, 96], BF16); Wkb = Buf("Wk")
                gp = sb(esA, "gpA", [128, 8], F32)
                dma("sp", gp[:], g_pre[:, :], [], [CONST])
                memset("pool", Wk[:], 0.0, [Wkb])
                with ExitStack() as es2:
                    ring = Ring([sb(es2, f"stgw{i}", [128, 1664], F32) for i in range(4)], "stgw")
                    prep_w(ring, Wl, w_in, 8, 3232, gp, Wlb, c_lo=C_CKV, c_hi=C_CKV + 256)
                    prep_w(ring, Wk[:, :, 64:96], w_in, 8, 3232, gp, Wkb, c_lo=C_KR, c_hi=C_KR + 32)
                    prep_w(ring, wukv, w_ukv, 2, 1024, None, wukvb)
                    rec.flush()
                F = {}
                F["xring"] = Ring([sb(esA, f"xr{i}", [128, D], F32) for i in range(3)], "xr")
                F["xsring"] = Ring([sb(esA, f"xs{i}", [128, D], BF16) for i in range(3)], "xs")
                F["junk"] = sb(esA, "junk", [128, D], BF16); F["junkb"] = Buf("junk")
                F["string"] = Ring([sb(esA, f"st{i}", [128, 4], F32) for i in range(4)], "st")
                F["pT"] = Ring([psum(esA, f"pT{i}", [128, 8, 128], BF16) for i in range(2)], "pT", psum=True)
                F["psring"] = Ring([psum(esA, f"psa_{i}", [128, 512], F32) for i in range(6)], "psv", psum=True)
                F["t1ring"] = Ring([sb(esA, f"t1_{i}", [128, 512], F32) for i in range(2)], "t1")
                F["t2ring"] = Ring([sb(esA, f"t2_{i}", [128, 512], F32) for i in range(2)], "t2")
                F["sqring"] = Ring([sb(esA, f"sq{i}", [128, 2, 512], BF16) for i in range(2)], "sq")
                F["rsring"] = Ring([sb(esA, f"rs{i}", [128, 512], F32) for i in range(2)], "rs")
                F["qrring"] = Ring([sb(esA, f"qr{i}", [128, 512], BF16) for i in range(2)], "qr")
                xnr = Ring([sb(esA, f"xnT{i}", [128, 8, 512], BF16) for i in range(2)], "xnT")
                tab = Ring([sb(esA, f"tab{i}", [128, 2, 512], F32) for i in range(2)], "tab")
                cf = sb(esA, "cf", [128, 2, 512], F32); cfb = Buf("cf")
                kf = sb(esA, "kf", [128, 512], F32); kfb = Buf("kf")
                ostg = Ring([sb(esA, f"ostgA{i}", [128, 288], F32) for i in range(2)], "ostgA")

                def quadA(tiles, cs_ap, sn_ap, cdst, cb_, kdst, kb_, out_c, out_k):
                    N = sum(r for _, r, _ in tiles)
                    xnT, xnb = xnr.next()
                    frontend(F, tiles, xnT, xnb)
                    tb_, tbb = tab.next()
                    dma("pool", tb_[64:96, 0, 0:N], cs_ap, [], [tbb]); dma("pool", tb_[64:96, 1, 0:N], sn_ap, [], [tbb])
                    chunks = []
                    for c in range(2):
                        ps, psb = F["psring"].next()
                        proj(ps, Wl, Wlb, c * 128, 128, xnT, xnb, 0, N, psb)
                        chunks.append((ps, psb))
                    rs, rsb = rms_fm(F, chunks, N, 256)
                    for c, (ps, psb) in enumerate(chunks):
                        if out_c is not None:
                            stt("dve", cf[:, c, 0:N], ps[:, :N], gkv[:, c:c + 1], rs[:, :N], ALU.mult, ALU.mult, [psb, rsb, CONST], [cfb])
                            cp("pool", cdst[:, c, :], cf[:, c, 0:N], [cfb], [cb_])
                        else:
                            stt("dve", cdst[:, c, :], ps[:, :N], gkv[:, c:c + 1], rs[:, :N], ALU.mult, ALU.mult, [psb, rsb, CONST], [cb_])
                    ps, psb = F["psring"].next()
                    proj(ps, Wk, Wkb, 0, 96, xnT, xnb, 0, N, psb)
                    if out_k is not None:
                        rope_fm(F, ps, psb, 96, 64, 96, N, pm96b, tb_[64:96, 0, 0:N], tb_[64:96, 1, 0:N], tbb, kdst, kb_,
                                out_f32=kf[64:96, 0:N], outfb=kfb)
                    else:
                        rope_fm(F, ps, psb, 96, 64, 96, N, pm96b, tb_[64:96, 0, 0:N], tb_[64:96, 1, 0:N], tbb, kdst, kb_)
                    if out_c is not None:
                        for ti, (_, rows, _) in enumerate(tiles):
                            pso, psob = F["psring"].next()
                            for c in range(2):
                                mm(pso[0:rows, c * 128:(c + 1) * 128], cf[:, c, ti * 128:ti * 128 + rows], identf[:, :], True, True, [cfb, CONST], [psob])
                            mm(pso[0:rows, 256:288], kf[64:96, ti * 128:ti * 128 + rows], identf[64:96, 64:96], True, True, [kfb, CONST], [psob])
                            og, ogb = ostg.next()
                            cp("act", og[0:rows, :], pso[0:rows, 0:288], [psob], [ogb])
                            dma("pool", out_c[ti * 128:ti * 128 + rows, :], og[0:rows, 0:256], [ogb], [])
                            dma("pool", out_k[ti * 128:ti * 128 + rows, :], og[0:rows, 256:288], [ogb], [])

                quadA([(xs_in[:, :], 64, None)], cosk_s[64:96, :], sink_s[64:96, :], cs_new[:, :, 0:64], bcsn,
                      krs_new[64:96, 0:64], bkrsn, ckv_s, kr_s)
                for q in range(16):
                    tiles = [(xw[(q * 4 + i) * 128:(q * 4 + i + 1) * 128, :], 128, None) for i in range(4)]
                    own = q >= 12
                    quadA(tiles, cosk[64:96, q * 512:(q + 1) * 512], sink[64:96, q * 512:(q + 1) * 512],
                          cT[:, :, q * 512:(q + 1) * 512], bcT[q], KT[64:96, q * 512:(q + 1) * 512], bKr[q],
                          ckv_p[(q - 12) * 512:(q - 11) * 512, :] if own else None,
                          kr_p[(q - 12) * 512:(q - 11) * 512, :] if own else None)
                rec.flush()
            if stages < 4:
                return

            def mla(esM, nkb, nk_last, QTd, QTb_, nq, diag0, vld, sg_t, sgb_):
                nacc = (nq + 511) // 512
                accs = [psum(esM, f"acc{i}", [128, 512], F32) for i in range(nacc)]
                accb = [Buf(f"acc{i}", True) for i in range(nacc)]
                sring = Ring([psum(esM, f"psS{i}", [128, 512], F32) for i in range(3)], "psS", psum=True)
                misc = sring
                V4 = sb(esM, "V4", [128, 64, 2, 128], BF16)

                bV = [Buf(f"V4_{i}") for i in range(64)]
                for j_ in range(2):
                    cp("dve" if j_ == 0 else "act", V4[:, 0:nkb, j_, 64:128], vld[:, 0:nkb].unsqueeze(2).to_broadcast([128, nkb, 64]), [CONST], bV[0:nkb])
                ptr = Ring([sb(esM, f"pt{i}", [128, 512], BF16) for i in range(6)], "pt")
                Rr = Ring([sb(esM, f"Rm{i}", [128, 512], F32) for i in range(2)], "Rm")
                tmr = Ring([sb(esM, f"tmm{i}", [128, 512], F32) for i in range(2)], "tmm")
                ngroups = (nq + 511) // 512
                nsb = (nkb + 3) // 4

                def nkeys(kb):
                    return nk_last if kb == nkb - 1 else 128

                def expandK(h, s):
                    ncol = sum(nkeys(kb) for kb in range(4 * s, min(4 * s + 4, nkb)))
                    ps, psb = misc.next()
                    for c in range(2):
                        mm(ps[0:64, 0:ncol], wukv[:, c, h * 128:h * 128 + 64], cT[:, c, s * 512:s * 512 + ncol], c == 0, c == 1, [wukvb, bcT[s]], [psb])
                    cp("dve", KT[0:64, s * 512:s * 512 + ncol], ps[0:64, 0:ncol], [psb], [bKn[s]])

                def expandV(hh):
                    wv = [wukv[:, c, :].rearrange("p (h e) -> p h e", e=128)[:, 2 * hh:2 * hh + 2, 64:128] for c in range(2)]
                    for kb in range(nkb):
                        nk = nkeys(kb)
                        ps, psb = misc.next()
                        pv = ps[0:nk, 0:128].rearrange("p (h e) -> p h e", e=64)
                        for c in range(2):
                            mm(pv, cT[:, c, kb * 128:kb * 128 + nk], wv[c], c == 0, c == 1, [wukvb, bcT[kb // 4]], [psb])
                        rec.op("dve", lambda kb=kb, nk=nk, pv=pv: nc.vector.tensor_scalar(out=V4[0:nk, kb, :, 0:64], in0=pv, scalar1=vld[0:nk, kb:kb + 1], scalar2=None, op0=ALU.mult), [psb, CONST], [bV[kb]])

                def head(h):
                    hj = h % 2
                    work = []
                    for kb in range(nkb):
                        qlo = 0 if diag0 is None else 128 * max(0, kb - diag0)
                        for g in range(ngroups):
                            g0, g1 = g * 512, min(nq, g * 512 + 512)
                            a = max(g0, qlo)
                            if a < g1:
                                work.append((kb, g, a, g1, diag0 is not None and kb >= diag0 and a == qlo))
                    state = {}
                    seen = set()

                    def emitS(i):
                        kb, g, a, b, dg = work[i]
                        nk = nkeys(kb)
                        if kb % 4 == 0 and kb not in seen and kb // 4 + 1 < nsb:
                            expandK(h, kb // 4 + 1)
                        seen.add(kb)
                        ps, psb = sring.next()
                        mm(ps[0:nk, 0:b - a], KT[0:96, kb * 128:kb * 128 + nk], QTd[0:96, h, a:b], True, True, [bKn[kb // 4], bKr[kb // 4], QTb_], [psb])
                        pt, ptb = ptr.next()
                        act(pt[0:nk, 0:b - a], ps[0:nk, 0:b - a], AF.Exp, [psb], [ptb])
                        state[i] = (pt, ptb)

                    def emitPV(i):
                        kb, g, a, b, dg = work[i]
                        nk = nkeys(kb)
                        pt, ptb = state.pop(i)
                        acc = accs[g]; g0 = g * 512
                        first = kb == 0
                        last = kb == nkb - 1
                        if dg:
                            mm(acc[0:128, a - g0:a - g0 + 64], V4[0:64, kb, hj, :], pt[0:64, 0:64], False, True, [bV[kb], ptb], [accb[g]], sgc=True)
                            mm(acc[0:128, a - g0 + 64:a - g0 + 128], V4[:, kb, hj, :], pt[:, 64:128], False, True, [bV[kb], ptb], [accb[g]], sgc=True)
                            if b > a + 128:
                                mm(acc[0:128, a - g0 + 128:b - g0], V4[:, kb, hj, :], pt[:, 128:b - a], False, False, [bV[kb], ptb], [accb[g]], sgc=True)
                        else:
                            mm(acc[0:128, a - g0:b - g0], V4[0:nk, kb, hj, :], pt[0:nk, 0:b - a], first, last, [bV[kb], ptb], [accb[g]], sgc=True)

                    c, pb = h // 2, (h % 2) * 64
                    lastkb = {}
                    for (kb_, g_, a_, b_, dg_) in work:
                        lastkb[g_] = kb_

                    def norm_group(g):
                        g0, g1 = g * 512, min(nq, g * 512 + 512)
                        n = g1 - g0
                        Rs, Rsb_ = Rr.next()
                        act(Rs[pb:pb + 64, 0:n], accs[g][64:128, 0:n], AF.Ln, [accb[g]], [Rsb_], bias=1e-30)
                        act(Rs[pb:pb + 64, 0:n], Rs[pb:pb + 64, 0:n], AF.Exp, [Rsb_], [Rsb_], scale=-1.0)
                        tm, tmb = tmr.next()
                        tt("dve", tm[pb:pb + 64, 0:n], Rs[pb:pb + 64, 0:n], sg_t[pb:pb + 64, c, g0:g1], ALU.mult, [Rsb_, sgb_], [tmb])
                        tt("dve", sg_t[pb:pb + 64, c, g0:g1], accs[g][0:64, 0:n], tm[pb:pb + 64, 0:n], ALU.mult, [accb[g], tmb], [sgb_])

                    expandK(h, 0)
                    for i in range(len(work) + 1):
                        if i < len(work):
                            emitS(i)
                        if i >= 1:
                            emitPV(i - 1)
                            kb_, g_ = work[i - 1][0], work[i - 1][1]
                            if lastkb[g_] == kb_ and (i == len(work) or work[i][0] != kb_ or work[i][1] != g_):
                                norm_group(g_)

                for hh in range(4):
                    expandV(hh)
                    for h in range(2 * hh, 2 * hh + 2):
                        head(h)

            with ExitStack() as esM:
                mla(esM, 64, 128, QT, bQT, NQC, T0, valid, sga, bsga)
                rec.flush()
            with ExitStack() as esS:
                pTs = Ring([psum(esS, "pTs0", [128, 8, 128], BF16)], "pTs", psum=True)
                cst = Ring([sb(esS, f"ccst{i}", [128, 256], F32) for i in range(2)], "ccst")
                cbf = Ring([sb(esS, f"ccbf{i}", [128, 256], BF16) for i in range(2)], "ccbf")
                kst = Ring([sb(esS, f"kst{i}", [128, 32], F32) for i in range(2)], "kst")
                kbf = Ring([sb(esS, f"kbf{i}", [128, 96], BF16) for i in range(2)], "kbf")
                for r_ in kbf.tiles:
                    memset("pool", r_[:], 0.0, kbf.bufs)
                for t in range(32):
                    st_, stb = cst.next()
                    dma("sp", st_[:], cache_ckv[t * 128:(t + 1) * 128, :], [], [stb])
                    bf, bfb = cbf.next()
                    cp(alt_eng(("dve", "pool")), bf[:], st_[:], [stb], [bfb])
                    ks, ksb = kst.next()
                    dma("sp", ks[:], cache_kr[t * 128:(t + 1) * 128, :], [], [ksb])
                    kb_, kbb = kbf.next()
                    cp("pool", kb_[:, 64:96], ks[:], [ksb], [kbb])
                    pT, pTb = pTs.next()
                    for c in range(2):
                        tr(pT[:, c, :], bf[:, c * 128:(c + 1) * 128], identb[:, :], [bfb, CONST], [pTb])
                    tr(pT[0:96, 2, :], kb_[:, 0:96], identb[:, :], [kbb, CONST], [pTb])
                    cp("act", cT[:, :, t * 128:(t + 1) * 128], pT[:, 0:2, :], [pTb], [bcT[t // 4]])
                    cp("dve", KT[64:96, t * 128:(t + 1) * 128], pT[64:96, 2, :], [pTb], [bKr[t // 4]])
                cp("dve", cT[:, :, 4096:4160], cs_new[:, :, 0:64], [bcsn], [bcT[8]])
                cp("dve", KT[64:96, 4096:4160], krs_new[64:96, 0:64], [bkrsn], [bKr[8]])
                with ExitStack() as esM:
                    mla(esM, 33, 64, QT_s, bQTs, 64, None, vone, sga_s, bsgas)
                    rec.flush()

    def make_out_proj(F, psY, tmpr):
        def out_proj(lhs_fn, lhsb, W, Wb_, rows, gb, res, resb, dst, dstb):
            ps, psb = psY.next()
            for hf in range(2):
                for kk in range(8):
                    mm(ps[0:rows, hf * 512:(hf + 1) * 512], lhs_fn(kk), W[:, kk, hf * 512:(hf + 1) * 512], kk == 0, kk == 7, lhsb + [Wb_], [psb])
            st_, stb = F["string"].next()
            tm, tmb = tmpr.next()
            act(tm[:rows, 0:512], ps[0:rows, 0:512], AF.Square, [psb], [tmb, stb], accum_out=st_[:rows, 0:1])
            act(tm[:rows, 512:1024], ps[0:rows, 512:1024], AF.Square, [psb], [tmb, stb], accum_out=st_[:rows, 3:4])
            tt("dve", st_[:rows, 0:1], st_[:rows, 0:1], st_[:rows, 3:4], ALU.add, [stb], [stb])
            act(st_[:rows, 1:2], st_[:rows, 0:1], AF.Ln, [stb], [stb], scale=1.0 / D, bias=EPS)
            act(st_[:rows, 2:3], st_[:rows, 1:2], AF.Exp, [stb], [stb], scale=-0.5)
            for hf in range(2):
                stt("dve", tm[0:rows, hf * 512:(hf + 1) * 512], ps[0:rows, hf * 512:(hf + 1) * 512], st_[:rows, 2:3], gb[0:rows, hf * 512:(hf + 1) * 512], ALU.mult, ALU.mult, [psb, stb, CONST], [tmb])
            tt("dve", dst, tm[0:rows, :], res, ALU.add, [tmb, resb], [dstb])

        return out_proj

    def stage_c0():
        with ExitStack() as es:
            wo0 = sb(es, "wo0", [128, 8, D], BF16); wo0b = Buf("wo0")
            gb0 = sb(es, "gb0", [128, D], F32)
            dma("sp", gb0[:], gb_post0[:, :], [], [CONST])
            with ExitStack() as es2:
                ring = Ring([sb(es2, f"stgw{i}", [128, 1664], F32) for i in range(4)], "stgw")
                prep_w(ring, wo0, w_out0, 8, D, None, wo0b)
                rec.flush()
            F = {}
            F["xring"] = Ring([sb(es, f"xr{i}", [128, D], F32) for i in range(4)], "xr")
            F["string"] = Ring([sb(es, f"st{i}", [128, 4], F32) for i in range(4)], "st")
            psY = Ring([psum(es, f"psY{i}", [128, 1024], F32) for i in range(3)], "psY", psum=True)
            tmpr = Ring([sb(es, f"tmpc{i}", [128, D], F32) for i in range(3)], "tmpc")
            h1r = Ring([sb(es, f"h1_{i}", [128, D], F32) for i in range(4)], "h1")
            out_proj = make_out_proj(F, psY, tmpr)
            for t in [None] + list(range(T0, 64)):
                is_sample = t is None
                rows = 64 if is_sample else 128
                oc = 0 if is_sample else (t - T0) * 128
                A_, B_ = (sga_s, mixb_s) if is_sample else (sga, mixb)
                xt, xb = F["xring"].next()
                dma("sp", xt[:rows, :], xs_in[:, :] if is_sample else xw[t * 128:(t + 1) * 128, :], [], [xb])
                h1, h1b = h1r.next()
                out_proj(lambda kk, oc=oc, rows=rows, A_=A_, B_=B_: (A_[:, kk, oc:oc + rows] if kk < 4 else B_[:, kk - 4, oc:oc + rows]),
                         [bsgas, bmixbs] if is_sample else [bsga, bmixb], wo0, wo0b, rows, gb0, xt[:rows, :], xb, h1[:rows, :], h1b)
                r0 = NQC if is_sample else oc
                dma("pool", h1d[r0:r0 + rows, :], h1[:rows, :], [h1b], [])
            rec.flush()

    def stage_c():
        with ExitStack() as es:
            wc = sb(es, "wc", [128, 8, 2304], BF16); wcb = Buf("wc")
            wo1 = sb(es, "wo1", [128, 8, D], BF16); wo1b = Buf("wo1")
            gp1 = sb(es, "gp1c", [128, 8], F32)
            gb1 = sb(es, "gb1", [128, D], F32)
            esk = sb(es, "esk", [128, 16], F32)
            dma("sp", gp1[:], g_pre1[:, :], [], [CONST])
            dma("sp", gb1[:], gb_post1[:, :], [], [CONST]); dma("sp", esk[:], sinks_d[:, :], [], [CONST])
            act(esk[:], esk[:], AF.Exp, [CONST], [CONST])
            with ExitStack() as es2:
                ring = Ring([sb(es2, f"stgw{i}", [128, 1664], F32) for i in range(4)], "stgw")
                prep_w(ring, wc, c_w_in, 8, 2304, gp1, wcb)
                prep_w(ring, wo1, c_w_out, 8, D, None, wo1b)
                rec.flush()
            F = {}
            F["xsring"] = Ring([sb(es, f"xs{i}", [128, D], BF16) for i in range(2)], "xs")
            F["string"] = Ring([sb(es, f"st{i}", [128, 4], F32) for i in range(4)], "st")
            F["pT"] = Ring([psum(es, "pT0", [128, 8, 128], BF16)], "pT", psum=True)
            psY = Ring([psum(es, f"psY{i}", [128, 1024], F32) for i in range(1)], "psY", psum=True)
            F["psring"] = Ring([psum(es, f"psc_{i}", [128, 512], F32) for i in range(5)], "psv", psum=True)
            F["t1ring"] = Ring([sb(es, f"t1_{i}", [128, 512], F32) for i in range(2)], "t1")
            F["t2ring"] = Ring([sb(es, f"t2_{i}", [128, 512], F32) for i in range(2)], "t2")
            F["qrring"] = Ring([sb(es, f"qr{i}", [128, 512], BF16) for i in range(2)], "qr")
            h1r = Ring([sb(es, f"h1_{i}", [128, D], F32) for i in range(6)], "h1")
            tmpr = Ring([sb(es, f"tmpc{i}", [128, D], F32) for i in range(2)], "tmpc")
            xn1r = Ring([sb(es, f"xn1T{i}", [128, 8, 512], BF16) for i in range(2)], "xn1T")
            q1r = Ring([sb(es, f"q1T{i}", [128, 8, 512], BF16) for i in range(2)], "q1T")
            cur = {}
            sg1r = Ring([sb(es, f"sg1_{i}", [128, 8, 512], BF16) for i in range(2)], "sg1")
            mx1 = sb(es, "mx1", [128, 8, 512], BF16); mx1b = Buf("mx1")
            tab = Ring([sb(es, f"tab{i}", [128, 4, 512], F32) for i in range(1)], "tab")
            k1f = sb(es, "k1f", [128, 512], F32); k1fb = Buf("k1f")
            k1h = sb(es, "k1h", [128, 512], BF16); k1hb = Buf("k1h")
            K2 = sb(es, "K2", [128, 2, NQC], BF16); V1 = sb(es, "V1", [128, NT, 2, 128], BF16)
            bkv = [Buf(f"kv1_{i}") for i in range(NT)]
            K2s = sb(es, "K2s", [128, 2, 192], BF16); V1s = sb(es, "V1s", [128, 2, 2, 128], BF16); bkvs = Buf("kv1s")
            ptr = Ring([sb(es, f"ptc{i}", [128, 2, 512], BF16) for i in range(3)], "ptc")
            Rr = Ring([sb(es, f"Rc{i}", [128, 512], F32) for i in range(2)], "Rc")
            tmr = Ring([sb(es, f"tmc{i}", [128, 512], F32) for i in range(1)], "tmc")
            ostg = Ring([sb(es, f"ostgc{i}", [128, 128], F32) for i in range(2)], "ostgc")
            out_proj = make_out_proj(F, psY, tmpr)

            def attn_tile(nq, qc, kprev, kown, nk_own, vprev, vown, kvbufs, eoff):
                q1T, q1b = cur["q1T"]
                sg1, sg1b = cur["sg1"]
                for g in range(2):
                    for par in range(2):
                        pb = 64 * par
                        pt, ptb = ptr.next()
                        for ki, (kfn, nk) in enumerate(((kprev, 128), (kown, nk_own))):
                            ps, psb = F["psring"].next()
                            pv = ps[0:nk, 0:4 * nq].rearrange("p (j q) -> p j q", j=4)
                            mm(pv, kfn(g, par), q1T[pb:pb + 64, 4 * g:4 * g + 4, qc:qc + nq], True, True, [kvbufs[ki], q1b], [psb])
                            act(pt[0:nk, ki, 0:4 * nq].rearrange("p (j q) -> p j q", j=4), pv, AF.Exp, [psb], [ptb])
                        p0v = pt[:, 0, 0:4 * nq].rearrange("p (j q) -> p j q", j=4)
                        p1v = pt[:, 1, 0:4 * nq].rearrange("p (j q) -> p j q", j=4)
                        acc, accb_ = F["psring"].next()
                        av = acc[0:128, 0:4 * nq].rearrange("p (j q) -> p j q", j=4)
                        if nq > 64:
                            memset("pool", p0v[0:64, :, 64:128], 0.0, [ptb])
                            memset("pool", p1v[64:128, :, 0:64], 0.0, [ptb])
                        mm(av[:, :, 0:nq], vprev(g, 0, 128), p0v[:, :, 0:nq], True, False, [kvbufs[0], ptb], [accb_])
                        mm(av[:, :, 0:nq], vown(g, 0, nk_own), p1v[0:nk_own, :, 0:nq], False, True, [kvbufs[1], ptb], [accb_])
                        Rs, Rsb_ = Rr.next()
                        Rv = Rs[0:128, 0:4 * nq].rearrange("p (j q) -> p j q", j=4)
                        for j in range(4):
                            h = 8 * g + 2 * j + par
                            act(Rv[pb:pb + 64, j, :], av[64:128, j, 0:nq], AF.Ln, [accb_, CONST], [Rsb_], bias=esk[64:128, h:h + 1])
                        act(Rv[pb:pb + 64], Rv[pb:pb + 64], AF.Exp, [Rsb_], [Rsb_], scale=-1.0)
                        tm, tmb = tmr.next()
                        tv = tm[pb:pb + 64, 0:4 * nq].rearrange("p (j q) -> p j q", j=4)
                        tt("dve", tv, Rv[pb:pb + 64], sg1[pb:pb + 64, 4 * g:4 * g + 4, qc:qc + nq], ALU.mult, [Rsb_, sg1b], [tmb])
                        tt("dve", mx1[pb:pb + 64, 4 * g:4 * g + 4, qc:qc + nq], av[0:64, :, 0:nq], tv, ALU.mult, [accb_, tmb], [mx1b])

            def group(tl, is_sample):
                N = sum(r for _, r in tl)
                h1s = []
                for ti, (t, rows) in enumerate(tl):
                    oc = 0 if is_sample else (t - T0) * 128
                    A_, B_ = (sga_s, mixb_s) if is_sample else (sga, mixb)
                    h1, h1b = h1r.next()
                    r0 = NQC if is_sample else oc
                    dma("sp", h1[:rows, :], h1d[r0:r0 + rows, :], [], [h1b])
                    h1s.append((h1, rows, h1b))
                xn1, xn1b = xn1r.next()
                q1T, q1b = q1r.next()
                cur["q1T"] = (q1T, q1b)
                sg1, sg1b = sg1r.next()
                cur["sg1"] = (sg1, sg1b)
                frontend(F, [(h1[:, :], rows, hb) for h1, rows, hb in h1s], xn1, xn1b)
                tb_, tbb = tab.next()
                c0 = 0 if is_sample else (tl[0][0] - T0) * 128
                srcs = (cos1q_s, sin1q_s, cos1k_s, sin1k_s) if is_sample else (cos1q, sin1q, cos1k, sin1k)
                for i_, s_ in enumerate(srcs):
                    dma("pool", tb_[:, i_, 0:N], s_[:, c0:c0 + N], [], [tbb])
                for c in range(8):
                    ps, psb = F["psring"].next()
                    proj(ps, wc, wcb, C1_Q + c * 128, 128, xn1, xn1b, 0, N, psb)
                    rope_fm(F, ps, psb, 128, 0, 128, N, pm128b, tb_[:, 0, 0:N], tb_[:, 1, 0:N], tbb, q1T[:, c, 0:N], q1b)
                    ps, psb = F["psring"].next()
                    proj(ps, wc, wcb, C1_G + c * 128, 128, xn1, xn1b, 0, N, psb)
                    silu_fm(F, ps, psb, N, sg1[:, c, 0:N], sg1b)
                ps, psb = F["psring"].next()
                proj(ps, wc, wcb, C1_K, 128, xn1, xn1b, 0, N, psb)
                rope_fm(F, ps, psb, 128, 0, 128, N, pm128b, tb_[:, 2, 0:N], tb_[:, 3, 0:N], tbb, k1h[:, 0:N], k1hb, out_f32=k1f[:, 0:N], outfb=k1fb)
                col = 0
                for ti, (t, rows) in enumerate(tl):
                    if is_sample:
                        Kd, kc, kb_ = K2s, 128, bkvs
                    else:
                        Kd, kc, kb_ = K2, (t - T0) * 128, bkv[t - T0]
                    for g in range(2):
                        for par in range(2):
                            cp("dve", Kd[64 * par:64 * par + 64, g, kc:kc + rows], k1h[64 * g:64 * g + 64, col:col + rows], [k1hb], [kb_])
                    ps, psb = F["psring"].next()
                    for k in range(8):
                        mm(ps[0:rows, 0:128], xn1[:, k, col:col + rows], wc[:, k, C1_V:C1_V + 128], k == 0, k == 7, [xn1b, wcb], [psb])
                    pv = ps[0:rows, 0:128].rearrange("p (g e) -> p g e", g=2)
                    if is_sample:
                        cp("dve", V1s[0:rows, 1, :, 0:64], pv, [psb], [kb_])
                    else:
                        rec.op("dve", lambda t=t, pv=pv: nc.vector.tensor_scalar(out=V1[:, t - T0, :, 0:64], in0=pv, scalar1=valid[:, t:t + 1], scalar2=None, op0=ALU.mult), [psb, CONST], [kb_])
                        cp("act", V1[:, t - T0, :, 64:128], valid[:, t:t + 1].unsqueeze(2).to_broadcast([128, 2, 64]), [CONST], [kb_])
                    if is_sample or t == 63:
                        og, ogb = ostg.next()
                        cp("act", og[0:rows, :], ps[0:rows, 0:128], [psb], [ogb])
                        dma("pool", cv_s[64:128, :] if is_sample else cv_p[:, :], og[0:rows, :], [ogb], [])
                        ps, psb = F["psring"].next()
                        mm(ps[0:rows, 0:128], k1f[:, col:col + rows], identf[:, :], True, True, [k1fb, CONST], [psb])
                        og, ogb = ostg.next()
                        cp("act", og[0:rows, :], ps[0:rows, 0:128], [psb], [ogb])
                        dma("pool", ck_s[64:128, :] if is_sample else ck_p[:, :], og[0:rows, :], [ogb], [])
                    col += rows
                col = 0
                for ti, (t, rows) in enumerate(tl):
                    if is_sample:
                        attn_tile(64, 0,
                                  lambda g, par: K2s[64 * par:64 * par + 64, g, 0:128], lambda g, par: K2s[64 * par:64 * par + 64, g, 128:192], 64,
                                  lambda g, k0, k1: V1s[k0:k1, 0, g, :], lambda g, k0, k1: V1s[k0:k1, 1, g, :], [bkvs, bkvs], 0)
                    elif t > T0:
                        j = t - T0
                        attn_tile(128, col,
                                  lambda g, par, j=j: K2[64 * par:64 * par + 64, g, (j - 1) * 128:j * 128],
                                  lambda g, par, j=j: K2[64 * par:64 * par + 64, g, j * 128:(j + 1) * 128], 128,
                                  lambda g, k0, k1, j=j: V1[k0:k1, j - 1, g, :], lambda g, k0, k1, j=j: V1[k0:k1, j, g, :],
                                  [bkv[j - 1], bkv[j]], 0)
                    col += rows
                col = 0
                for ti, (t, rows) in enumerate(tl):
                    if is_sample or t > T0:
                        h1, _, h1b = h1s[ti]
                        y, yb = tmpr.next()
                        out_proj(lambda kk, col=col, rows=rows: mx1[:, kk, col:col + rows], [mx1b], wo1, wo1b, rows, gb1, h1[:rows, :], h1b, y[:rows, :], yb)
                        dma("sp", y_s[:, :] if is_sample else y_p[(t - 48) * 128:(t - 47) * 128, :], y[:rows, :], [yb], [])
                    col += rows

            with ExitStack() as es2:
                cs_ = tmpr.tiles[0][:, 0:256].rearrange("p (a b) -> p a b", a=2); bb_ = tmpr.bufs[0]
                cb2 = F["xsring"].tiles[0][:, 0:128]; cbb = F["xsring"].bufs[0]
                dma("sp", cs_[:, 0, :], cache_ck[:, :], [], [bb_]); dma("sp", cs_[:, 1, :], cache_cv[:, :], [], [bb_])
                cp("dve", cb2, cs_[:, 0, :], [bb_], [cbb])
                pT, pTb = F["pT"].next()
                tr(pT[:, 0, :], cb2, identb[:, :], [cbb, CONST], [pTb])
                for g in range(2):
                    for par in range(2):
                        cp("dve", K2s[64 * par:64 * par + 64, g, 0:128], pT[64 * g:64 * g + 64, 0, :], [pTb], [bkvs])
                cp("dve", V1s[:, 0, :, 0:64], cs_[:, 1, :].rearrange("p (g e) -> p g e", g=2), [bb_], [bkvs])
                memset("pool", V1s[:, :, :, 64:128], 1.0, [bkvs])
                dma("pool", ck_s[0:64, :], cache_ck[64:128, :], [], [])
                dma("pool", cv_s[0:64, :], cache_cv[64:128, :], [], [])
                rec.flush()
            group([(None, 64)], True)
            group([(T0, 128)], False)
            for q in range(12, 16):
                group([(q * 4 + i, 128) for i in range(4)], False)
            rec.flush()

    with ExitStack() as esP:
        sga = sb(esP, "sga", [128, 4, NQC], BF16)
        mixb = sb(esP, "mixb", [128, 4, NQC], BF16)
        stage_b2()
        with ExitStack() as esQ:
            QT = sb(esQ, "QT", [128, 8, NQC], BF16)
            if stages >= 2:
                stage_b1()
            if stages >= 3:
                stage_am()
        if stages >= 5:
            stage_c0()
    if stages >= 5:
        stage_c()
    rec.barrier()
    top.close()
    return nc, rec


_PROG = {}


def _get_prog(stages):
    if stages not in _PROG:
        _PROG[stages] = build_program(stages)
    return _PROG[stages]


def kernel(x_prompt, x_sample, cache_a_ckv, cache_a_krope, cache_b_k, cache_b_v, cache_c_k, cache_c_v,
           ab_pre_norm, ab_post_norm, ab_w_in, ab_q_norm, ab_kv_norm, ab_w_uq, ab_w_ukv, ab_rel_bias, ab_w_out,
           c_pre_norm, c_post_norm, c_w_in, c_sinks, c_w_out):
    stages = int(os.environ.get("KSTAGES", "99"))
    f = lambda a: np.ascontiguousarray(np.asarray(a, np.float32))
    x_prompt = f(x_prompt); x_sample = f(x_sample)
    shared = {
        "w_in": _wl(ab_w_in[0], 8), "g_pre": _pk(ab_pre_norm[0], 8),
        "w_uq": _wl(ab_w_uq[0], 3), "g_q": _pk(ab_q_norm[0], 3),
        "w_ukv": _wl(ab_w_ukv[0], 2), "g_kv": _pk(ab_kv_norm[0], 2),
        "w_out0": _wl(ab_w_out[0], 8), "gb_post0": f(np.broadcast_to(np.asarray(ab_post_norm[0], np.float32)[None, :], (128, D))),
        "c_w_in": _wl(c_w_in[0], 8), "g_pre1": _pk(c_pre_norm[0], 8),
        "c_w_out": _wl(c_w_out[0], 8), "gb_post1": f(np.broadcast_to(np.asarray(c_post_norm[0], np.float32)[None, :], (128, D))),
        "pm96": _perm_lhsT(96, [(64, 32)]), "pm128": _perm_lhsT(128, [(0, 16), (64, 16)]),
        "ident": np.eye(128, dtype=np.float32),
        "sinks": f(np.broadcast_to(np.asarray(c_sinks[0], np.float32)[None, :], (128, 16))),
    }
    tbl = np.asarray(ab_rel_bias[0], np.float32)
    kk = np.arange(128)[:, None]; qq = np.arange(128)[None, :]
    bd = np.zeros((128, 2, 8, 128), np.float32)
    for d_ in range(2):
        idx = np.clip(128 * d_ + qq - kk, -128, 128) + 128
        bd[:, d_, :, :] = np.transpose(tbl[:, idx], (1, 0, 2))
    shared["bd"] = bd
    shared["cb"] = f(np.broadcast_to(tbl[None, :, 256], (128, 8)))
    pos_s = PAST + np.arange(64)
    shared["cosk_s"], shared["sink_s"] = _rope_tables_fm(pos_s, 32, 96, [64], 1.0)
    shared["cosq_s"], shared["sinq_s"] = _rope_tables_fm(pos_s, 32, 96, [64], A_SCALE)
    shared["cos1q_s"], shared["sin1q_s"] = _rope_tables_fm(pos_s, 16, 128, [0, 64], C_SCALE)
    shared["cos1k_s"], shared["sin1k_s"] = _rope_tables_fm(pos_s, 16, 128, [0, 64], 1.0)
    tabs = {}
    in_maps = []
    for core in range(8):
        b, c = core // 4, core % 4
        end = 2048 * (c + 1)
        start = end - 8192
        m = dict(shared)
        xw = np.zeros((8192, D), np.float32)
        lo = max(0, -start)
        xw[lo:] = x_prompt[b, start + lo:end]
        m["xw"] = xw
        m["xs_in"] = f(x_sample[core])
        if c not in tabs:
            pos = start + np.arange(8192)
            t = {}
            t["cosk"], t["sink"] = _rope_tables_fm(pos, 32, 96, [64], 1.0)
            t["cosq"], t["sinq"] = _rope_tables_fm(pos[5632:], 32, 96, [64], A_SCALE)
            t["cos1q"], t["sin1q"] = _rope_tables_fm(pos[6016:], 16, 128, [0, 64], C_SCALE)
            t["cos1k"], t["sin1k"] = _rope_tables_fm(pos[6016:], 16, 128, [0, 64], 1.0)
            t["valid"] = f((pos >= 0).astype(np.float32).reshape(64, 128).T)
            tabs[c] = t
        m.update(tabs[c])
        m["cache_ckv"] = f(cache_a_ckv[0, core]); m["cache_kr"] = f(cache_a_krope[0, core])
        m["cache_bk"] = f(np.asarray(cache_b_k[0, core]).reshape(512, 512)); m["cache_bv"] = f(np.asarray(cache_b_v[0, core]).reshape(512, 512))
        m["cache_ck"] = f(np.asarray(cache_c_k[0, core]).reshape(128, 128)); m["cache_cv"] = f(np.asarray(cache_c_v[0, core]).reshape(128, 128))
        in_maps.append(m)
    nc, _ = _get_prog(stages)
    res = run_bass_kernel_spmd(nc, in_maps, core_ids=list(range(8)))
    R = res.results
    y_prompt = np.zeros((2, SEQ, D), np.float32); y_sample = np.zeros((8, 64, D), np.float32)
    a_ckv_p = np.zeros((1, 2, SEQ, 256), np.float32); a_kr_p = np.zeros((1, 2, SEQ, 32), np.float32)
    b_k_p = np.zeros((1, 2, 512, 8, 64), np.float32); b_v_p = np.zeros((1, 2, 512, 8, 64), np.float32)
    c_k_p = np.zeros((1, 2, 128, 2, 64), np.float32); c_v_p = np.zeros((1, 2, 128, 2, 64), np.float32)
    a_ckv_s = np.zeros((1, 8, 64, 256), np.float32); a_kr_s = np.zeros((1, 8, 64, 32), np.float32)
    b_k_s = np.zeros((1, 8, 512, 8, 64), np.float32); b_v_s = np.zeros((1, 8, 512, 8, 64), np.float32)
    c_k_s = np.zeros((1, 8, 128, 2, 64), np.float32); c_v_s = np.zeros((1, 8, 128, 2, 64), np.float32)
    for core in range(8):
        b, c = core // 4, core % 4
        r = R[core]
        sl = slice(2048 * c, 2048 * (c + 1))
        y_prompt[b, sl] = r["y_p"]; y_sample[core] = r["y_s"]
        a_ckv_p[0, b, sl] = r["ckv_p"]; a_kr_p[0, b, sl] = r["kr_p"]
        if c == 3:
            b_k_p[0, b] = r["bk_p"].reshape(512, 8, 64); b_v_p[0, b] = r["bv_p"].reshape(512, 8, 64)
            c_k_p[0, b] = r["ck_p"].reshape(128, 2, 64); c_v_p[0, b] = r["cv_p"].reshape(128, 2, 64)
        a_ckv_s[0, core] = r["ckv_s"]; a_kr_s[0, core] = r["kr_s"]
        b_k_s[0, core] = r["bk_s"].reshape(512, 8, 64); b_v_s[0, core] = r["bv_s"].reshape(512, 8, 64)
        c_k_s[0, core] = r["ck_s"].reshape(128, 2, 64); c_v_s[0, core] = r["cv_s"].reshape(128, 2, 64)
    return (y_prompt, y_sample, a_ckv_p, a_kr_p, b_k_p, b_v_p, c_k_p, c_v_p,
            a_ckv_s, a_kr_s, b_k_s, b_v_s, c_k_s, c_v_s)
```

```python
import os
import numpy as np
import concourse.bass as bass
import concourse.mybir as mybir
from concourse.bass_utils import run_bass_kernel_spmd

F32, BF16 = mybir.dt.float32, mybir.dt.bfloat16
AF = mybir.ActivationFunctionType
ALU = mybir.AluOpType

D = 1024
SEQ = 8192
PAST = 4096
CH = 64
EPS = 1e-6
THETA = 500000.0
A_SCALE = 96 ** -0.5
B_SCALE = 64 ** -0.5
C_SCALE = 64 ** -0.5
C_QLAT, C_CKV, C_KR, C_GA, C_QB, C_KB, C_VB, C_GB = 0, 384, 640, 672, 1184, 1696, 2208, 2720
C1_Q, C1_K, C1_V, C1_G = 0, 1024, 1152, 1280
NT = 17
T0 = 47
NQC = NT * 128


class Buf:
    __slots__ = ("name", "psum")

    def __init__(self, name, psum=False):
        self.name = name
        self.psum = psum


class Rec:
    ENGS = ("pe", "act", "dve", "pool", "sp")

    def __init__(self, nc, sems, dsems):
        self.nc = nc
        self.eng = {"pe": nc.tensor, "act": nc.scalar, "dve": nc.vector, "pool": nc.gpsimd, "sp": nc.sync}
        self.sem = sems
        self.tick = {e: 0 for e in self.ENGS}
        self.known = {e: {} for e in self.ENGS}
        self.dsems = dsems
        self.dcount = {q: 0 for q in dsems}
        self.items = []
        self.n_ins = 0
        self.reorder = os.environ.get("KREORDER", "1") == "1"

    def op(self, eng, fn, r=(), w=(), cost=300.0):
        self.items.append((eng, fn, tuple(r), tuple(w), False, cost))

    def dma(self, q, fn, r=(), w=(), cost=3000.0):
        self.items.append((q, fn, tuple(r), tuple(w), True, cost))

    def _schedule(self, items, alld, W=640):
        n = len(items)
        succ = [[] for _ in range(n)]
        left = [0] * n
        for i in range(n):
            left[i] = len(alld[i])
            for j in alld[i]:
                succ[j].append(i)
        rdy = [0.0] * n
        fin = [0.0] * n
        engfree = {}
        done = [False] * n
        order = []
        ready = [i for i in range(min(n, W)) if left[i] == 0]
        hi = min(n, W)
        lo = 0
        while len(order) < n:
            best, bt = -1, None
            for i in ready:
                t = max(rdy[i], engfree.get(items[i][0], 0.0))
                if bt is None or t < bt - 1e-9 or (abs(t - bt) <= 1e-9 and i < best):
                    best, bt = i, t
            i = best
            ready.remove(i)
            eng, isd, cost = items[i][0], items[i][4], items[i][5]
            if isd:
                engfree[eng] = bt + 60.0
                fin[i] = bt + cost
            else:
                engfree[eng] = bt + cost
                fin[i] = bt + cost
            done[i] = True
            order.append(i)
            for k in succ[i]:
                left[k] -= 1
                if fin[i] > rdy[k]:
                    rdy[k] = fin[i]
                if left[k] == 0 and k < hi:
                    ready.append(k)
            while lo < n and done[lo]:
                lo += 1
            nh = min(n, lo + W)
            while hi < nh:
                if left[hi] == 0 and not done[hi]:
                    ready.append(hi)
                hi += 1
        return order

    def _ensure(self, e, sem, val):
        k = self.known[e]
        if k.get(sem, 0) < val:
            self.eng[e].wait_ge(sem, val)
            k[sem] = val

    def flush(self):
        items = self.items
        n = len(items)
        last_w = {}
        readers = {}
        need = [None] * n
        inc = [False] * n
        alld = [None] * n
        for i, (eng, fn, R, W, isd, _c) in enumerate(items):
            d = {}
            for b in R:
                j = last_w.get(b)
                if j is not None:
                    d[j] = "raw"
                if b.psum:
                    for r_ in readers.get(b, ()):
                        if items[r_][0] != eng and r_ not in d:
                            d[r_] = "rar"
            for b in W:
                j = last_w.get(b)
                if j is not None and j not in d:
                    d[j] = "waw"
                for r_ in readers.get(b, ()):
                    if r_ != i and r_ not in d:
                        d[r_] = "war"
            lst = []
            for j, kind in d.items():
                ej, jd = items[j][0], items[j][4]
                if (not jd) and (not isd) and ej == eng:
                    if eng != "pool" and (kind != "raw" or eng == "pe"):
                        continue
                lst.append(j)
                inc[j] = True
            need[i] = lst
            alld[i] = list(d.keys())
            for b in R:
                readers.setdefault(b, []).append(i)
            for b in W:
                last_w[b] = i
                readers[b] = []
        order = self._schedule(items, alld) if (self.reorder and n > 2) else list(range(n))
        last_eng = {}
        for i in order:
            if not items[i][4]:
                last_eng[items[i][0]] = i
        for e, i in last_eng.items():
            inc[i] = True
        ev = [None] * n
        for i in order:
            eng, fn, R, W, isd, _c = items[i]
            wl = {}
            for j in need[i]:
                s, v = ev[j]
                if wl.get(s, 0) < v:
                    wl[s] = v
            if isd:
                m = self.dcount[eng]
                K = len(self.dsems[eng])
                ds = self.dsems[eng][m % K]
                if m >= K and wl.get(ds, 0) < 16 * (m // K):
                    wl[ds] = 16 * (m // K)
            k_ = self.known[eng]
            pend = [(s, v) for s, v in wl.items() if k_.get(s, 0) < v]
            for s, v in pend[:-1]:
                self._ensure(eng, s, v)
            if isd:
                ins = fn()
                if pend:
                    ins._wait_ge(pend[-1][0], pend[-1][1])
                    k_[pend[-1][0]] = pend[-1][1]
                ins.then_inc(ds, 16)
                ev[i] = (ds, 16 * (m // K + 1))
                self.dcount[eng] = m + 1
            else:
                ins = fn()
                if pend:
                    ins._wait_ge(pend[-1][0], pend[-1][1])
                    k_[pend[-1][0]] = pend[-1][1]
                if inc[i]:
                    self.tick[eng] += 1
                    ins.then_inc(self.sem[eng], 1)
                    ev[i] = (self.sem[eng], self.tick[eng])
            self.n_ins += 1
        self.items = []
        self.barrier()

    def barrier(self):
        for e in self.ENGS:
            for f in self.ENGS:
                if f != e and self.tick[f] > 0:
                    self._ensure(e, self.sem[f], self.tick[f])
            for q, lst in self.dsems.items():
                m = self.dcount[q]
                K = len(lst)
                for k_, s in enumerate(lst):
                    cnt = (m - k_ + K - 1) // K if m > k_ else 0
                    if cnt > 0:
                        self._ensure(e, s, 16 * cnt)


class Ring:
    def __init__(self, tiles, name, bufs=None, psum=False):
        self.tiles = tiles
        self.bufs = bufs if bufs is not None else [Buf(f"{name}{i}", psum) for i in range(len(tiles))]
        self.i = 0

    def next(self):
        k = self.i % len(self.tiles)
        self.i += 1
        return self.tiles[k], self.bufs[k]


def _rope_tables_fm(pos, rot, rows, row0s, scale):
    half = rot // 2
    inv = np.power(np.float32(THETA), -np.arange(half, dtype=np.float32) * np.float32(2.0) / np.float32(rot)).astype(np.float32)
    ang = pos.astype(np.float32)[None, :] * inv[:, None]
    cos = np.cos(ang).astype(np.float32)
    sin = np.sin(ang).astype(np.float32)
    n = len(pos)
    ct = np.full((rows, n), scale, np.float32)
    st = np.zeros((rows, n), np.float32)
    for row0 in row0s:
        ct[row0:row0 + half] = cos * scale
        ct[row0 + half:row0 + rot] = cos * scale
        st[row0:row0 + half] = sin * scale
        st[row0 + half:row0 + rot] = sin * scale
    return ct, st


def _perm_lhsT(rows, blocks):
    P = np.zeros((rows, rows), np.float32)
    for row0, rot in blocks:
        half = rot // 2
        for j in range(half):
            P[row0 + j + half, row0 + j] = -1.0
            P[row0 + j, row0 + j + half] = 1.0
    return P


def _pk(v, k):
    return np.ascontiguousarray(np.asarray(v, np.float32).reshape(k, 128).T)


def _wl(w, k):
    w = np.asarray(w, np.float32)
    return np.ascontiguousarray(w.reshape(k, 128, w.shape[1]).transpose(1, 0, 2))


def build_program(stages=99):
    from contextlib import ExitStack
    nc = bass.Bass("TRN2", target_bir_lowering=False)

    def din(name, shape):
        return nc.dram_tensor(name, list(shape), F32, kind="ExternalInput").ap()

    def dout(name, shape):
        return nc.dram_tensor(name, list(shape), F32, kind="ExternalOutput").ap()

    xw = din("xw", [8192, D]); xs_in = din("xs_in", [64, D])
    w_in = din("w_in", [128, 8, 3232]); g_pre = din("g_pre", [128, 8])
    w_uq = din("w_uq", [128, 3, 768]); g_q = din("g_q", [128, 3])
    w_ukv = din("w_ukv", [128, 2, 1024]); g_kv = din("g_kv", [128, 2])
    w_out0 = din("w_out0", [128, 8, D]); gb_post0 = din("gb_post0", [128, D])
    c_w_in = din("c_w_in", [128, 8, 2304]); g_pre1 = din("g_pre1", [128, 8])
    c_w_out = din("c_w_out", [128, 8, D]); gb_post1 = din("gb_post1", [128, D])
    cosk = din("cosk", [96, 8192]); sink = din("sink", [96, 8192])
    cosq = din("cosq", [96, 2560]); sinq = din("sinq", [96, 2560])
    cosk_s = din("cosk_s", [96, 64]); sink_s = din("sink_s", [96, 64])
    cosq_s = din("cosq_s", [96, 64]); sinq_s = din("sinq_s", [96, 64])
    cos1q = din("cos1q", [128, NQC]); sin1q = din("sin1q", [128, NQC])
    cos1k = din("cos1k", [128, NQC]); sin1k = din("sin1k", [128, NQC])
    cos1q_s = din("cos1q_s", [128, 64]); sin1q_s = din("sin1q_s", [128, 64])
    cos1k_s = din("cos1k_s", [128, 64]); sin1k_s = din("sin1k_s", [128, 64])
    pm96_d = din("pm96", [96, 96]); pm128_d = din("pm128", [128, 128]); ident_d = din("ident", [128, 128])
    valid_d = din("valid", [128, 64])
    bd_d = din("bd", [128, 2, 8, 128]); cb_d = din("cb", [128, 8])
    sinks_d = din("sinks", [128, 16])
    cache_ckv = din("cache_ckv", [PAST, 256]); cache_kr = din("cache_kr", [PAST, 32])
    cache_bk = din("cache_bk", [512, 512]); cache_bv = din("cache_bv", [512, 512])
    cache_ck = din("cache_ck", [128, 128]); cache_cv = din("cache_cv", [128, 128])

    y_p = dout("y_p", [2048, D]); y_s = dout("y_s", [64, D])
    ckv_p = dout("ckv_p", [2048, 256]); kr_p = dout("kr_p", [2048, 32])
    bk_p = dout("bk_p", [512, 512]); bv_p = dout("bv_p", [512, 512])
    ck_p = dout("ck_p", [128, 128]); cv_p = dout("cv_p", [128, 128])
    ckv_s = dout("ckv_s", [64, 256]); kr_s = dout("kr_s", [64, 32])
    bk_s = dout("bk_s", [512, 512]); bv_s = dout("bv_s", [512, 512])
    ck_s = dout("ck_s", [128, 128]); cv_s = dout("cv_s", [128, 128])

    h1d = nc.dram_tensor("h1_scratch", [NQC + 64, D], F32, kind="Internal").ap()
    top = ExitStack()
    sems = {e: top.enter_context(nc.semaphore("sem_" + e)) for e in Rec.ENGS}
    dsems = {q: [top.enter_context(nc.semaphore(f"dsem_{q}{i}")) for i in range(8)] for q in ("sp", "pool")}
    rec = Rec(nc, sems, dsems)
    E = rec.eng

    uid = {"n": 0}

    def sb(es, name, shape, dt):
        uid["n"] += 1
        return es.enter_context(nc.sbuf_tensor(f"s{uid['n']}_{name}", list(shape), dt))

    def psum(es, name, shape, dt):
        uid["n"] += 1
        return es.enter_context(nc.psum_tensor(f"p{uid['n']}_{name}", list(shape), dt))

    def fsz(ap):
        n = 1
        for d_ in list(ap.shape)[1:]:
            n *= int(d_)
        return n

    def ecost(eng, ap):
        f = fsz(ap)
        if eng == "act":
            return f / 1.2 + 200.0
        if eng == "dve":
            return f / 0.96 + 120.0
        return f / 0.45 + 150.0

    def mm(out, lhsT, rhs, start, stop, r, w, sgc=False):
        f32 = rhs.dtype == F32
        rec.op("pe", lambda: nc.tensor.matmul(out, lhsT=lhsT, rhs=rhs, start=start, stop=stop, skip_group_check=sgc), r, w,
               cost=(max(64, fsz(rhs)) / 2.4) * (4 if f32 else 1) + 70.0)

    def tr(out, in_, ident, r, w):
        rec.op("pe", lambda: nc.tensor.transpose(out, in_, ident), r, w, cost=120.0)

    def act(out, in_, func, r, w, **kw):
        rec.op("act", lambda: nc.scalar.activation(out=out, in_=in_, func=func, **kw), r, w, cost=ecost("act", out))

    def cp(eng, out, in_, r, w):
        if eng == "act":
            rec.op("act", lambda: nc.scalar.copy(out=out, in_=in_), r, w, cost=ecost("act", out))
        else:
            rec.op(eng, lambda: E[eng].tensor_copy(out=out, in_=in_), r, w, cost=ecost(eng, out))

    def tt(eng, out, in0, in1, op, r, w):
        rec.op(eng, lambda: E[eng].tensor_tensor(out=out, in0=in0, in1=in1, op=op), r, w, cost=ecost(eng, out))

    def ts(eng, out, in0, s1, op0, r, w):
        rec.op(eng, lambda: E[eng].tensor_scalar(out=out, in0=in0, scalar1=s1, scalar2=None, op0=op0), r, w, cost=ecost(eng, out))

    def stt(eng, out, in0, scalar, in1, op0, op1, r, w):
        rec.op(eng, lambda: E[eng].scalar_tensor_tensor(out=out, in0=in0, scalar=scalar, in1=in1, op0=op0, op1=op1), r, w, cost=ecost(eng, out))

    def recip(out, in_, r, w):
        rec.op("dve", lambda: nc.vector.reciprocal(out=out, in_=in_), r, w, cost=5 * ecost("dve", out))

    def memset(eng, ap, val, w):
        rec.op(eng, lambda: E[eng].memset(ap, val), [], w, cost=ecost(eng, ap))

    def dma(q, out, in_, r, w):
        nb = 1
        for d_ in list(out.shape):
            nb *= int(d_)
        rec.dma(q, lambda: E[q].dma_start(out=out, in_=in_), r, w, cost=2500.0 + nb * 4 / 120.0)

    alt = {"i": 0}

    def alt_eng(choices=("dve", "act")):
        alt["i"] += 1
        return choices[alt["i"] % len(choices)]

    identb = sb(top, "identb", [128, 128], BF16); identf = sb(top, "identf", [128, 128], F32)
    onesb = sb(top, "onesb", [128, 128], BF16); onesf = sb(top, "onesf", [128, 128], F32)
    pm96b = sb(top, "pm96b", [128, 96], BF16); pm128b = sb(top, "pm128b", [128, 128], BF16)
    valid = sb(top, "valid", [128, 64], F32); vone = sb(top, "vone", [128, 64], F32)
    QT_s = sb(top, "QT_s", [128, 8, 64], BF16)
    sga_s = sb(top, "sga_s", [128, 4, 64], BF16)
    mixb_s = sb(top, "mixb_s", [128, 4, 64], BF16)
    cs_new = sb(top, "cs_new", [128, 2, 64], BF16)
    krs_new = sb(top, "krs_new", [128, 64], BF16)
    CONST = Buf("const")
    bQT, bsga, bmixb = Buf("QT"), Buf("sga"), Buf("mixb")
    bQTs, bsgas, bmixbs, bcsn, bkrsn = Buf("QTs"), Buf("sgas"), Buf("mixbs"), Buf("csn"), Buf("krsn")

    with ExitStack() as es:
        stg = sb(es, "cstg", [128, 3, 128], F32)
        dma("sp", identf[:], ident_d[:, :], [], [CONST])
        dma("sp", stg[0:96, 0, 0:96], pm96_d[:, :], [], [CONST])
        dma("sp", stg[:, 1, :], pm128_d[:, :], [], [CONST])
        dma("sp", valid[:], valid_d[:, :], [], [CONST])
        cp("dve", identb[:], identf[:], [CONST], [CONST])
        cp("dve", pm96b[0:96, :], stg[0:96, 0, 0:96], [CONST], [CONST])
        cp("dve", pm128b[:], stg[:, 1, :], [CONST], [CONST])
        memset("dve", onesb[:], 1.0, [CONST]); memset("dve", onesf[:], 1.0, [CONST]); memset("dve", vone[:], 1.0, [CONST])
        rec.flush()

    def prep_w(es_ring, dst, src, K, C, gain, wbuf, c_lo=0, c_hi=None):
        c_hi = C if c_hi is None else c_hi
        CP = 1664
        for k in range(K):
            c = c_lo
            while c < c_hi:
                n = min(CP, c_hi - c)
                st_, sbf = es_ring.next()
                dma("sp" if alt["i"] % 4 < 2 else "pool", st_[:, :n], src[:, k, c:c + n], [], [sbf])
                e = alt_eng(("dve", "act"))
                o = dst[:, k, c - c_lo:c - c_lo + n]
                if gain is None:
                    cp(e, o, st_[:, :n], [sbf], [wbuf])
                elif e == "act":
                    act(o, st_[:, :n], AF.Copy, [sbf, CONST], [wbuf], scale=gain[:, k:k + 1])
                else:
                    ts(e, o, st_[:, :n], gain[:, k:k + 1], ALU.mult, [sbf, CONST], [wbuf])
                c += n

    def frontend(F, tiles, xnT, xnb):
        for ti, (src, rows, srcb) in enumerate(tiles):
            if srcb is None:
                xt, xb = F["xring"].next()
                dma("sp", xt[:rows, :], src, [], [xb])
            else:
                xt, xb = src, srcb
            st_, stb = F["string"].next()
            xs, xsb = F["xsring"].next()
            if F.get("junk") is None:
                act(xs[:rows, :], xt[:rows, :], AF.Square, [xb], [xsb, stb], accum_out=st_[:rows, 0:1])
            else:
                act(F["junk"][:rows, :], xt[:rows, :], AF.Square, [xb], [F["junkb"], stb], accum_out=st_[:rows, 0:1])
            act(st_[:rows, 1:2], st_[:rows, 0:1], AF.Ln, [stb], [stb], scale=1.0 / D, bias=EPS)
            act(st_[:rows, 2:3], st_[:rows, 1:2], AF.Exp, [stb], [stb], scale=-0.5)
            ts("dve", xs[:rows, :], xt[:rows, :], st_[:rows, 2:3], ALU.mult, [xb, stb], [xsb])
            pT, pTb = F["pT"].next()
            for k in range(8):
                tr(pT[:, k, :rows], xs[:rows, k * 128:(k + 1) * 128], identb[:rows, :rows], [xsb, CONST], [pTb])
            cp(alt_eng(("dve", "act")), xnT[:, :, ti * 128:ti * 128 + rows], pT[:, :, :rows], [pTb], [xnb])

    def proj(ps, W, Wb, c0, ncols, xnT, xnb, t0, n, psb):
        for k in range(8):
            mm(ps[0:ncols, 0:n], W[:, k, c0:c0 + ncols], xnT[:, k, t0:t0 + n], k == 0, k == 7, [Wb, xnb], [psb])

    def rms_fm(F, chunks, n, nfeat):
        sq, sqb = F["sqring"].next()
        for c, (ps, psb) in enumerate(chunks):
            act(sq[:, c, :n], ps[:, :n], AF.Square, [psb], [sqb])
        pss, pssb = F["psring"].next()
        for c in range(len(chunks)):
            mm(pss[:, :n], onesb[:, :], sq[:, c, :n], c == 0, c == len(chunks) - 1, [sqb, CONST], [pssb])
        rs, rsb = F["rsring"].next()
        act(rs[:, :n], pss[:, :n], AF.Ln, [pssb], [rsb], scale=1.0 / nfeat, bias=EPS)
        act(rs[:, :n], rs[:, :n], AF.Exp, [rsb], [rsb], scale=-0.5)
        return rs, rsb

    def rope_fm(F, ps, psb, R, p0, p1, n, pm, cosT, sinT, tb, out_bf, outb, out_f32=None, outfb=None):
        qr, qrb = F["qrring"].next()
        cp("act", qr[0:R, :n], ps[0:R, :n], [psb], [qrb])
        ps2, ps2b = F["psring"].next()
        mm(ps2[0:R, :n], pm[0:R, 0:R], qr[0:R, :n], True, True, [qrb, CONST], [ps2b])
        t1, t1b = F["t1ring"].next()
        t2, t2b = F["t2ring"].next()
        tt("dve", t1[p0:p1, :n], ps[p0:p1, :n], cosT, ALU.mult, [psb, tb], [t1b])
        tt("dve", t2[p0:p1, :n], ps2[p0:p1, :n], sinT, ALU.mult, [ps2b, tb], [t2b])
        if out_f32 is not None:
            tt("pool", out_f32, t1[p0:p1, :n], t2[p0:p1, :n], ALU.add, [t1b, t2b], [outfb])
            cp("pool", out_bf, out_f32, [outfb], [outb])
        else:
            tt("pool", out_bf, t1[p0:p1, :n], t2[p0:p1, :n], ALU.add, [t1b, t2b], [outb])

    def silu_fm(F, ps, psb, n, dst, dstb):
        e1, e1b = F["t1ring"].next()
        act(e1[:, :n], ps[:, :n], AF.Exp, [psb], [e1b], scale=-1.0)
        act(e1[:, :n], e1[:, :n], AF.Ln, [e1b], [e1b], bias=1.0)
        act(e1[:, :n], e1[:, :n], AF.Exp, [e1b], [e1b], scale=-1.0)
        tt("dve", dst, ps[:, :n], e1[:, :n], ALU.mult, [psb, e1b], [dstb])

    def norm_out(F, po_views, pob, nq, sg_fn, out_fn, extra_l=None):
        for hf, (po, pb_) in enumerate(zip(po_views, pob)):
            for par in range(2):
                p0 = 64 * par
                Rs, Rsb_ = F["Rring"].next()
                Rv = Rs[p0:p0 + 64, :].rearrange("p (a q) -> p a q", a=2)[:, :, :nq]
                act(Rv, po[64:128, :, par, :nq], AF.Ln, [pb_], [Rsb_], bias=1e-30)
                act(Rv, Rv, AF.Exp, [Rsb_], [Rsb_], scale=-1.0)
                sg, sgb_ = sg_fn(hf, par)
                o, ob = out_fn(hf, par)
                tm, tmb = F["tmring"].next()
                tv = tm[p0:p0 + 64, :].rearrange("p (a q) -> p a q", a=2)[:, :, :nq]
                tt("dve", tv, Rv, sg, ALU.mult, [Rsb_, sgb_], [tmb])
                tt("dve", o, po[0:64, :, par, :nq], tv, ALU.mult, [pb_, tmb], [ob])

    def stage_b2():
        with ExitStack() as es:
            Wp = sb(es, "Wb2", [128, 8, 2048], BF16); Wb = Buf("Wb2")
            gp = sb(es, "gp", [128, 8], F32)
            dma("sp", gp[:], g_pre[:, :], [], [CONST])
            bdb = sb(es, "bdb", [128, 2, 8, 128], BF16); bbd = Buf("bd")
            with ExitStack() as es2:
                ring = Ring([sb(es2, f"stgw{i}", [128, 1664], F32) for i in range(4)], "stgw")
                prep_w(ring, Wp, w_in, 8, 3232, gp, Wb, c_lo=C_QB, c_hi=3232)
                bdf = sb(es2, "bdf", [128, 2, 8, 128], F32); cbt = sb(es2, "cbt", [128, 8], F32)
                dma("sp", bdf[:], bd_d[:, :, :, :], [], [bbd]); dma("sp", cbt[:], cb_d[:, :], [], [bbd])
                rec.op("dve", lambda: nc.vector.tensor_scalar(out=cbt[:], in0=cbt[:], scalar1=-1.0, scalar2=None, op0=ALU.mult), [bbd], [bbd])
                for d_ in range(2):
                    for h in range(8):
                        act(bdb[:, d_, h, :], bdf[:, d_, h, :], AF.Exp, [bbd], [bbd], bias=cbt[:, h:h + 1])
                rec.flush()
            O_QB, O_KB, O_VB, O_GB = 0, 512, 1024, 1536
            F = {}
            F["xring"] = Ring([sb(es, f"xr{i}", [128, D], F32) for i in range(3)], "xr")
            F["xsring"] = Ring([sb(es, f"xs{i}", [128, D], BF16) for i in range(2)], "xs")
            F["junk"] = sb(es, "junk", [128, D], BF16); F["junkb"] = Buf("junk")
            F["string"] = Ring([sb(es, f"st{i}", [128, 4], F32) for i in range(4)], "st")
            F["pT"] = Ring([psum(es, "pT0", [128, 8, 128], BF16)], "pT", psum=True)
            ps2 = [psum(es, f"ps2_{i}", [128, 1024], F32) for i in range(2)]
            ps1 = [psum(es, f"ps1_{i}", [128, 512], F32) for i in range(3)]
            views = [ps2[0][:, 0:512], ps2[0][:, 512:1024], ps2[1][:, 0:512], ps2[1][:, 512:1024]] + [p[:, :] for p in ps1]
            F["psring"] = Ring(views, "psv", psum=True)
            vb_ = F["psring"].bufs
            F["prring"] = Ring([ps1[2][:, :]], "pr", [vb_[6]])
            F["t1ring"] = Ring([sb(es, f"t1_{i}", [128, 512], F32) for i in range(2)], "t1")
            F["Rring"] = Ring([sb(es, f"Rs{i}", [128, 512], F32) for i in range(2)], "Rs")
            F["tmring"] = Ring([sb(es, f"tm{i}", [128, 256], F32) for i in range(2)], "tm")
            xnr = Ring([sb(es, f"xnT{i}", [128, 8, 512], BF16) for i in range(2)], "xnT")
            sgbq = sb(es, "sgbq", [128, 4, 512], BF16); bsgbq = Buf("sgbq")
            qbq = sb(es, "qbq", [128, 4, 512], BF16); bqbq = Buf("qbq")
            kbT = sb(es, "kbT", [128, 4, 12 * 128], BF16)
            vb = sb(es, "vb", [128, 12, 8, 128], BF16)
            bslot = [Buf(f"kvslot{i}") for i in range(12)]
            ptr = Ring([sb(es, f"pTs{i}", [128, 640], BF16) for i in range(4)], "pTs")
            ostg = Ring([sb(es, f"ostg{i}", [128, 512], F32) for i in range(2)], "ostg")
            kbT_s = sb(es, "kbT_s", [128, 4, 640], BF16); vb_s = sb(es, "vb_s", [128, 5, 8, 128], BF16); bkvs = Buf("kvs")

            def b_attention(nq, qT, qTb, qc0, keyT, vaug, kbufs, nks, sg_fn, out_fn):
                po = [ps1[0], ps1[1]]
                pob = [vb_[4], vb_[5]]
                for hp in range(4):
                    for j in range(5):
                        for h in (2 * hp, 2 * hp + 1):
                            c, pb = h // 2, (h % 2) * 64
                            psS = ps2[h % 2]; psSb = [vb_[2 * (h % 2)], vb_[2 * (h % 2) + 1]]
                            nk = nks[j]
                            o = psS[0:nk, j * 128:j * 128 + nq]
                            wb_ = [psSb[0] if j < 4 else psSb[1]]
                            mm(o, keyT(j, c, pb), qT[pb:pb + 64, c, qc0:qc0 + nq], True, True, [kbufs[j], qTb], wb_)
                    for h in (2 * hp, 2 * hp + 1):
                        psS = ps2[h % 2]; psSb = [vb_[2 * (h % 2)], vb_[2 * (h % 2) + 1]]
                        pt, ptb = ptr.next()
                        act(pt[:, 0:512].rearrange("p (j q) -> p j q", j=4)[:, :, 0:nq], psS[:, 0:512].rearrange("p (j q) -> p j q", j=4)[:, :, 0:nq], AF.Exp, [psSb[0]], [ptb])
                        act(pt[0:nks[4], 512:512 + nq], psS[0:nks[4], 512:512 + nq], AF.Exp, [psSb[1]], [ptb])
                        ov = po[h // 4][0:128, :].rearrange("p (a b q) -> p a b q", a=2, b=2)[:, (h % 4) // 2, h % 2, :]
                        ob = [pob[h // 4]]
                        tt("pool", pt[:, 384:384 + nq], pt[:, 384:384 + nq], bdb[:, 1, h, 0:nq], ALU.mult, [ptb, bbd], [ptb])
                        tt("pool", pt[0:nks[4], 512:512 + nq], pt[0:nks[4], 512:512 + nq], bdb[0:nks[4], 0, h, 0:nq], ALU.mult, [ptb, bbd], [ptb])
                        if nq > 64:
                            memset("pool", pt[0:64, 64:128], 0.0, [ptb])
                            memset("pool", pt[64:128, 512:576], 0.0, [ptb])
                        for j in range(5):
                            mm(ov[:, 0:nq], vaug(j, h, 0, nks[j]), pt[0:nks[j], j * 128:j * 128 + nq], j == 0, j == 4, [kbufs[j], ptb], ob)
                pov = [p[:, :].rearrange("p (a b q) -> p a b q", a=2, b=2) for p in po]
                norm_out(F, pov, pob, nq, sg_fn, out_fn)

            def quad(tiles, wt0, full_from, is_sample):
                N = sum(r for _, r, _ in tiles)
                xnT, xnb = xnr.next()
                frontend(F, tiles, xnT, xnb)
                for c in range(4):
                    ps, psb = F["psring"].next()
                    proj(ps, Wp, Wb, O_KB + c * 128, 128, xnT, xnb, 0, N, psb)
                    if is_sample:
                        cp(alt_eng(), kbT_s[:, c, 512:512 + N], ps[:, :N], [psb], [bkvs])
                    else:
                        s0 = (wt0 - 40) % 12
                        cp(alt_eng(), kbT[:, c, s0 * 128:s0 * 128 + N], ps[:, :N], [psb], bslot[s0:s0 + 4])
                want_out = is_sample or wt0 == 60
                for ti, (_, rows, _) in enumerate(tiles):
                    ps, psb = F["psring"].next()
                    for k in range(8):
                        mm(ps[0:rows, :], xnT[:, k, ti * 128:ti * 128 + rows], Wp[:, k, O_VB:O_VB + 512], k == 0, k == 7, [xnb, Wb], [psb])
                    pv = ps[0:rows, :].rearrange("p (h e) -> p h e", h=8)
                    if is_sample:
                        cp("dve", vb_s[0:rows, 4, :, 0:64], pv, [psb], [bkvs])
                    else:
                        s = (wt0 + ti - 40) % 12
                        rec.op("dve", lambda s=s, pv=pv, t=wt0 + ti: nc.vector.tensor_scalar(out=vb[:, s, :, 0:64], in0=pv, scalar1=valid[:, t:t + 1], scalar2=None, op0=ALU.mult), [psb, CONST], [bslot[s]])
                        cp("act", vb[:, s, :, 64:128], valid[:, wt0 + ti:wt0 + ti + 1].unsqueeze(2).to_broadcast([128, 8, 64]), [CONST], [bslot[s]])
                    if want_out:
                        og, ogb = ostg.next()
                        cp("act", og[0:rows, :], ps[0:rows, :], [psb], [ogb])
                        dst = bv_s[448:512, :] if is_sample else bv_p[ti * 128:ti * 128 + 128, :]
                        dma("pool", dst, og[0:rows, :], [ogb], [])
                        ps, psb = F["psring"].next()
                        for k in range(8):
                            mm(ps[0:rows, :], xnT[:, k, ti * 128:ti * 128 + rows], Wp[:, k, O_KB:O_KB + 512], k == 0, k == 7, [xnb, Wb], [psb])
                        og, ogb = ostg.next()
                        cp("act", og[0:rows, :], ps[0:rows, :], [psb], [ogb])
                        dst = bk_s[448:512, :] if is_sample else bk_p[ti * 128:ti * 128 + 128, :]
                        dma("pool", dst, og[0:rows, :], [ogb], [])
                if full_from is None:
                    return
                f0 = full_from
                n = N - f0
                for c in range(4):
                    ps, psb = F["psring"].next()
                    proj(ps, Wp, Wb, O_QB + c * 128, 128, xnT, xnb, f0, n, psb)
                    act(qbq[:, c, 0:n], ps[:, :n], AF.Copy, [psb], [bqbq], scale=B_SCALE)
                    ps, psb = F["psring"].next()
                    proj(ps, Wp, Wb, O_GB + c * 128, 128, xnT, xnb, f0, n, psb)
                    silu_fm(F, ps, psb, n, sgbq[:, c, 0:n], bsgbq)
                if is_sample:
                    def keyT(j, c, pb):
                        return kbT_s[pb:pb + 64, c, j * 128:j * 128 + (128 if j < 4 else 64)]

                    def vaug(j, h, k0, k1):
                        return vb_s[k0:k1, j, h, :]
                    b_attention(64, qbq, bqbq, 0, keyT, vaug, [bkvs] * 5, [128, 128, 128, 128, 64],
                                lambda hf, par: (sgbq[64 * par:64 * par + 64, 2 * hf:2 * hf + 2, 0:64], bsgbq),
                                lambda hf, par: (mixb_s[64 * par:64 * par + 64, 2 * hf:2 * hf + 2, 0:64], bmixbs))
                else:
                    for ti in range(f0 // 128, len(tiles)):
                        t = wt0 + ti
                        qc = ti * 128 - f0
                        oc = (t - T0) * 128
                        slots = [(t - 4 + j - 40) % 12 for j in range(5)]

                        def keyT(j, c, pb, slots=slots):
                            return kbT[pb:pb + 64, c, slots[j] * 128:slots[j] * 128 + 128]

                        def vaug(j, h, k0, k1, slots=slots):
                            return vb[k0:k1, slots[j], h, :]
                        b_attention(128, qbq, bqbq, qc, keyT, vaug, [bslot[s] for s in slots], [128] * 5,
                                    lambda hf, par, qc=qc: (sgbq[64 * par:64 * par + 64, 2 * hf:2 * hf + 2, qc:qc + 128], bsgbq),
                                    lambda hf, par, oc=oc: (mixb[64 * par:64 * par + 64, 2 * hf:2 * hf + 2, oc:oc + 128], bmixb))

            with ExitStack() as es2:
                cst = Ring([sb(es2, f"cst{i}", [128, 512], F32) for i in range(2)], "cst")
                cbf = Ring([sb(es2, f"cbf{i}", [128, 512], BF16) for i in range(2)], "cbf")
                for j in range(4):
                    st_, stb = cst.next()
                    dma("sp", st_[:], cache_bk[j * 128:(j + 1) * 128, :], [], [stb])
                    bf, bfb = cbf.next()
                    cp("dve", bf[:], st_[:], [stb], [bfb])
                    pT, pTb = F["pT"].next()
                    for c in range(4):
                        tr(pT[:, c, :], bf[:, c * 128:(c + 1) * 128], identb[:, :], [bfb, CONST], [pTb])
                    cp("act", kbT_s[:, :, j * 128:(j + 1) * 128], pT[:, 0:4, :], [pTb], [bkvs])
                    st_, stb = cst.next()
                    dma("sp", st_[:], cache_bv[j * 128:(j + 1) * 128, :], [], [stb])
                    cp("dve", vb_s[:, j, :, 0:64], st_[:].rearrange("p (h e) -> p h e", h=8), [stb], [bkvs])
                memset("pool", vb_s[:, :, :, 64:128], 1.0, [bkvs])
                dma("pool", bk_s[0:448, :], cache_bk[64:512, :], [], [])
                dma("pool", bv_s[0:448, :], cache_bv[64:512, :], [], [])
                quad([(xs_in[:, :], 64, None)], None, 0, True)
                rec.flush()
            for q in range(10, 16):
                wt0 = q * 4
                tiles = [(xw[(wt0 + i) * 128:(wt0 + i + 1) * 128, :], 128, None) for i in range(4)]
                full_from = None if q == 10 else (384 if q == 11 else 0)
                quad(tiles, wt0, full_from, False)
            rec.flush()

    def stage_b1():
        with ExitStack() as es:
            Wp = sb(es, "Wb1", [128, 8, 896], BF16); Wb = Buf("Wb1")
            wuq = sb(es, "wuq", [128, 3, 768], BF16); wuqb = Buf("wuq")
            gp = sb(es, "gp1", [128, 8], F32); gq = sb(es, "gq1", [128, 3], F32)
            dma("sp", gp[:], g_pre[:, :], [], [CONST]); dma("sp", gq[:], g_q[:, :], [], [CONST])
            with ExitStack() as es2:
                ring = Ring([sb(es2, f"stgw{i}", [128, 1664], F32) for i in range(4)], "stgw")
                prep_w(ring, Wp[:, :, 0:384], w_in, 8, 3232, gp, Wb, c_lo=0, c_hi=384)
                prep_w(ring, Wp[:, :, 384:896], w_in, 8, 3232, gp, Wb, c_lo=C_GA, c_hi=C_GA + 512)
                prep_w(ring, wuq, w_uq, 3, 768, gq, wuqb)
                rec.flush()
            F = {}
            F["xring"] = Ring([sb(es, f"xr{i}", [128, D], F32) for i in range(4)], "xr")
            F["xsring"] = Ring([sb(es, f"xs{i}", [128, D], BF16) for i in range(3)], "xs")
            F["junk"] = sb(es, "junk", [128, D], BF16); F["junkb"] = Buf("junk")
            F["string"] = Ring([sb(es, f"st{i}", [128, 4], F32) for i in range(4)], "st")
            F["pT"] = Ring([psum(es, f"pT{i}", [128, 8, 128], BF16) for i in range(2)], "pT", psum=True)
            F["psring"] = Ring([psum(es, f"psb1_{i}", [128, 512], F32) for i in range(6)], "psv", psum=True)
            F["t1ring"] = Ring([sb(es, f"t1_{i}", [128, 512], F32) for i in range(3)], "t1")
            F["t2ring"] = Ring([sb(es, f"t2_{i}", [128, 512], F32) for i in range(2)], "t2")
            F["sqring"] = Ring([sb(es, f"sq{i}", [128, 3, 512], BF16) for i in range(2)], "sq")
            F["rsring"] = Ring([sb(es, f"rs{i}", [128, 512], F32) for i in range(2)], "rs")
            F["qrring"] = Ring([sb(es, f"qr{i}", [128, 512], BF16) for i in range(2)], "qr")
            xnr = Ring([sb(es, f"xnT{i}", [128, 8, 512], BF16) for i in range(2)], "xnT")
            qln = sb(es, "qln", [128, 3, 512], BF16); qlnb = Buf("qln")
            tab = Ring([sb(es, f"tab{i}", [128, 2, 512], F32) for i in range(2)], "tab")

            def quad(tiles, f0, cs_ap, sn_ap, QTd, QTb_, qc, sgd, sgb_):
                N = sum(r for _, r, _ in tiles)
                n = N - f0
                xnT, xnb = xnr.next()
                frontend(F, tiles, xnT, xnb)
                tb_, tbb = tab.next()
                dma("pool", tb_[0:96, 0, 0:n], cs_ap, [], [tbb]); dma("pool", tb_[0:96, 1, 0:n], sn_ap, [], [tbb])
                chunks = []
                for c in range(3):
                    ps, psb = F["psring"].next()
                    proj(ps, Wp, Wb, c * 128, 128, xnT, xnb, f0, n, psb)
                    chunks.append((ps, psb))
                rs, rsb = rms_fm(F, chunks, n, 384)
                for c, (ps, psb) in enumerate(chunks):
                    tt("dve", qln[:, c, 0:n], ps[:, :n], rs[:, :n], ALU.mult, [psb, rsb], [qlnb])
                for h in range(8):
                    ps, psb = F["psring"].next()
                    for c in range(3):
                        mm(ps[0:96, 0:n], wuq[:, c, h * 96:(h + 1) * 96], qln[:, c, 0:n], c == 0, c == 2, [wuqb, qlnb], [psb])
                    rope_fm(F, ps, psb, 96, 0, 96, n, pm96b, tb_[0:96, 0, 0:n], tb_[0:96, 1, 0:n], tbb,
                            QTd[0:96, h, qc:qc + n], QTb_)
                for c in range(4):
                    ps, psb = F["psring"].next()
                    proj(ps, Wp, Wb, 384 + c * 128, 128, xnT, xnb, f0, n, psb)
                    silu_fm(F, ps, psb, n, sgd[:, c, qc:qc + n], sgb_)

            quad([(xs_in[:, :], 64, None)], 0, cosq_s[:, :], sinq_s[:, :], QT_s, bQTs, 0, sga_s, bsgas)
            for q in range(11, 16):
                wt0 = q * 4
                tiles = [(xw[(wt0 + i) * 128:(wt0 + i + 1) * 128, :], 128, None) for i in range(4)]
                f0 = 384 if q == 11 else 0
                tc0 = (wt0 - 44) * 128 + f0
                qc = (wt0 - T0) * 128 + f0
                quad(tiles, f0, cosq[:, tc0:tc0 + 512 - f0], sinq[:, tc0:tc0 + 512 - f0], QT, bQT, qc, sga, bsga)
            rec.flush()

    def stage_am():
        with ExitStack() as es:
            cT = sb(es, "cT", [128, 2, 8192], BF16)
            KT = sb(es, "KT", [128, 8192], BF16)
            bcT = [Buf(f"cT{i}") for i in range(16)]
            bKr = [Buf(f"KTr{i}") for i in range(16)]
            bKn = [Buf(f"KTn{i}") for i in range(16)]
            wukv = sb(es, "wukv", [128, 2, 1024], BF16); wukvb = Buf("wukv")
            gkv = sb(es, "gkv", [128, 2], F32)
            dma("sp", gkv[:], g_kv[:, :], [], [CONST])
            with ExitStack() as esA:
                Wl = sb(esA, "Wl", [128, 8, 256], BF16); Wlb = Buf("Wl")
                Wk = sb(esA, "Wk", [128, 8, 96], BF16); Wkb = Buf("Wk")
                gp = sb(esA, "gpA", [128, 8], F32)
                dma("sp", gp[:], g_pre[:, :], [], [CONST])
                memset("pool", Wk[:], 0.0, [Wkb])
                with ExitStack() as es2:
                    ring = Ring([sb(es2, f"stgw{i}", [128, 1664], F32) for i in range(4)], "stgw")
                    prep_w(ring, Wl, w_in, 8, 3232, gp, Wlb, c_lo=C_CKV, c_hi=C_CKV + 256)
                    prep_w(ring, Wk[:, :, 64:96], w_in, 8, 3232, gp, Wkb, c_lo=C_KR, c_hi=C_KR + 32)
                    prep_w(ring, wukv, w_ukv, 2, 1024, None, wukvb)
                    rec.flush()
                F = {}
                F["xring"] = Ring([sb(esA, f"xr{i}", [128, D], F32) for i in range(3)], "xr")
                F["xsring"] = Ring([sb(esA, f"xs{i}", [128, D], BF16) for i in range(3)], "xs")
                F["junk"] = sb(esA, "junk", [128, D], BF16); F["junkb"] = Buf("junk")
                F["string"] = Ring([sb(esA, f"st{i}", [128, 4], F32) for i in range(4)], "st")
                F["pT"] = Ring([psum(esA, f"pT{i}", [128, 8, 128], BF16) for i in range(2)], "pT", psum=True)
                F["psring"] = Ring([psum(esA, f"psa_{i}", [128, 512], F32) for i in range(6)], "psv", psum=True)
                F["t1ring"] = Ring([sb(esA, f"t1_{i}", [128, 512], F32) for i in range(2)], "t1")
                F["t2ring"] = Ring([sb(esA, f"t2_{i}", [128, 512], F32) for i in range(2)], "t2")
                F["sqring"] = Ring([sb(esA, f"sq{i}", [128, 2, 512], BF16) for i in range(2)], "sq")
                F["rsring"] = Ring([sb(esA, f"rs{i}", [128, 512], F32) for i in range(2)], "rs")
                F["qrring"] = Ring([sb(esA, f"qr{i}", [128, 512], BF16) for i in range(2)], "qr")
                xnr = Ring([sb(esA, f"xnT{i}", [128, 8, 512], BF16) for i in range(2)], "xnT")
                tab = Ring([sb(esA, f"tab{i}", [128, 2, 512], F32) for i in range(2)], "tab")
                cf = sb(esA, "cf", [128, 2, 512], F32); cfb = Buf("cf")
                kf = sb(esA, "kf", [128, 512], F32); kfb = Buf("kf")
                ostg = Ring([sb(esA, f"ostgA{i}", [128, 288], F32) for i in range(2)], "ostgA")

                def quadA(tiles, cs_ap, sn_ap, cdst, cb_, kdst, kb_, out_c, out_k):
                    N = sum(r for _, r, _ in tiles)
                    xnT, xnb = xnr.next()
                    frontend(F, tiles, xnT, xnb)
                    tb_, tbb = tab.next()
                    dma("pool", tb_[64:96, 0, 0:N], cs_ap, [], [tbb]); dma("pool", tb_[64:96, 1, 0:N], sn_ap, [], [tbb])
                    chunks = []
                    for c in range(2):
                        ps, psb = F["psring"].next()
                        proj(ps, Wl, Wlb, c * 128, 128, xnT, xnb, 0, N, psb)
                        chunks.append((ps, psb))
                    rs, rsb = rms_fm(F, chunks, N, 256)
                    for c, (ps, psb) in enumerate(chunks):
                        if out_c is not None:
                            stt("dve", cf[:, c, 0:N], ps[:, :N], gkv[:, c:c + 1], rs[:, :N], ALU.mult, ALU.mult, [psb, rsb, CONST], [cfb])
                            cp("pool", cdst[:, c, :], cf[:, c, 0:N], [cfb], [cb_])
                        else:
                            stt("dve", cdst[:, c, :], ps[:, :N], gkv[:, c:c + 1], rs[:, :N], ALU.mult, ALU.mult, [psb, rsb, CONST], [cb_])
                    ps, psb = F["psring"].next()
                    proj(ps, Wk, Wkb, 0, 96, xnT, xnb, 0, N, psb)
                    if out_k is not None:
                        rope_fm(F, ps, psb, 96, 64, 96, N, pm96b, tb_[64:96, 0, 0:N], tb_[64:96, 1, 0:N], tbb, kdst, kb_,
                                out_f32=kf[64:96, 0:N], outfb=kfb)
                    else:
                        rope_fm(F, ps, psb, 96, 64, 96, N, pm96b, tb_[64:96, 0, 0:N], tb_[64:96, 1, 0:N], tbb, kdst, kb_)
                    if out_c is not None:
                        for ti, (_, rows, _) in enumerate(tiles):
                            pso, psob = F["psring"].next()
                            for c in range(2):
                                mm(pso[0:rows, c * 128:(c + 1) * 128], cf[:, c, ti * 128:ti * 128 + rows], identf[:, :], True, True, [cfb, CONST], [psob])
                            mm(pso[0:rows, 256:288], kf[64:96, ti * 128:ti * 128 + rows], identf[64:96, 64:96], True, True, [kfb, CONST], [psob])
                            og, ogb = ostg.next()
                            cp("act", og[0:rows, :], pso[0:rows, 0:288], [psob], [ogb])
                            dma("pool", out_c[ti * 128:ti * 128 + rows, :], og[0:rows, 0:256], [ogb], [])
                            dma("pool", out_k[ti * 128:ti * 128 + rows, :], og[0:rows, 256:288], [ogb], [])

                quadA([(xs_in[:, :], 64, None)], cosk_s[64:96, :], sink_s[64:96, :], cs_new[:, :, 0:64], bcsn,
                      krs_new[64:96, 0:64], bkrsn, ckv_s, kr_s)
                for q in range(16):
                    tiles = [(xw[(q * 4 + i) * 128:(q * 4 + i + 1) * 128, :], 128, None) for i in range(4)]
                    own = q >= 12
                    quadA(tiles, cosk[64:96, q * 512:(q + 1) * 512], sink[64:96, q * 512:(q + 1) * 512],
                          cT[:, :, q * 512:(q + 1) * 512], bcT[q], KT[64:96, q * 512:(q + 1) * 512], bKr[q],
                          ckv_p[(q - 12) * 512:(q - 11) * 512, :] if own else None,
                          kr_p[(q - 12) * 512:(q - 11) * 512, :] if own else None)
                rec.flush()
            if stages < 4:
                return

            def mla(esM, nkb, nk_last, QTd, QTb_, nq, diag0, vld, sg_t, sgb_):
                nacc = (nq + 511) // 512
                accs = [psum(esM, f"acc{i}", [128, 512], F32) for i in range(nacc)]
                accb = [Buf(f"acc{i}", True) for i in range(nacc)]
                sring = Ring([psum(esM, f"psS{i}", [128, 512], F32) for i in range(3)], "psS", psum=True)
                misc = sring
                V4 = sb(esM, "V4", [128, 64, 2, 128], BF16)

                bV = [Buf(f"V4_{i}") for i in range(64)]
                for j_ in range(2):
                    cp("dve" if j_ == 0 else "act", V4[:, 0:nkb, j_, 64:128], vld[:, 0:nkb].unsqueeze(2).to_broadcast([128, nkb, 64]), [CONST], bV[0:nkb])
                ptr = Ring([sb(esM, f"pt{i}", [128, 512], BF16) for i in range(6)], "pt")
                Rr = Ring([sb(esM, f"Rm{i}", [128, 512], F32) for i in range(2)], "Rm")
                tmr = Ring([sb(esM, f"tmm{i}", [128, 512], F32) for i in range(2)], "tmm")
                ngroups = (nq + 511) // 512
                nsb = (nkb + 3) // 4

                def nkeys(kb):
                    return nk_last if kb == nkb - 1 else 128

                def expandK(h, s):
                    ncol = sum(nkeys(kb) for kb in range(4 * s, min(4 * s + 4, nkb)))
                    ps, psb = misc.next()
                    for c in range(2):
                        mm(ps[0:64, 0:ncol], wukv[:, c, h * 128:h * 128 + 64], cT[:, c, s * 512:s * 512 + ncol], c == 0, c == 1, [wukvb, bcT[s]], [psb])
                    cp("dve", KT[0:64, s * 512:s * 512 + ncol], ps[0:64, 0:ncol], [psb], [bKn[s]])

                def expandV(hh):
                    wv = [wukv[:, c, :].rearrange("p (h e) -> p h e", e=128)[:, 2 * hh:2 * hh + 2, 64:128] for c in range(2)]
                    for kb in range(nkb):
                        nk = nkeys(kb)
                        ps, psb = misc.next()
                        pv = ps[0:nk, 0:128].rearrange("p (h e) -> p h e", e=64)
                        for c in range(2):
                            mm(pv, cT[:, c, kb * 128:kb * 128 + nk], wv[c], c == 0, c == 1, [wukvb, bcT[kb // 4]], [psb])
                        rec.op("dve", lambda kb=kb, nk=nk, pv=pv: nc.vector.tensor_scalar(out=V4[0:nk, kb, :, 0:64], in0=pv, scalar1=vld[0:nk, kb:kb + 1], scalar2=None, op0=ALU.mult), [psb, CONST], [bV[kb]])

                def head(h):
                    hj = h % 2
                    work = []
                    for kb in range(nkb):
                        qlo = 0 if diag0 is None else 128 * max(0, kb - diag0)
                        for g in range(ngroups):
                            g0, g1 = g * 512, min(nq, g * 512 + 512)
                            a = max(g0, qlo)
                            if a < g1:
                                work.append((kb, g, a, g1, diag0 is not None and kb >= diag0 and a == qlo))
                    state = {}
                    seen = set()

                    def emitS(i):
                        kb, g, a, b, dg = work[i]
                        nk = nkeys(kb)
                        if kb % 4 == 0 and kb not in seen and kb // 4 + 1 < nsb:
                            expandK(h, kb // 4 + 1)
                        seen.add(kb)
                        ps, psb = sring.next()
                        mm(ps[0:nk, 0:b - a], KT[0:96, kb * 128:kb * 128 + nk], QTd[0:96, h, a:b], True, True, [bKn[kb // 4], bKr[kb // 4], QTb_], [psb])
                        pt, ptb = ptr.next()
                        act(pt[0:nk, 0:b - a], ps[0:nk, 0:b - a], AF.Exp, [psb], [ptb])
                        state[i] = (pt, ptb)

                    def emitPV(i):
                        kb, g, a, b, dg = work[i]
                        nk = nkeys(kb)
                        pt, ptb = state.pop(i)
                        acc = accs[g]; g0 = g * 512
                        first = kb == 0
                        last = kb == nkb - 1
                        if dg:
                            mm(acc[0:128, a - g0:a - g0 + 64], V4[0:64, kb, hj, :], pt[0:64, 0:64], False, True, [bV[kb], ptb], [accb[g]], sgc=True)
                            mm(acc[0:128, a - g0 + 64:a - g0 + 128], V4[:, kb, hj, :], pt[:, 64:128], False, True, [bV[kb], ptb], [accb[g]], sgc=True)
                            if b > a + 128:
                                mm(acc[0:128, a - g0 + 128:b - g0], V4[:, kb, hj, :], pt[:, 128:b - a], False, False, [bV[kb], ptb], [accb[g]], sgc=True)
                        else:
                            mm(acc[0:128, a - g0:b - g0], V4[0:nk, kb, hj, :], pt[0:nk, 0:b - a], first, last, [bV[kb], ptb], [accb[g]], sgc=True)

                    c, pb = h // 2, (h % 2) * 64
                    lastkb = {}
                    for (kb_, g_, a_, b_, dg_) in work:
                        lastkb[g_] = kb_

                    def norm_group(g):
                        g0, g1 = g * 512, min(nq, g * 512 + 512)
                        n = g1 - g0
                        Rs, Rsb_ = Rr.next()
                        act(Rs[pb:pb + 64, 0:n], accs[g][64:128, 0:n], AF.Ln, [accb[g]], [Rsb_], bias=1e-30)
                        act(Rs[pb:pb + 64, 0:n], Rs[pb:pb + 64, 0:n], AF.Exp, [Rsb_], [Rsb_], scale=-1.0)
                        tm, tmb = tmr.next()
                        tt("dve", tm[pb:pb + 64, 0:n], Rs[pb:pb + 64, 0:n], sg_t[pb:pb + 64, c, g0:g1], ALU.mult, [Rsb_, sgb_], [tmb])
                        tt("dve", sg_t[pb:pb + 64, c, g0:g1], accs[g][0:64, 0:n], tm[pb:pb + 64, 0:n], ALU.mult, [accb[g], tmb], [sgb_])

                    expandK(h, 0)
                    for i in range(len(work) + 1):
                        if i < len(work):
                            emitS(i)
                        if i >= 1:
                            emitPV(i - 1)
                            kb_, g_ = work[i - 1][0], work[i - 1][1]
                            if lastkb[g_] == kb_ and (i == len(work) or work[i][0] != kb_ or work[i][1] != g_):
                                norm_group(g_)

                for hh in range(4):
                    expandV(hh)
                    for h in range(2 * hh, 2 * hh + 2):
                        head(h)

            with ExitStack() as esM:
                mla(esM, 64, 128, QT, bQT, NQC, T0, valid, sga, bsga)
                rec.flush()
            with ExitStack() as esS:
                pTs = Ring([psum(esS, "pTs0", [128, 8, 128], BF16)], "pTs", psum=True)
                cst = Ring([sb(esS, f"ccst{i}", [128, 256], F32) for i in range(2)], "ccst")
                cbf = Ring([sb(esS, f"ccbf{i}", [128, 256], BF16) for i in range(2)], "ccbf")
                kst = Ring([sb(esS, f"kst{i}", [128, 32], F32) for i in range(2)], "kst")
                kbf = Ring([sb(esS, f"kbf{i}", [128, 96], BF16) for i in range(2)], "kbf")
                for r_ in kbf.tiles:
                    memset("pool", r_[:], 0.0, kbf.bufs)
                for t in range(32):
                    st_, stb = cst.next()
                    dma("sp", st_[:], cache_ckv[t * 128:(t + 1) * 128, :], [], [stb])
                    bf, bfb = cbf.next()
                    cp(alt_eng(("dve", "pool")), bf[:], st_[:], [stb], [bfb])
                    ks, ksb = kst.next()
                    dma("sp", ks[:], cache_kr[t * 128:(t + 1) * 128, :], [], [ksb])
                    kb_, kbb = kbf.next()
                    cp("pool", kb_[:, 64:96], ks[:], [ksb], [kbb])
                    pT, pTb = pTs.next()
                    for c in range(2):
                        tr(pT[:, c, :], bf[:, c * 128:(c + 1) * 128], identb[:, :], [bfb, CONST], [pTb])
                    tr(pT[0:96, 2, :], kb_[:, 0:96], identb[:, :], [kbb, CONST], [pTb])
                    cp("act", cT[:, :, t * 128:(t + 1) * 128], pT[:, 0:2, :], [pTb], [bcT[t // 4]])
                    cp("dve", KT[64:96, t * 128:(t + 1) * 128], pT[64:96, 2, :], [pTb], [bKr[t // 4]])
                cp("dve", cT[:, :, 4096:4160], cs_new[:, :, 0:64], [bcsn], [bcT[8]])
                cp("dve", KT[64:96, 4096:4160], krs_new[64:96, 0:64], [bkrsn], [bKr[8]])
                with ExitStack() as esM:
                    mla(esM, 33, 64, QT_s, bQTs, 64, None, vone, sga_s, bsgas)
                    rec.flush()

    def make_out_proj(F, psY, tmpr):
        def out_proj(lhs_fn, lhsb, W, Wb_, rows, gb, res, resb, dst, dstb):
            ps, psb = psY.next()
            for hf in range(2):
                for kk in range(8):
                    mm(ps[0:rows, hf * 512:(hf + 1) * 512], lhs_fn(kk), W[:, kk, hf * 512:(hf + 1) * 512], kk == 0, kk == 7, lhsb + [Wb_], [psb])
            st_, stb = F["string"].next()
            tm, tmb = tmpr.next()
            act(tm[:rows, 0:512], ps[0:rows, 0:512], AF.Square, [psb], [tmb, stb], accum_out=st_[:rows, 0:1])
            act(tm[:rows, 512:1024], ps[0:rows, 512:1024], AF.Square, [psb], [tmb, stb], accum_out=st_[:rows, 3:4])
            tt("dve", st_[:rows, 0:1], st_[:rows, 0:1], st_[:rows, 3:4], ALU.add, [stb], [stb])
            act(st_[:rows, 1:2], st_[:rows, 0:1], AF.Ln, [stb], [stb], scale=1.0 / D, bias=EPS)
            act(st_[:rows, 2:3], st_[:rows, 1:2], AF.Exp, [stb], [stb], scale=-0.5)
            for hf in range(2):
                stt("dve", tm[0:rows, hf * 512:(hf + 1) * 512], ps[0:rows, hf * 512:(hf + 1) * 512], st_[:rows, 2:3], gb[0:rows, hf * 512:(hf + 1) * 512], ALU.mult, ALU.mult, [psb, stb, CONST], [tmb])
            tt("dve", dst, tm[0:rows, :], res, ALU.add, [tmb, resb], [dstb])

        return out_proj

    def stage_c0():
        with ExitStack() as es:
            wo0 = sb(es, "wo0", [128, 8, D], BF16); wo0b = Buf("wo0")
            gb0 = sb(es, "gb0", [128, D], F32)
            dma("sp", gb0[:], gb_post0[:, :], [], [CONST])
            with ExitStack() as es2:
                ring = Ring([sb(es2, f"stgw{i}", [128, 1664], F32) for i in range(4)], "stgw")
                prep_w(ring, wo0, w_out0, 8, D, None, wo0b)
                rec.flush()
            F = {}
            F["xring"] = Ring([sb(es, f"xr{i}", [128, D], F32) for i in range(4)], "xr")
            F["string"] = Ring([sb(es, f"st{i}", [128, 4], F32) for i in range(4)], "st")
            psY = Ring([psum(es, f"psY{i}", [128, 1024], F32) for i in range(3)], "psY", psum=True)
            tmpr = Ring([sb(es, f"tmpc{i}", [128, D], F32) for i in range(3)], "tmpc")
            h1r = Ring([sb(es, f"h1_{i}", [128, D], F32) for i in range(4)], "h1")
            out_proj = make_out_proj(F, psY, tmpr)
            for t in [None] + list(range(T0, 64)):
                is_sample = t is None
                rows = 64 if is_sample else 128
                oc = 0 if is_sample else (t - T0) * 128
                A_, B_ = (sga_s, mixb_s) if is_sample else (sga, mixb)
                xt, xb = F["xring"].next()
                dma("sp", xt[:rows, :], xs_in[:, :] if is_sample else xw[t * 128:(t + 1) * 128, :], [], [xb])
                h1, h1b = h1r.next()
                out_proj(lambda kk, oc=oc, rows=rows, A_=A_, B_=B_: (A_[:, kk, oc:oc + rows] if kk < 4 else B_[:, kk - 4, oc:oc + rows]),
                         [bsgas, bmixbs] if is_sample else [bsga, bmixb], wo0, wo0b, rows, gb0, xt[:rows, :], xb, h1[:rows, :], h1b)
                r0 = NQC if is_sample else oc
                dma("pool", h1d[r0:r0 + rows, :], h1[:rows, :], [h1b], [])
            rec.flush()

    def stage_c():
        with ExitStack() as es:
            wc = sb(es, "wc", [128, 8, 2304], BF16); wcb = Buf("wc")
            wo1 = sb(es, "wo1", [128, 8, D], BF16); wo1b = Buf("wo1")
            gp1 = sb(es, "gp1c", [128, 8], F32)
            gb1 = sb(es, "gb1", [128, D], F32)
            esk = sb(es, "esk", [128, 16], F32)
            dma("sp", gp1[:], g_pre1[:, :], [], [CONST])
            dma("sp", gb1[:], gb_post1[:, :], [], [CONST]); dma("sp", esk[:], sinks_d[:, :], [], [CONST])
            act(esk[:], esk[:], AF.Exp, [CONST], [CONST])
            with ExitStack() as es2:
                ring = Ring([sb(es2, f"stgw{i}", [128, 1664], F32) for i in range(4)], "stgw")
                prep_w(ring, wc, c_w_in, 8, 2304, gp1, wcb)
                prep_w(ring, wo1, c_w_out, 8, D, None, wo1b)
                rec.flush()
            F = {}
            F["xsring"] = Ring([sb(es, f"xs{i}", [128, D], BF16) for i in range(2)], "xs")
            F["string"] = Ring([sb(es, f"st{i}", [128, 4], F32) for i in range(4)], "st")
            F["pT"] = Ring([psum(es, "pT0", [128, 8, 128], BF16)], "pT", psum=True)
            psY = Ring([psum(es, f"psY{i}", [128, 1024], F32) for i in range(1)], "psY", psum=True)
            F["psring"] = Ring([psum(es, f"psc_{i}", [128, 512], F32) for i in range(5)], "psv", psum=True)
            F["t1ring"] = Ring([sb(es, f"t1_{i}", [128, 512], F32) for i in range(2)], "t1")
            F["t2ring"] = Ring([sb(es, f"t2_{i}", [128, 512], F32) for i in range(2)], "t2")
            F["qrring"] = Ring([sb(es, f"qr{i}", [128, 512], BF16) for i in range(2)], "qr")
            h1r = Ring([sb(es, f"h1_{i}", [128, D], F32) for i in range(6)], "h1")
            tmpr = Ring([sb(es, f"tmpc{i}", [128, D], F32) for i in range(2)], "tmpc")
            xn1r = Ring([sb(es, f"xn1T{i}", [128, 8, 512], BF16) for i in range(2)], "xn1T")
            q1r = Ring([sb(es, f"q1T{i}", [128, 8, 512], BF16) for i in range(2)], "q1T")
            cur = {}
            sg1r = Ring([sb(es, f"sg1_{i}", [128, 8, 512], BF16) for i in range(2)], "sg1")
            mx1 = sb(es, "mx1", [128, 8, 512], BF16); mx1b = Buf("mx1")
            tab = Ring([sb(es, f"tab{i}", [128, 4, 512], F32) for i in range(1)], "tab")
            k1f = sb(es, "k1f", [128, 512], F32); k1fb = Buf("k1f")
            k1h = sb(es, "k1h", [128, 512], BF16); k1hb = Buf("k1h")
            K2 = sb(es, "K2", [128, 2, NQC], BF16); V1 = sb(es, "V1", [128, NT, 2, 128], BF16)
            bkv = [Buf(f"kv1_{i}") for i in range(NT)]
            K2s = sb(es, "K2s", [128, 2, 192], BF16); V1s = sb(es, "V1s", [128, 2, 2, 128], BF16); bkvs = Buf("kv1s")
            ptr = Ring([sb(es, f"ptc{i}", [128, 2, 512], BF16) for i in range(3)], "ptc")
            Rr = Ring([sb(es, f"Rc{i}", [128, 512], F32) for i in range(2)], "Rc")
            tmr = Ring([sb(es, f"tmc{i}", [128, 512], F32) for i in range(1)], "tmc")
            ostg = Ring([sb(es, f"ostgc{i}", [128, 128], F32) for i in range(2)], "ostgc")
            out_proj = make_out_proj(F, psY, tmpr)

            def attn_tile(nq, qc, kprev, kown, nk_own, vprev, vown, kvbufs, eoff):
                q1T, q1b = cur["q1T"]
                sg1, sg1b = cur["sg1"]
                for g in range(2):
                    for par in range(2):
                        pb = 64 * par
                        pt, ptb = ptr.next()
                        for ki, (kfn, nk) in enumerate(((kprev, 128), (kown, nk_own))):
                            ps, psb = F["psring"].next()
                            pv = ps[0:nk, 0:4 * nq].rearrange("p (j q) -> p j q", j=4)
                            mm(pv, kfn(g, par), q1T[pb:pb + 64, 4 * g:4 * g + 4, qc:qc + nq], True, True, [kvbufs[ki], q1b], [psb])
                            act(pt[0:nk, ki, 0:4 * nq].rearrange("p (j q) -> p j q", j=4), pv, AF.Exp, [psb], [ptb])
                        p0v = pt[:, 0, 0:4 * nq].rearrange("p (j q) -> p j q", j=4)
                        p1v = pt[:, 1, 0:4 * nq].rearrange("p (j q) -> p j q", j=4)
                        acc, accb_ = F["psring"].next()
                        av = acc[0:128, 0:4 * nq].rearrange("p (j q) -> p j q", j=4)
                        if nq > 64:
                            memset("pool", p0v[0:64, :, 64:128], 0.0, [ptb])
                            memset("pool", p1v[64:128, :, 0:64], 0.0, [ptb])
                        mm(av[:, :, 0:nq], vprev(g, 0, 128), p0v[:, :, 0:nq], True, False, [kvbufs[0], ptb], [accb_])
                        mm(av[:, :, 0:nq], vown(g, 0, nk_own), p1v[0:nk_own, :, 0:nq], False, True, [kvbufs[1], ptb], [accb_])
                        Rs, Rsb_ = Rr.next()
                        Rv = Rs[0:128, 0:4 * nq].rearrange("p (j q) -> p j q", j=4)
                        for j in range(4):
                            h = 8 * g + 2 * j + par
                            act(Rv[pb:pb + 64, j, :], av[64:128, j, 0:nq], AF.Ln, [accb_, CONST], [Rsb_], bias=esk[64:128, h:h + 1])
                        act(Rv[pb:pb + 64], Rv[pb:pb + 64], AF.Exp, [Rsb_], [Rsb_], scale=-1.0)
                        tm, tmb = tmr.next()
                        tv = tm[pb:pb + 64, 0:4 * nq].rearrange("p (j q) -> p j q", j=4)
                        tt("dve", tv, Rv[pb:pb + 64], sg1[pb:pb + 64, 4 * g:4 * g + 4, qc:qc + nq], ALU.mult, [Rsb_, sg1b], [tmb])
                        tt("dve", mx1[pb:pb + 64, 4 * g:4 * g + 4, qc:qc + nq], av[0:64, :, 0:nq], tv, ALU.mult, [accb_, tmb], [mx1b])

            def group(tl, is_sample):
                N = sum(r for _, r in tl)
                h1s = []
                for ti, (t, rows) in enumerate(tl):
                    oc = 0 if is_sample else (t - T0) * 128
                    A_, B_ = (sga_s, mixb_s) if is_sample else (sga, mixb)
                    h1, h1b = h1r.next()
                    r0 = NQC if is_sample else oc
                    dma("sp", h1[:rows, :], h1d[r0:r0 + rows, :], [], [h1b])
                    h1s.append((h1, rows, h1b))
                xn1, xn1b = xn1r.next()
                q1T, q1b = q1r.next()
                cur["q1T"] = (q1T, q1b)
                sg1, sg1b = sg1r.next()
                cur["sg1"] = (sg1, sg1b)
                frontend(F, [(h1[:, :], rows, hb) for h1, rows, hb in h1s], xn1, xn1b)
                tb_, tbb = tab.next()
                c0 = 0 if is_sample else (tl[0][0] - T0) * 128
                srcs = (cos1q_s, sin1q_s, cos1k_s, sin1k_s) if is_sample else (cos1q, sin1q, cos1k, sin1k)
                for i_, s_ in enumerate(srcs):
                    dma("pool", tb_[:, i_, 0:N], s_[:, c0:c0 + N], [], [tbb])
                for c in range(8):
                    ps, psb = F["psring"].next()
                    proj(ps, wc, wcb, C1_Q + c * 128, 128, xn1, xn1b, 0, N, psb)
                    rope_fm(F, ps, psb, 128, 0, 128, N, pm128b, tb_[:, 0, 0:N], tb_[:, 1, 0:N], tbb, q1T[:, c, 0:N], q1b)
                    ps, psb = F["psring"].next()
                    proj(ps, wc, wcb, C1_G + c * 128, 128, xn1, xn1b, 0, N, psb)
                    silu_fm(F, ps, psb, N, sg1[:, c, 0:N], sg1b)
                ps, psb = F["psring"].next()
                proj(ps, wc, wcb, C1_K, 128, xn1, xn1b, 0, N, psb)
                rope_fm(F, ps, psb, 128, 0, 128, N, pm128b, tb_[:, 2, 0:N], tb_[:, 3, 0:N], tbb, k1h[:, 0:N], k1hb, out_f32=k1f[:, 0:N], outfb=k1fb)
                col = 0
                for ti, (t, rows) in enumerate(tl):
                    if is_sample:
                        Kd, kc, kb_ = K2s, 128, bkvs
                    else:
                        Kd, kc, kb_ = K2, (t - T0) * 128, bkv[t - T0]
                    for g in range(2):
                        for par in range(2):
                            cp("dve", Kd[64 * par:64 * par + 64, g, kc:kc + rows], k1h[64 * g:64 * g + 64, col:col + rows], [k1hb], [kb_])
                    ps, psb = F["psring"].next()
                    for k in range(8):
                        mm(ps[0:rows, 0:128], xn1[:, k, col:col + rows], wc[:, k, C1_V:C1_V + 128], k == 0, k == 7, [xn1b, wcb], [psb])
                    pv = ps[0:rows, 0:128].rearrange("p (g e) -> p g e", g=2)
                    if is_sample:
                        cp("dve", V1s[0:rows, 1, :, 0:64], pv, [psb], [kb_])
                    else:
                        rec.op("dve", lambda t=t, pv=pv: nc.vector.tensor_scalar(out=V1[:, t - T0, :, 0:64], in0=pv, scalar1=valid[:, t:t + 1], scalar2=None, op0=ALU.mult), [psb, CONST], [kb_])
                        cp("act", V1[:, t - T0, :, 64:128], valid[:, t:t + 1].unsqueeze(2).to_broadcast([128, 2, 64]), [CONST], [kb_])
                    if is_sample or t == 63:
                        og, ogb = ostg.next()
                        cp("act", og[0:rows, :], ps[0:rows, 0:128], [psb], [ogb])
                        dma("pool", cv_s[64:128, :] if is_sample else cv_p[:, :], og[0:rows, :], [ogb], [])
                        ps, psb = F["psring"].next()
                        mm(ps[0:rows, 0:128], k1f[:, col:col + rows], identf[:, :], True, True, [k1fb, CONST], [psb])
                        og, ogb = ostg.next()
                        cp("act", og[0:rows, :], ps[0:rows, 0:128], [psb], [ogb])
                        dma("pool", ck_s[64:128, :] if is_sample else ck_p[:, :], og[0:rows, :], [ogb], [])
                    col += rows
                col = 0
                for ti, (t, rows) in enumerate(tl):
                    if is_sample:
                        attn_tile(64, 0,
                                  lambda g, par: K2s[64 * par:64 * par + 64, g, 0:128], lambda g, par: K2s[64 * par:64 * par + 64, g, 128:192], 64,
                                  lambda g, k0, k1: V1s[k0:k1, 0, g, :], lambda g, k0, k1: V1s[k0:k1, 1, g, :], [bkvs, bkvs], 0)
                    elif t > T0:
                        j = t - T0
                        attn_tile(128, col,
                                  lambda g, par, j=j: K2[64 * par:64 * par + 64, g, (j - 1) * 128:j * 128],
                                  lambda g, par, j=j: K2[64 * par:64 * par + 64, g, j * 128:(j + 1) * 128], 128,
                                  lambda g, k0, k1, j=j: V1[k0:k1, j - 1, g, :], lambda g, k0, k1, j=j: V1[k0:k1, j, g, :],
                                  [bkv[j - 1], bkv[j]], 0)
                    col += rows
                col = 0
                for ti, (t, rows) in enumerate(tl):
                    if is_sample or t > T0:
                        h1, _, h1b = h1s[ti]
                        y, yb = tmpr.next()
                        out_proj(lambda kk, col=col, rows=rows: mx1[:, kk, col:col + rows], [mx1b], wo1, wo1b, rows, gb1, h1[:rows, :], h1b, y[:rows, :], yb)
                        dma("sp", y_s[:, :] if is_sample else y_p[(t - 48) * 128:(t - 47) * 128, :], y[:rows, :], [yb], [])
                    col += rows

            with ExitStack() as es2:
                cs_ = tmpr.tiles[0][:, 0:256].rearrange("p (a b) -> p a b", a=2); bb_ = tmpr.bufs[0]
                cb2 = F["xsring"].tiles[0][:, 0:128]; cbb = F["xsring"].bufs[0]
                dma("sp", cs_[:, 0, :], cache_ck[:, :], [], [bb_]); dma("sp", cs_[:, 1, :], cache_cv[:, :], [], [bb_])
                cp("dve", cb2, cs_[:, 0, :], [bb_], [cbb])
                pT, pTb = F["pT"].next()
                tr(pT[:, 0, :], cb2, identb[:, :], [cbb, CONST], [pTb])
                for g in range(2):
                    for par in range(2):
                        cp("dve", K2s[64 * par:64 * par + 64, g, 0:128], pT[64 * g:64 * g + 64, 0, :], [pTb], [bkvs])
                cp("dve", V1s[:, 0, :, 0:64], cs_[:, 1, :].rearrange("p (g e) -> p g e", g=2), [bb_], [bkvs])
                memset("pool", V1s[:, :, :, 64:128], 1.0, [bkvs])
                dma("pool", ck_s[0:64, :], cache_ck[64:128, :], [], [])
                dma("pool", cv_s[0:64, :], cache_cv[64:128, :], [], [])
                rec.flush()
            group([(None, 64)], True)
            group([(T0, 128)], False)
            for q in range(12, 16):
                group([(q * 4 + i, 128) for i in range(4)], False)
            rec.flush()

    with ExitStack() as esP:
        sga = sb(esP, "sga", [128, 4, NQC], BF16)
        mixb = sb(esP, "mixb", [128, 4, NQC], BF16)
        stage_b2()
        with ExitStack() as esQ:
            QT = sb(esQ, "QT", [128, 8, NQC], BF16)
            if stages >= 2:
                stage_b1()
            if stages >= 3:
                stage_am()
        if stages >= 5:
            stage_c0()
    if stages >= 5:
        stage_c()
    rec.barrier()
    top.close()
    return nc, rec


_PROG = {}


def _get_prog(stages):
    if stages not in _PROG:
        _PROG[stages] = build_program(stages)
    return _PROG[stages]


def kernel(x_prompt, x_sample, cache_a_ckv, cache_a_krope, cache_b_k, cache_b_v, cache_c_k, cache_c_v,
           ab_pre_norm, ab_post_norm, ab_w_in, ab_q_norm, ab_kv_norm, ab_w_uq, ab_w_ukv, ab_rel_bias, ab_w_out,
           c_pre_norm, c_post_norm, c_w_in, c_sinks, c_w_out):
    stages = int(os.environ.get("KSTAGES", "99"))
    f = lambda a: np.ascontiguousarray(np.asarray(a, np.float32))
    x_prompt = f(x_prompt); x_sample = f(x_sample)
    shared = {
        "w_in": _wl(ab_w_in[0], 8), "g_pre": _pk(ab_pre_norm[0], 8),
        "w_uq": _wl(ab_w_uq[0], 3), "g_q": _pk(ab_q_norm[0], 3),
        "w_ukv": _wl(ab_w_ukv[0], 2), "g_kv": _pk(ab_kv_norm[0], 2),
        "w_out0": _wl(ab_w_out[0], 8), "gb_post0": f(np.broadcast_to(np.asarray(ab_post_norm[0], np.float32)[None, :], (128, D))),
        "c_w_in": _wl(c_w_in[0], 8), "g_pre1": _pk(c_pre_norm[0], 8),
        "c_w_out": _wl(c_w_out[0], 8), "gb_post1": f(np.broadcast_to(np.asarray(c_post_norm[0], np.float32)[None, :], (128, D))),
        "pm96": _perm_lhsT(96, [(64, 32)]), "pm128": _perm_lhsT(128, [(0, 16), (64, 16)]),
        "ident": np.eye(128, dtype=np.float32),
        "sinks": f(np.broadcast_to(np.asarray(c_sinks[0], np.float32)[None, :], (128, 16))),
    }
    tbl = np.asarray(ab_rel_bias[0], np.float32)
    kk = np.arange(128)[:, None]; qq = np.arange(128)[None, :]
    bd = np.zeros((128, 2, 8, 128), np.float32)
    for d_ in range(2):
        idx = np.clip(128 * d_ + qq - kk, -128, 128) + 128
        bd[:, d_, :, :] = np.transpose(tbl[:, idx], (1, 0, 2))
    shared["bd"] = bd
    shared["cb"] = f(np.broadcast_to(tbl[None, :, 256], (128, 8)))
    pos_s = PAST + np.arange(64)
    shared["cosk_s"], shared["sink_s"] = _rope_tables_fm(pos_s, 32, 96, [64], 1.0)
    shared["cosq_s"], shared["sinq_s"] = _rope_tables_fm(pos_s, 32, 96, [64], A_SCALE)
    shared["cos1q_s"], shared["sin1q_s"] = _rope_tables_fm(pos_s, 16, 128, [0, 64], C_SCALE)
    shared["cos1k_s"], shared["sin1k_s"] = _rope_tables_fm(pos_s, 16, 128, [0, 64], 1.0)
    tabs = {}
    in_maps = []
    for core in range(8):
        b, c = core // 4, core % 4
        end = 2048 * (c + 1)
        start = end - 8192
        m = dict(shared)
        xw = np.zeros((8192, D), np.float32)
        lo = max(0, -start)
        xw[lo:] = x_prompt[b, start + lo:end]
        m["xw"] = xw
        m["xs_in"] = f(x_sample[core])
        if c not in tabs:
            pos = start + np.arange(8192)
            t = {}
            t["cosk"], t["sink"] = _rope_tables_fm(pos, 32, 96, [64], 1.0)
            t["cosq"], t["sinq"] = _rope_tables_fm(pos[5632:], 32, 96, [64], A_SCALE)
            t["cos1q"], t["sin1q"] = _rope_tables_fm(pos[6016:], 16, 128, [0, 64], C_SCALE)
            t["cos1k"], t["sin1k"] = _rope_tables_fm(pos[6016:], 16, 128, [0, 64], 1.0)
            t["valid"] = f((pos >= 0).astype(np.float32).reshape(64, 128).T)
            tabs[c] = t
        m.update(tabs[c])
        m["cache_ckv"] = f(cache_a_ckv[0, core]); m["cache_kr"] = f(cache_a_krope[0, core])
        m["cache_bk"] = f(np.asarray(cache_b_k[0, core]).reshape(512, 512)); m["cache_bv"] = f(np.asarray(cache_b_v[0, core]).reshape(512, 512))
        m["cache_ck"] = f(np.asarray(cache_c_k[0, core]).reshape(128, 128)); m["cache_cv"] = f(np.asarray(cache_c_v[0, core]).reshape(128, 128))
        in_maps.append(m)
    nc, _ = _get_prog(stages)
    res = run_bass_kernel_spmd(nc, in_maps, core_ids=list(range(8)))
    R = res.results
    y_prompt = np.zeros((2, SEQ, D), np.float32); y_sample = np.zeros((8, 64, D), np.float32)
    a_ckv_p = np.zeros((1, 2, SEQ, 256), np.float32); a_kr_p = np.zeros((1, 2, SEQ, 32), np.float32)
    b_k_p = np.zeros((1, 2, 512, 8, 64), np.float32); b_v_p = np.zeros((1, 2, 512, 8, 64), np.float32)
    c_k_p = np.zeros((1, 2, 128, 2, 64), np.float32); c_v_p = np.zeros((1, 2, 128, 2, 64), np.float32)
    a_ckv_s = np.zeros((1, 8, 64, 256), np.float32); a_kr_s = np.zeros((1, 8, 64, 32), np.float32)
    b_k_s = np.zeros((1, 8, 512, 8, 64), np.float32); b_v_s = np.zeros((1, 8, 512, 8, 64), np.float32)
    c_k_s = np.zeros((1, 8, 128, 2, 64), np.float32); c_v_s = np.zeros((1, 8, 128, 2, 64), np.float32)
    for core in range(8):
        b, c = core // 4, core % 4
        r = R[core]
        sl = slice(2048 * c, 2048 * (c + 1))
        y_prompt[b, sl] = r["y_p"]; y_sample[core] = r["y_s"]
        a_ckv_p[0, b, sl] = r["ckv_p"]; a_kr_p[0, b, sl] = r["kr_p"]
        if c == 3:
            b_k_p[0, b] = r["bk_p"].reshape(512, 8, 64); b_v_p[0, b] = r["bv_p"].reshape(512, 8, 64)
            c_k_p[0, b] = r["ck_p"].reshape(128, 2, 64); c_v_p[0, b] = r["cv_p"].reshape(128, 2, 64)
        a_ckv_s[0, core] = r["ckv_s"]; a_kr_s[0, core] = r["kr_s"]
        b_k_s[0, core] = r["bk_s"].reshape(512, 8, 64); b_v_s[0, core] = r["bv_s"].reshape(512, 8, 64)
        c_k_s[0, core] = r["ck_s"].reshape(128, 2, 64); c_v_s[0, core] = r["cv_s"].reshape(128, 2, 64)
    return (y_prompt, y_sample, a_ckv_p, a_kr_p, b_k_p, b_v_p, c_k_p, c_v_p,
            a_ckv_s, a_kr_s, b_k_s, b_v_s, c_k_s, c_v_s)
```

```python
import os
import numpy as np
import concourse.bass as bass
import concourse.mybir as mybir
from concourse.bass_utils import run_bass_kernel_spmd

F32, BF16 = mybir.dt.float32, mybir.dt.bfloat16
AF = mybir.ActivationFunctionType
ALU = mybir.AluOpType

D = 1024
SEQ = 8192
PAST = 4096
CH = 64
EPS = 1e-6
THETA = 500000.0
A_SCALE = 96 ** -0.5
B_SCALE = 64 ** -0.5
C_SCALE = 64 ** -0.5
C_QLAT, C_CKV, C_KR, C_GA, C_QB, C_KB, C_VB, C_GB = 0, 384, 640, 672, 1184, 1696, 2208, 2720
C1_Q, C1_K, C1_V, C1_G = 0, 1024, 1152, 1280
NT = 17
T0 = 47
NQC = NT * 128


class Buf:
    __slots__ = ("name", "psum")

    def __init__(self, name, psum=False):
        self.name = name
        self.psum = psum


class Rec:
    ENGS = ("pe", "act", "dve", "pool", "sp")

    def __init__(self, nc, sems, dsems):
        self.nc = nc
        self.eng = {"pe": nc.tensor, "act": nc.scalar, "dve": nc.vector, "pool": nc.gpsimd, "sp": nc.sync}
        self.sem = sems
        self.tick = {e: 0 for e in self.ENGS}
        self.known = {e: {} for e in self.ENGS}
        self.dsems = dsems
        self.dcount = {q: 0 for q in dsems}
        self.items = []
        self.n_ins = 0
        self.reorder = os.environ.get("KREORDER", "1") == "1"

    def op(self, eng, fn, r=(), w=(), cost=300.0):
        self.items.append((eng, fn, tuple(r), tuple(w), False, cost))

    def dma(self, q, fn, r=(), w=(), cost=3000.0):
        self.items.append((q, fn, tuple(r), tuple(w), True, cost))

    def _schedule(self, items, alld, W=640):
        n = len(items)
        succ = [[] for _ in range(n)]
        left = [0] * n
        for i in range(n):
            left[i] = len(alld[i])
            for j in alld[i]:
                succ[j].append(i)
        rdy = [0.0] * n
        fin = [0.0] * n
        engfree = {}
        done = [False] * n
        order = []
        ready = [i for i in range(min(n, W)) if left[i] == 0]
        hi = min(n, W)
        lo = 0
        while len(order) < n:
            best, bt = -1, None
            for i in ready:
                t = max(rdy[i], engfree.get(items[i][0], 0.0))
                if bt is None or t < bt - 1e-9 or (abs(t - bt) <= 1e-9 and i < best):
                    best, bt = i, t
            i = best
            ready.remove(i)
            eng, isd, cost = items[i][0], items[i][4], items[i][5]
            if isd:
                engfree[eng] = bt + 60.0
                fin[i] = bt + cost
            else:
                engfree[eng] = bt + cost
                fin[i] = bt + cost
            done[i] = True
            order.append(i)
            for k in succ[i]:
                left[k] -= 1
                if fin[i] > rdy[k]:
                    rdy[k] = fin[i]
                if left[k] == 0 and k < hi:
                    ready.append(k)
            while lo < n and done[lo]:
                lo += 1
            nh = min(n, lo + W)
            while hi < nh:
                if left[hi] == 0 and not done[hi]:
                    ready.append(hi)
                hi += 1
        return order

    def _ensure(self, e, sem, val):
        k = self.known[e]
        if k.get(sem, 0) < val:
            self.eng[e].wait_ge(sem, val)
            k[sem] = val

    def flush(self):
        items = self.items
        n = len(items)
        last_w = {}
        readers = {}
        need = [None] * n
        inc = [False] * n
        alld = [None] * n
        for i, (eng, fn, R, W, isd, _c) in enumerate(items):
            d = {}
            for b in R:
                j = last_w.get(b)
                if j is not None:
                    d[j] = "raw"
                if b.psum:
                    for r_ in readers.get(b, ()):
                        if items[r_][0] != eng and r_ not in d:
                            d[r_] = "rar"
            for b in W:
                j = last_w.get(b)
                if j is not None and j not in d:
                    d[j] = "waw"
                for r_ in readers.get(b, ()):
                    if r_ != i and r_ not in d:
                        d[r_] = "war"
            lst = []
            for j, kind in d.items():
                ej, jd = items[j][0], items[j][4]
                if (not jd) and (not isd) and ej == eng:
                    if eng != "pool" and (kind != "raw" or eng == "pe"):
                        continue
                lst.append(j)
                inc[j] = True
            need[i] = lst
            alld[i] = list(d.keys())
            for b in R:
                readers.setdefault(b, []).append(i)
            for b in W:
                last_w[b] = i
                readers[b] = []
        order = self._schedule(items, alld) if (self.reorder and n > 2) else list(range(n))
        last_eng = {}
        for i in order:
            if not items[i][4]:
                last_eng[items[i][0]] = i
        for e, i in last_eng.items():
            inc[i] = True
        ev = [None] * n
        snap = [None] * n
        pos = [0] * n
        for p_, i_ in enumerate(order):
            pos[i_] = p_
        for i in order:
            eng, fn, R, W, isd, _c = items[i]
            k_ = self.known[eng]
            pend = []
            for j in sorted(need[i], key=lambda j_: -pos[j_]):
                s, v = ev[j]
                if k_.get(s, 0) >= v:
                    continue
                pend.append((s, v))
                k_[s] = v
                for s2, v2 in snap[j].items():
                    if k_.get(s2, 0) < v2:
                        k_[s2] = v2
            if isd:
                m = self.dcount[eng]
                K = len(self.dsems[eng])
                ds = self.dsems[eng][m % K]
                if m >= K and k_.get(ds, 0) < 16 * (m // K):
                    pend.append((ds, 16 * (m // K)))
                    k_[ds] = 16 * (m // K)
            for s, v in pend[:-1]:
                self.eng[eng].wait_ge(s, v)
            if isd:
                ins = fn()
                if pend:
                    ins._wait_ge(pend[-1][0], pend[-1][1])
                ins.then_inc(ds, 16)
                ev[i] = (ds, 16 * (m // K + 1))
                snap[i] = dict(k_)
                self.dcount[eng] = m + 1
            else:
                ins = fn()
                if pend:
                    ins._wait_ge(pend[-1][0], pend[-1][1])
                if inc[i]:
                    self.tick[eng] += 1
                    ins.then_inc(self.sem[eng], 1)
                    ev[i] = (self.sem[eng], self.tick[eng])
                    snap[i] = dict(k_)
            self.n_ins += 1
        self.items = []
        self.barrier()

    def barrier(self):
        for e in self.ENGS:
            for f in self.ENGS:
                if f != e and self.tick[f] > 0:
                    self._ensure(e, self.sem[f], self.tick[f])
            for q, lst in self.dsems.items():
                m = self.dcount[q]
                K = len(lst)
                for k_, s in enumerate(lst):
                    cnt = (m - k_ + K - 1) // K if m > k_ else 0
                    if cnt > 0:
                        self._ensure(e, s, 16 * cnt)


class Ring:
    def __init__(self, tiles, name, bufs=None, psum=False):
        self.tiles = tiles
        self.bufs = bufs if bufs is not None else [Buf(f"{name}{i}", psum) for i in range(len(tiles))]
        self.i = 0

    def next(self):
        k = self.i % len(self.tiles)
        self.i += 1
        return self.tiles[k], self.bufs[k]


def _rope_tables_fm(pos, rot, rows, row0s, scale):
    half = rot // 2
    inv = np.power(np.float32(THETA), -np.arange(half, dtype=np.float32) * np.float32(2.0) / np.float32(rot)).astype(np.float32)
    ang = pos.astype(np.float32)[None, :] * inv[:, None]
    cos = np.cos(ang).astype(np.float32)
    sin = np.sin(ang).astype(np.float32)
    n = len(pos)
    ct = np.full((rows, n), scale, np.float32)
    st = np.zeros((rows, n), np.float32)
    for row0 in row0s:
        ct[row0:row0 + half] = cos * scale
        ct[row0 + half:row0 + rot] = cos * scale
        st[row0:row0 + half] = sin * scale
        st[row0 + half:row0 + rot] = sin * scale
    return ct, st


def _perm_lhsT(rows, blocks):
    P = np.zeros((rows, rows), np.float32)
    for row0, rot in blocks:
        half = rot // 2
        for j in range(half):
            P[row0 + j + half, row0 + j] = -1.0
            P[row0 + j, row0 + j + half] = 1.0
    return P


def _pk(v, k):
    return np.ascontiguousarray(np.asarray(v, np.float32).reshape(k, 128).T)


def _wl(w, k):
    w = np.asarray(w, np.float32)
    return np.ascontiguousarray(w.reshape(k, 128, w.shape[1]).transpose(1, 0, 2))


def build_program(stages=99):
    from contextlib import ExitStack
    nc = bass.Bass("TRN2", target_bir_lowering=False)

    def din(name, shape):
        return nc.dram_tensor(name, list(shape), F32, kind="ExternalInput").ap()

    def dout(name, shape):
        return nc.dram_tensor(name, list(shape), F32, kind="ExternalOutput").ap()

    xw = din("xw", [8192, D]); xs_in = din("xs_in", [64, D])
    w_in = din("w_in", [128, 8, 3232]); g_pre = din("g_pre", [128, 8])
    w_uq = din("w_uq", [128, 3, 768]); g_q = din("g_q", [128, 3])
    w_ukv = din("w_ukv", [128, 2, 1024]); g_kv = din("g_kv", [128, 2])
    w_out0 = din("w_out0", [128, 8, D]); gb_post0 = din("gb_post0", [128, D])
    c_w_in = din("c_w_in", [128, 8, 2304]); g_pre1 = din("g_pre1", [128, 8])
    c_w_out = din("c_w_out", [128, 8, D]); gb_post1 = din("gb_post1", [128, D])
    cosk = din("cosk", [96, 8192]); sink = din("sink", [96, 8192])
    cosq = din("cosq", [96, 2560]); sinq = din("sinq", [96, 2560])
    cosk_s = din("cosk_s", [96, 64]); sink_s = din("sink_s", [96, 64])
    cosq_s = din("cosq_s", [96, 64]); sinq_s = din("sinq_s", [96, 64])
    cos1q = din("cos1q", [128, NQC]); sin1q = din("sin1q", [128, NQC])
    cos1k = din("cos1k", [128, NQC]); sin1k = din("sin1k", [128, NQC])
    cos1q_s = din("cos1q_s", [128, 64]); sin1q_s = din("sin1q_s", [128, 64])
    cos1k_s = din("cos1k_s", [128, 64]); sin1k_s = din("sin1k_s", [128, 64])
    pm96_d = din("pm96", [96, 96]); pm128_d = din("pm128", [128, 128]); ident_d = din("ident", [128, 128])
    valid_d = din("valid", [128, 64])
    bd_d = din("bd", [128, 2, 8, 128]); cb_d = din("cb", [128, 8])
    sinks_d = din("sinks", [128, 16])
    cache_ckv = din("cache_ckv", [PAST, 256]); cache_kr = din("cache_kr", [PAST, 32])
    cache_bk = din("cache_bk", [512, 512]); cache_bv = din("cache_bv", [512, 512])
    cache_ck = din("cache_ck", [128, 128]); cache_cv = din("cache_cv", [128, 128])

    y_p = dout("y_p", [2048, D]); y_s = dout("y_s", [64, D])
    ckv_p = dout("ckv_p", [2048, 256]); kr_p = dout("kr_p", [2048, 32])
    bk_p = dout("bk_p", [512, 512]); bv_p = dout("bv_p", [512, 512])
    ck_p = dout("ck_p", [128, 128]); cv_p = dout("cv_p", [128, 128])
    ckv_s = dout("ckv_s", [64, 256]); kr_s = dout("kr_s", [64, 32])
    bk_s = dout("bk_s", [512, 512]); bv_s = dout("bv_s", [512, 512])
    ck_s = dout("ck_s", [128, 128]); cv_s = dout("cv_s", [128, 128])

    h1d = nc.dram_tensor("h1_scratch", [NQC + 64, D], F32, kind="Internal").ap()
    top = ExitStack()
    sems = {e: top.enter_context(nc.semaphore("sem_" + e)) for e in Rec.ENGS}
    dsems = {q: [top.enter_context(nc.semaphore(f"dsem_{q}{i}")) for i in range(8)] for q in ("sp", "pool")}
    rec = Rec(nc, sems, dsems)
    E = rec.eng

    uid = {"n": 0}

    def sb(es, name, shape, dt):
        uid["n"] += 1
        return es.enter_context(nc.sbuf_tensor(f"s{uid['n']}_{name}", list(shape), dt))

    def psum(es, name, shape, dt):
        uid["n"] += 1
        return es.enter_context(nc.psum_tensor(f"p{uid['n']}_{name}", list(shape), dt))

    def fsz(ap):
        n = 1
        for d_ in list(ap.shape)[1:]:
            n *= int(d_)
        return n

    def ecost(eng, ap):
        f = fsz(ap)
        if eng == "act":
            return f / 1.2 + 200.0
        if eng == "dve":
            return f / 0.96 + 120.0
        return f / 0.45 + 150.0

    def mm(out, lhsT, rhs, start, stop, r, w, sgc=False):
        f32 = rhs.dtype == F32
        rec.op("pe", lambda: nc.tensor.matmul(out, lhsT=lhsT, rhs=rhs, start=start, stop=stop, skip_group_check=sgc), r, w,
               cost=(max(64, fsz(rhs)) / 2.4) * (4 if f32 else 1) + 70.0)

    def tr(out, in_, ident, r, w):
        rec.op("pe", lambda: nc.tensor.transpose(out, in_, ident), r, w, cost=120.0)

    def act(out, in_, func, r, w, **kw):
        rec.op("act", lambda: nc.scalar.activation(out=out, in_=in_, func=func, **kw), r, w, cost=ecost("act", out))

    def cp(eng, out, in_, r, w):
        if eng == "act":
            rec.op("act", lambda: nc.scalar.copy(out=out, in_=in_), r, w, cost=ecost("act", out))
        else:
            rec.op(eng, lambda: E[eng].tensor_copy(out=out, in_=in_), r, w, cost=ecost(eng, out))

    def tt(eng, out, in0, in1, op, r, w):
        rec.op(eng, lambda: E[eng].tensor_tensor(out=out, in0=in0, in1=in1, op=op), r, w, cost=ecost(eng, out))

    def ts(eng, out, in0, s1, op0, r, w):
        rec.op(eng, lambda: E[eng].tensor_scalar(out=out, in0=in0, scalar1=s1, scalar2=None, op0=op0), r, w, cost=ecost(eng, out))

    def stt(eng, out, in0, scalar, in1, op0, op1, r, w):
        rec.op(eng, lambda: E[eng].scalar_tensor_tensor(out=out, in0=in0, scalar=scalar, in1=in1, op0=op0, op1=op1), r, w, cost=ecost(eng, out))

    def recip(out, in_, r, w):
        rec.op("dve", lambda: nc.vector.reciprocal(out=out, in_=in_), r, w, cost=5 * ecost("dve", out))

    def memset(eng, ap, val, w):
        rec.op(eng, lambda: E[eng].memset(ap, val), [], w, cost=ecost(eng, ap))

    def dma(q, out, in_, r, w):
        nb = 1
        for d_ in list(out.shape):
            nb *= int(d_)
        rec.dma(q, lambda: E[q].dma_start(out=out, in_=in_), r, w, cost=2500.0 + nb * 4 / 120.0)

    alt = {"i": 0}

    def alt_eng(choices=("dve", "act")):
        alt["i"] += 1
        return choices[alt["i"] % len(choices)]

    identb = sb(top, "identb", [128, 128], BF16); identf = sb(top, "identf", [128, 128], F32)
    onesb = sb(top, "onesb", [128, 128], BF16); onesf = sb(top, "onesf", [128, 128], F32)
    pm96b = sb(top, "pm96b", [128, 96], BF16); pm128b = sb(top, "pm128b", [128, 128], BF16)
    valid = sb(top, "valid", [128, 64], F32); vone = sb(top, "vone", [128, 64], F32)
    QT_s = sb(top, "QT_s", [128, 8, 64], BF16)
    sga_s = sb(top, "sga_s", [128, 4, 64], BF16)
    mixb_s = sb(top, "mixb_s", [128, 4, 64], BF16)
    cs_new = sb(top, "cs_new", [128, 2, 64], BF16)
    krs_new = sb(top, "krs_new", [128, 64], BF16)
    CONST = Buf("const")
    bQT, bsga, bmixb = Buf("QT"), Buf("sga"), Buf("mixb")
    bQTs, bsgas, bmixbs, bcsn, bkrsn = Buf("QTs"), Buf("sgas"), Buf("mixbs"), Buf("csn"), Buf("krsn")

    with ExitStack() as es:
        stg = sb(es, "cstg", [128, 3, 128], F32)
        dma("sp", identf[:], ident_d[:, :], [], [CONST])
        dma("sp", stg[0:96, 0, 0:96], pm96_d[:, :], [], [CONST])
        dma("sp", stg[:, 1, :], pm128_d[:, :], [], [CONST])
        dma("sp", valid[:], valid_d[:, :], [], [CONST])
        cp("dve", identb[:], identf[:], [CONST], [CONST])
        cp("dve", pm96b[0:96, :], stg[0:96, 0, 0:96], [CONST], [CONST])
        cp("dve", pm128b[:], stg[:, 1, :], [CONST], [CONST])
        memset("dve", onesb[:], 1.0, [CONST]); memset("dve", onesf[:], 1.0, [CONST]); memset("dve", vone[:], 1.0, [CONST])
        rec.flush()

    def prep_w(es_ring, dst, src, K, C, gain, wbuf, c_lo=0, c_hi=None):
        c_hi = C if c_hi is None else c_hi
        CP = 1664
        for k in range(K):
            c = c_lo
            while c < c_hi:
                n = min(CP, c_hi - c)
                st_, sbf = es_ring.next()
                dma("sp" if alt["i"] % 4 < 2 else "pool", st_[:, :n], src[:, k, c:c + n], [], [sbf])
                e = alt_eng(("dve", "act"))
                o = dst[:, k, c - c_lo:c - c_lo + n]
                if gain is None:
                    cp(e, o, st_[:, :n], [sbf], [wbuf])
                elif e == "act":
                    act(o, st_[:, :n], AF.Copy, [sbf, CONST], [wbuf], scale=gain[:, k:k + 1])
                else:
                    ts(e, o, st_[:, :n], gain[:, k:k + 1], ALU.mult, [sbf, CONST], [wbuf])
                c += n

    def frontend(F, tiles, xnT, xnb):
        for ti, (src, rows, srcb) in enumerate(tiles):
            if srcb is None:
                xt, xb = F["xring"].next()
                dma("sp", xt[:rows, :], src, [], [xb])
            else:
                xt, xb = src, srcb
            st_, stb = F["string"].next()
            xs, xsb = F["xsring"].next()
            if F.get("junk") is None:
                act(xs[:rows, :], xt[:rows, :], AF.Square, [xb], [xsb, stb], accum_out=st_[:rows, 0:1])
            else:
                act(F["junk"][:rows, :], xt[:rows, :], AF.Square, [xb], [F["junkb"], stb], accum_out=st_[:rows, 0:1])
            act(st_[:rows, 1:2], st_[:rows, 0:1], AF.Ln, [stb], [stb], scale=1.0 / D, bias=EPS)
            act(st_[:rows, 2:3], st_[:rows, 1:2], AF.Exp, [stb], [stb], scale=-0.5)
            ts("dve", xs[:rows, :], xt[:rows, :], st_[:rows, 2:3], ALU.mult, [xb, stb], [xsb])
            pT, pTb = F["pT"].next()
            for k in range(8):
                tr(pT[:, k, :rows], xs[:rows, k * 128:(k + 1) * 128], identb[:rows, :rows], [xsb, CONST], [pTb])
            cp(alt_eng(("dve", "act")), xnT[:, :, ti * 128:ti * 128 + rows], pT[:, :, :rows], [pTb], [xnb])

    def proj(ps, W, Wb, c0, ncols, xnT, xnb, t0, n, psb):
        for k in range(8):
            mm(ps[0:ncols, 0:n], W[:, k, c0:c0 + ncols], xnT[:, k, t0:t0 + n], k == 0, k == 7, [Wb, xnb], [psb])

    def rms_fm(F, chunks, n, nfeat):
        sq, sqb = F["sqring"].next()
        for c, (ps, psb) in enumerate(chunks):
            act(sq[:, c, :n], ps[:, :n], AF.Square, [psb], [sqb])
        pss, pssb = F["psring"].next()
        for c in range(len(chunks)):
            mm(pss[:, :n], onesb[:, :], sq[:, c, :n], c == 0, c == len(chunks) - 1, [sqb, CONST], [pssb])
        rs, rsb = F["rsring"].next()
        act(rs[:, :n], pss[:, :n], AF.Ln, [pssb], [rsb], scale=1.0 / nfeat, bias=EPS)
        act(rs[:, :n], rs[:, :n], AF.Exp, [rsb], [rsb], scale=-0.5)
        return rs, rsb

    def rope_fm(F, ps, psb, R, p0, p1, n, pm, cosT, sinT, tb, out_bf, outb, out_f32=None, outfb=None):
        qr, qrb = F["qrring"].next()
        cp("act", qr[0:R, :n], ps[0:R, :n], [psb], [qrb])
        ps2, ps2b = F["psring"].next()
        mm(ps2[0:R, :n], pm[0:R, 0:R], qr[0:R, :n], True, True, [qrb, CONST], [ps2b])
        t1, t1b = F["t1ring"].next()
        t2, t2b = F["t2ring"].next()
        tt("dve", t1[p0:p1, :n], ps[p0:p1, :n], cosT, ALU.mult, [psb, tb], [t1b])
        tt("dve", t2[p0:p1, :n], ps2[p0:p1, :n], sinT, ALU.mult, [ps2b, tb], [t2b])
        if out_f32 is not None:
            tt("pool", out_f32, t1[p0:p1, :n], t2[p0:p1, :n], ALU.add, [t1b, t2b], [outfb])
            cp("pool", out_bf, out_f32, [outfb], [outb])
        else:
            tt("pool", out_bf, t1[p0:p1, :n], t2[p0:p1, :n], ALU.add, [t1b, t2b], [outb])

    def silu_fm(F, ps, psb, n, dst, dstb):
        e1, e1b = F["t1ring"].next()
        act(e1[:, :n], ps[:, :n], AF.Exp, [psb], [e1b], scale=-1.0)
        act(e1[:, :n], e1[:, :n], AF.Ln, [e1b], [e1b], bias=1.0)
        act(e1[:, :n], e1[:, :n], AF.Exp, [e1b], [e1b], scale=-1.0)
        tt("dve", dst, ps[:, :n], e1[:, :n], ALU.mult, [psb, e1b], [dstb])

    def norm_out(F, po_views, pob, nq, sg_fn, out_fn, extra_l=None):
        for hf, (po, pb_) in enumerate(zip(po_views, pob)):
            for par in range(2):
                p0 = 64 * par
                Rs, Rsb_ = F["Rring"].next()
                Rv = Rs[p0:p0 + 64, :].rearrange("p (a q) -> p a q", a=2)[:, :, :nq]
                act(Rv, po[64:128, :, par, :nq], AF.Ln, [pb_], [Rsb_], bias=1e-30)
                act(Rv, Rv, AF.Exp, [Rsb_], [Rsb_], scale=-1.0)
                sg, sgb_ = sg_fn(hf, par)
                o, ob = out_fn(hf, par)
                tm, tmb = F["tmring"].next()
                tv = tm[p0:p0 + 64, :].rearrange("p (a q) -> p a q", a=2)[:, :, :nq]
                tt("dve", tv, Rv, sg, ALU.mult, [Rsb_, sgb_], [tmb])
                tt("dve", o, po[0:64, :, par, :nq], tv, ALU.mult, [pb_, tmb], [ob])

    def stage_b2():
        with ExitStack() as es:
            Wp = sb(es, "Wb2", [128, 8, 2048], BF16); Wb = Buf("Wb2")
            gp = sb(es, "gp", [128, 8], F32)
            dma("sp", gp[:], g_pre[:, :], [], [CONST])
            bdb = sb(es, "bdb", [128, 2, 8, 128], BF16); bbd = Buf("bd")
            with ExitStack() as es2:
                ring = Ring([sb(es2, f"stgw{i}", [128, 1664], F32) for i in range(4)], "stgw")
                prep_w(ring, Wp, w_in, 8, 3232, gp, Wb, c_lo=C_QB, c_hi=3232)
                bdf = sb(es2, "bdf", [128, 2, 8, 128], F32); cbt = sb(es2, "cbt", [128, 8], F32)
                dma("sp", bdf[:], bd_d[:, :, :, :], [], [bbd]); dma("sp", cbt[:], cb_d[:, :], [], [bbd])
                rec.op("dve", lambda: nc.vector.tensor_scalar(out=cbt[:], in0=cbt[:], scalar1=-1.0, scalar2=None, op0=ALU.mult), [bbd], [bbd])
                for d_ in range(2):
                    for h in range(8):
                        act(bdb[:, d_, h, :], bdf[:, d_, h, :], AF.Exp, [bbd], [bbd], bias=cbt[:, h:h + 1])
                rec.flush()
            O_QB, O_KB, O_VB, O_GB = 0, 512, 1024, 1536
            F = {}
            F["xring"] = Ring([sb(es, f"xr{i}", [128, D], F32) for i in range(3)], "xr")
            F["xsring"] = Ring([sb(es, f"xs{i}", [128, D], BF16) for i in range(2)], "xs")
            F["junk"] = sb(es, "junk", [128, D], BF16); F["junkb"] = Buf("junk")
            F["string"] = Ring([sb(es, f"st{i}", [128, 4], F32) for i in range(4)], "st")
            F["pT"] = Ring([psum(es, "pT0", [128, 8, 128], BF16)], "pT", psum=True)
            ps2 = [psum(es, f"ps2_{i}", [128, 1024], F32) for i in range(2)]
            ps1 = [psum(es, f"ps1_{i}", [128, 512], F32) for i in range(3)]
            views = [ps2[0][:, 0:512], ps2[0][:, 512:1024], ps2[1][:, 0:512], ps2[1][:, 512:1024]] + [p[:, :] for p in ps1]
            F["psring"] = Ring(views, "psv", psum=True)
            vb_ = F["psring"].bufs
            F["prring"] = Ring([ps1[2][:, :]], "pr", [vb_[6]])
            F["t1ring"] = Ring([sb(es, f"t1_{i}", [128, 512], F32) for i in range(2)], "t1")
            F["Rring"] = Ring([sb(es, f"Rs{i}", [128, 512], F32) for i in range(2)], "Rs")
            F["tmring"] = Ring([sb(es, f"tm{i}", [128, 256], F32) for i in range(2)], "tm")
            xnr = Ring([sb(es, f"xnT{i}", [128, 8, 512], BF16) for i in range(2)], "xnT")
            sgbq = sb(es, "sgbq", [128, 4, 512], BF16); bsgbq = Buf("sgbq")
            qbq = sb(es, "qbq", [128, 4, 512], BF16); bqbq = Buf("qbq")
            kbT = sb(es, "kbT", [128, 4, 12 * 128], BF16)
            vb = sb(es, "vb", [128, 12, 8, 128], BF16)
            bslot = [Buf(f"kvslot{i}") for i in range(12)]
            ptr = Ring([sb(es, f"pTs{i}", [128, 640], BF16) for i in range(4)], "pTs")
            ostg = Ring([sb(es, f"ostg{i}", [128, 512], F32) for i in range(2)], "ostg")
            kbT_s = sb(es, "kbT_s", [128, 4, 640], BF16); vb_s = sb(es, "vb_s", [128, 5, 8, 128], BF16); bkvs = Buf("kvs")

            def b_attention(nq, qT, qTb, qc0, keyT, vaug, kbufs, nks, sg_fn, out_fn):
                po = [ps1[0], ps1[1]]
                pob = [vb_[4], vb_[5]]
                for hp in range(4):
                    for j in range(5):
                        for h in (2 * hp, 2 * hp + 1):
                            c, pb = h // 2, (h % 2) * 64
                            psS = ps2[h % 2]; psSb = [vb_[2 * (h % 2)], vb_[2 * (h % 2) + 1]]
                            nk = nks[j]
                            o = psS[0:nk, j * 128:j * 128 + nq]
                            wb_ = [psSb[0] if j < 4 else psSb[1]]
                            mm(o, keyT(j, c, pb), qT[pb:pb + 64, c, qc0:qc0 + nq], True, True, [kbufs[j], qTb], wb_)
                    for h in (2 * hp, 2 * hp + 1):
                        psS = ps2[h % 2]; psSb = [vb_[2 * (h % 2)], vb_[2 * (h % 2) + 1]]
                        pt, ptb = ptr.next()
                        act(pt[:, 0:512].rearrange("p (j q) -> p j q", j=4)[:, :, 0:nq], psS[:, 0:512].rearrange("p (j q) -> p j q", j=4)[:, :, 0:nq], AF.Exp, [psSb[0]], [ptb])
                        act(pt[0:nks[4], 512:512 + nq], psS[0:nks[4], 512:512 + nq], AF.Exp, [psSb[1]], [ptb])
                        ov = po[h // 4][0:128, :].rearrange("p (a b q) -> p a b q", a=2, b=2)[:, (h % 4) // 2, h % 2, :]
                        ob = [pob[h // 4]]
                        tt("pool", pt[:, 384:384 + nq], pt[:, 384:384 + nq], bdb[:, 1, h, 0:nq], ALU.mult, [ptb, bbd], [ptb])
                        tt("pool", pt[0:nks[4], 512:512 + nq], pt[0:nks[4], 512:512 + nq], bdb[0:nks[4], 0, h, 0:nq], ALU.mult, [ptb, bbd], [ptb])
                        if nq > 64:
                            memset("pool", pt[0:64, 64:128], 0.0, [ptb])
                            memset("pool", pt[64:128, 512:576], 0.0, [ptb])
                        for j in range(5):
                            mm(ov[:, 0:nq], vaug(j, h, 0, nks[j]), pt[0:nks[j], j * 128:j * 128 + nq], j == 0, j == 4, [kbufs[j], ptb], ob)
                pov = [p[:, :].rearrange("p (a b q) -> p a b q", a=2, b=2) for p in po]
                norm_out(F, pov, pob, nq, sg_fn, out_fn)

            def quad(tiles, wt0, full_from, is_sample):
                N = sum(r for _, r, _ in tiles)
                xnT, xnb = xnr.next()
                frontend(F, tiles, xnT, xnb)
                for c in range(4):
                    ps, psb = F["psring"].next()
                    proj(ps, Wp, Wb, O_KB + c * 128, 128, xnT, xnb, 0, N, psb)
                    if is_sample:
                        cp(alt_eng(), kbT_s[:, c, 512:512 + N], ps[:, :N], [psb], [bkvs])
                    else:
                        s0 = (wt0 - 40) % 12
                        cp(alt_eng(), kbT[:, c, s0 * 128:s0 * 128 + N], ps[:, :N], [psb], bslot[s0:s0 + 4])
                want_out = is_sample or wt0 == 60
                for ti, (_, rows, _) in enumerate(tiles):
                    ps, psb = F["psring"].next()
                    for k in range(8):
                        mm(ps[0:rows, :], xnT[:, k, ti * 128:ti * 128 + rows], Wp[:, k, O_VB:O_VB + 512], k == 0, k == 7, [xnb, Wb], [psb])
                    pv = ps[0:rows, :].rearrange("p (h e) -> p h e", h=8)
                    if is_sample:
                        cp("dve", vb_s[0:rows, 4, :, 0:64], pv, [psb], [bkvs])
                    else:
                        s = (wt0 + ti - 40) % 12
                        rec.op("dve", lambda s=s, pv=pv, t=wt0 + ti: nc.vector.tensor_scalar(out=vb[:, s, :, 0:64], in0=pv, scalar1=valid[:, t:t + 1], scalar2=None, op0=ALU.mult), [psb, CONST], [bslot[s]])
                        cp("act", vb[:, s, :, 64:128], valid[:, wt0 + ti:wt0 + ti + 1].unsqueeze(2).to_broadcast([128, 8, 64]), [CONST], [bslot[s]])
                    if want_out:
                        og, ogb = ostg.next()
                        cp("act", og[0:rows, :], ps[0:rows, :], [psb], [ogb])
                        dst = bv_s[448:512, :] if is_sample else bv_p[ti * 128:ti * 128 + 128, :]
                        dma("pool", dst, og[0:rows, :], [ogb], [])
                        ps, psb = F["psring"].next()
                        for k in range(8):
                            mm(ps[0:rows, :], xnT[:, k, ti * 128:ti * 128 + rows], Wp[:, k, O_KB:O_KB + 512], k == 0, k == 7, [xnb, Wb], [psb])
                        og, ogb = ostg.next()
                        cp("act", og[0:rows, :], ps[0:rows, :], [psb], [ogb])
                        dst = bk_s[448:512, :] if is_sample else bk_p[ti * 128:ti * 128 + 128, :]
                        dma("pool", dst, og[0:rows, :], [ogb], [])
                if full_from is None:
                    return
                f0 = full_from
                n = N - f0
                for c in range(4):
                    ps, psb = F["psring"].next()
                    proj(ps, Wp, Wb, O_QB + c * 128, 128, xnT, xnb, f0, n, psb)
                    act(qbq[:, c, 0:n], ps[:, :n], AF.Copy, [psb], [bqbq], scale=B_SCALE)
                    ps, psb = F["psring"].next()
                    proj(ps, Wp, Wb, O_GB + c * 128, 128, xnT, xnb, f0, n, psb)
                    silu_fm(F, ps, psb, n, sgbq[:, c, 0:n], bsgbq)
                if is_sample:
                    def keyT(j, c, pb):
                        return kbT_s[pb:pb + 64, c, j * 128:j * 128 + (128 if j < 4 else 64)]

                    def vaug(j, h, k0, k1):
                        return vb_s[k0:k1, j, h, :]
                    b_attention(64, qbq, bqbq, 0, keyT, vaug, [bkvs] * 5, [128, 128, 128, 128, 64],
                                lambda hf, par: (sgbq[64 * par:64 * par + 64, 2 * hf:2 * hf + 2, 0:64], bsgbq),
                                lambda hf, par: (mixb_s[64 * par:64 * par + 64, 2 * hf:2 * hf + 2, 0:64], bmixbs))
                else:
                    for ti in range(f0 // 128, len(tiles)):
                        t = wt0 + ti
                        qc = ti * 128 - f0
                        oc = (t - T0) * 128
                        slots = [(t - 4 + j - 40) % 12 for j in range(5)]

                        def keyT(j, c, pb, slots=slots):
                            return kbT[pb:pb + 64, c, slots[j] * 128:slots[j] * 128 + 128]

                        def vaug(j, h, k0, k1, slots=slots):
                            return vb[k0:k1, slots[j], h, :]
                        b_attention(128, qbq, bqbq, qc, keyT, vaug, [bslot[s] for s in slots], [128] * 5,
                                    lambda hf, par, qc=qc: (sgbq[64 * par:64 * par + 64, 2 * hf:2 * hf + 2, qc:qc + 128], bsgbq),
                                    lambda hf, par, oc=oc: (mixb[64 * par:64 * par + 64, 2 * hf:2 * hf + 2, oc:oc + 128], bmixb))

            with ExitStack() as es2:
                cst = Ring([sb(es2, f"cst{i}", [128, 512], F32) for i in range(2)], "cst")
                cbf = Ring([sb(es2, f"cbf{i}", [128, 512], BF16) for i in range(2)], "cbf")
                for j in range(4):
                    st_, stb = cst.next()
                    dma("sp", st_[:], cache_bk[j * 128:(j + 1) * 128, :], [], [stb])
                    bf, bfb = cbf.next()
                    cp("dve", bf[:], st_[:], [stb], [bfb])
                    pT, pTb = F["pT"].next()
                    for c in range(4):
                        tr(pT[:, c, :], bf[:, c * 128:(c + 1) * 128], identb[:, :], [bfb, CONST], [pTb])
                    cp("act", kbT_s[:, :, j * 128:(j + 1) * 128], pT[:, 0:4, :], [pTb], [bkvs])
                    st_, stb = cst.next()
                    dma("sp", st_[:], cache_bv[j * 128:(j + 1) * 128, :], [], [stb])
                    cp("dve", vb_s[:, j, :, 0:64], st_[:].rearrange("p (h e) -> p h e", h=8), [stb], [bkvs])
                memset("pool", vb_s[:, :, :, 64:128], 1.0, [bkvs])
                dma("pool", bk_s[0:448, :], cache_bk[64:512, :], [], [])
                dma("pool", bv_s[0:448, :], cache_bv[64:512, :], [], [])
                quad([(xs_in[:, :], 64, None)], None, 0, True)
                rec.flush()
            for q in range(10, 16):
                wt0 = q * 4
                tiles = [(xw[(wt0 + i) * 128:(wt0 + i + 1) * 128, :], 128, None) for i in range(4)]
                full_from = None if q == 10 else (384 if q == 11 else 0)
                quad(tiles, wt0, full_from, False)
            rec.flush()

    def stage_b1():
        with ExitStack() as es:
            Wp = sb(es, "Wb1", [128, 8, 896], BF16); Wb = Buf("Wb1")
            wuq = sb(es, "wuq", [128, 3, 768], BF16); wuqb = Buf("wuq")
            gp = sb(es, "gp1", [128, 8], F32); gq = sb(es, "gq1", [128, 3], F32)
            dma("sp", gp[:], g_pre[:, :], [], [CONST]); dma("sp", gq[:], g_q[:, :], [], [CONST])
            with ExitStack() as es2:
                ring = Ring([sb(es2, f"stgw{i}", [128, 1664], F32) for i in range(4)], "stgw")
                prep_w(ring, Wp[:, :, 0:384], w_in, 8, 3232, gp, Wb, c_lo=0, c_hi=384)
                prep_w(ring, Wp[:, :, 384:896], w_in, 8, 3232, gp, Wb, c_lo=C_GA, c_hi=C_GA + 512)
                prep_w(ring, wuq, w_uq, 3, 768, gq, wuqb)
                rec.flush()
            F = {}
            F["xring"] = Ring([sb(es, f"xr{i}", [128, D], F32) for i in range(4)], "xr")
            F["xsring"] = Ring([sb(es, f"xs{i}", [128, D], BF16) for i in range(3)], "xs")
            F["junk"] = sb(es, "junk", [128, D], BF16); F["junkb"] = Buf("junk")
            F["string"] = Ring([sb(es, f"st{i}", [128, 4], F32) for i in range(4)], "st")
            F["pT"] = Ring([psum(es, f"pT{i}", [128, 8, 128], BF16) for i in range(2)], "pT", psum=True)
            F["psring"] = Ring([psum(es, f"psb1_{i}", [128, 512], F32) for i in range(6)], "psv", psum=True)
            F["t1ring"] = Ring([sb(es, f"t1_{i}", [128, 512], F32) for i in range(3)], "t1")
            F["t2ring"] = Ring([sb(es, f"t2_{i}", [128, 512], F32) for i in range(2)], "t2")
            F["sqring"] = Ring([sb(es, f"sq{i}", [128, 3, 512], BF16) for i in range(2)], "sq")
            F["rsring"] = Ring([sb(es, f"rs{i}", [128, 512], F32) for i in range(2)], "rs")
            F["qrring"] = Ring([sb(es, f"qr{i}", [128, 512], BF16) for i in range(2)], "qr")
            xnr = Ring([sb(es, f"xnT{i}", [128, 8, 512], BF16) for i in range(2)], "xnT")
            qln = sb(es, "qln", [128, 3, 512], BF16); qlnb = Buf("qln")
            tab = Ring([sb(es, f"tab{i}", [128, 2, 512], F32) for i in range(2)], "tab")

            def quad(tiles, f0, cs_ap, sn_ap, QTd, QTb_, qc, sgd, sgb_):
                N = sum(r for _, r, _ in tiles)
                n = N - f0
                xnT, xnb = xnr.next()
                frontend(F, tiles, xnT, xnb)
                tb_, tbb = tab.next()
                dma("pool", tb_[0:96, 0, 0:n], cs_ap, [], [tbb]); dma("pool", tb_[0:96, 1, 0:n], sn_ap, [], [tbb])
                chunks = []
                for c in range(3):
                    ps, psb = F["psring"].next()
                    proj(ps, Wp, Wb, c * 128, 128, xnT, xnb, f0, n, psb)
                    chunks.append((ps, psb))
                rs, rsb = rms_fm(F, chunks, n, 384)
                for c, (ps, psb) in enumerate(chunks):
                    tt("dve", qln[:, c, 0:n], ps[:, :n], rs[:, :n], ALU.mult, [psb, rsb], [qlnb])
                for h in range(8):
                    ps, psb = F["psring"].next()
                    for c in range(3):
                        mm(ps[0:96, 0:n], wuq[:, c, h * 96:(h + 1) * 96], qln[:, c, 0:n], c == 0, c == 2, [wuqb, qlnb], [psb])
                    rope_fm(F, ps, psb, 96, 0, 96, n, pm96b, tb_[0:96, 0, 0:n], tb_[0:96, 1, 0:n], tbb,
                            QTd[0:96, h, qc:qc + n], QTb_)
                for c in range(4):
                    ps, psb = F["psring"].next()
                    proj(ps, Wp, Wb, 384 + c * 128, 128, xnT, xnb, f0, n, psb)
                    silu_fm(F, ps, psb, n, sgd[:, c, qc:qc + n], sgb_)

            quad([(xs_in[:, :], 64, None)], 0, cosq_s[:, :], sinq_s[:, :], QT_s, bQTs, 0, sga_s, bsgas)
            for q in range(11, 16):
                wt0 = q * 4
                tiles = [(xw[(wt0 + i) * 128:(wt0 + i + 1) * 128, :], 128, None) for i in range(4)]
                f0 = 384 if q == 11 else 0
                tc0 = (wt0 - 44) * 128 + f0
                qc = (wt0 - T0) * 128 + f0
                quad(tiles, f0, cosq[:, tc0:tc0 + 512 - f0], sinq[:, tc0:tc0 + 512 - f0], QT, bQT, qc, sga, bsga)
            rec.flush()

    def stage_am():
        with ExitStack() as es:
            cT = sb(es, "cT", [128, 2, 8192], BF16)
            KT = sb(es, "KT", [128, 8192], BF16)
            bcT = [Buf(f"cT{i}") for i in range(16)]
            bKr = [Buf(f"KTr{i}") for i in range(16)]
            bKn = [Buf(f"KTn{i}") for i in range(16)]
            wukv = sb(es, "wukv", [128, 2, 1024], BF16); wukvb = Buf("wukv")
            gkv = sb(es, "gkv", [128, 2], F32)
            dma("sp", gkv[:], g_kv[:, :], [], [CONST])
            with ExitStack() as esA:
                Wl = sb(esA, "Wl", [128, 8, 256], BF16); Wlb = Buf("Wl")
                Wk = sb(esA, "Wk", [128, 8, 96], BF16); Wkb = Buf("Wk")
                gp = sb(esA, "gpA", [128, 8], F32)
                dma("sp", gp[:], g_pre[:, :], [], [CONST])
                memset("pool", Wk[:], 0.0, [Wkb])
                with ExitStack() as es2:
                    ring = Ring([sb(es2, f"stgw{i}", [128, 1664], F32) for i in range(4)], "stgw")
                    prep_w(ring, Wl, w_in, 8, 3232, gp, Wlb, c_lo=C_CKV, c_hi=C_CKV + 256)
                    prep_w(ring, Wk[:, :, 64:96], w_in, 8, 3232, gp, Wkb, c_lo=C_KR, c_hi=C_KR + 32)
                    prep_w(ring, wukv, w_ukv, 2, 1024, None, wukvb)
                    rec.flush()
                F = {}
                F["xring"] = Ring([sb(esA, f"xr{i}", [128, D], F32) for i in range(3)], "xr")
                F["xsring"] = Ring([sb(esA, f"xs{i}", [128, D], BF16) for i in range(3)], "xs")
                F["junk"] = sb(esA, "junk", [128, D], BF16); F["junkb"] = Buf("junk")
                F["string"] = Ring([sb(esA, f"st{i}", [128, 4], F32) for i in range(4)], "st")
                F["pT"] = Ring([psum(esA, f"pT{i}", [128, 8, 128], BF16) for i in range(2)], "pT", psum=True)
                F["psring"] = Ring([psum(esA, f"psa_{i}", [128, 512], F32) for i in range(6)], "psv", psum=True)
                F["t1ring"] = Ring([sb(esA, f"t1_{i}", [128, 512], F32) for i in range(2)], "t1")
                F["t2ring"] = Ring([sb(esA, f"t2_{i}", [128, 512], F32) for i in range(2)], "t2")
                F["sqring"] = Ring([sb(esA, f"sq{i}", [128, 2, 512], BF16) for i in range(2)], "sq")
                F["rsring"] = Ring([sb(esA, f"rs{i}", [128, 512], F32) for i in range(2)], "rs")
                F["qrring"] = Ring([sb(esA, f"qr{i}", [128, 512], BF16) for i in range(2)], "qr")
                xnr = Ring([sb(esA, f"xnT{i}", [128, 8, 512], BF16) for i in range(2)], "xnT")
                tab = Ring([sb(esA, f"tab{i}", [128, 2, 512], F32) for i in range(2)], "tab")
                cf = sb(esA, "cf", [128, 2, 512], F32); cfb = Buf("cf")
                kf = sb(esA, "kf", [128, 512], F32); kfb = Buf("kf")
                ostg = Ring([sb(esA, f"ostgA{i}", [128, 288], F32) for i in range(2)], "ostgA")

                def quadA(tiles, cs_ap, sn_ap, cdst, cb_, kdst, kb_, out_c, out_k):
                    N = sum(r for _, r, _ in tiles)
                    xnT, xnb = xnr.next()
                    frontend(F, tiles, xnT, xnb)
                    tb_, tbb = tab.next()
                    dma("pool", tb_[64:96, 0, 0:N], cs_ap, [], [tbb]); dma("pool", tb_[64:96, 1, 0:N], sn_ap, [], [tbb])
                    chunks = []
                    for c in range(2):
                        ps, psb = F["psring"].next()
                        proj(ps, Wl, Wlb, c * 128, 128, xnT, xnb, 0, N, psb)
                        chunks.append((ps, psb))
                    rs, rsb = rms_fm(F, chunks, N, 256)
                    for c, (ps, psb) in enumerate(chunks):
                        if out_c is not None:
                            stt("dve", cf[:, c, 0:N], ps[:, :N], gkv[:, c:c + 1], rs[:, :N], ALU.mult, ALU.mult, [psb, rsb, CONST], [cfb])
                            cp("pool", cdst[:, c, :], cf[:, c, 0:N], [cfb], [cb_])
                        else:
                            stt("dve", cdst[:, c, :], ps[:, :N], gkv[:, c:c + 1], rs[:, :N], ALU.mult, ALU.mult, [psb, rsb, CONST], [cb_])
                    ps, psb = F["psring"].next()
                    proj(ps, Wk, Wkb, 0, 96, xnT, xnb, 0, N, psb)
                    if out_k is not None:
                        rope_fm(F, ps, psb, 96, 64, 96, N, pm96b, tb_[64:96, 0, 0:N], tb_[64:96, 1, 0:N], tbb, kdst, kb_,
                                out_f32=kf[64:96, 0:N], outfb=kfb)
                    else:
                        rope_fm(F, ps, psb, 96, 64, 96, N, pm96b, tb_[64:96, 0, 0:N], tb_[64:96, 1, 0:N], tbb, kdst, kb_)
                    if out_c is not None:
                        for ti, (_, rows, _) in enumerate(tiles):
                            pso, psob = F["psring"].next()
                            for c in range(2):
                                mm(pso[0:rows, c * 128:(c + 1) * 128], cf[:, c, ti * 128:ti * 128 + rows], identf[:, :], True, True, [cfb, CONST], [psob])
                            mm(pso[0:rows, 256:288], kf[64:96, ti * 128:ti * 128 + rows], identf[64:96, 64:96], True, True, [kfb, CONST], [psob])
                            og, ogb = ostg.next()
                            cp("act", og[0:rows, :], pso[0:rows, 0:288], [psob], [ogb])
                            dma("pool", out_c[ti * 128:ti * 128 + rows, :], og[0:rows, 0:256], [ogb], [])
                            dma("pool", out_k[ti * 128:ti * 128 + rows, :], og[0:rows, 256:288], [ogb], [])

                quadA([(xs_in[:, :], 64, None)], cosk_s[64:96, :], sink_s[64:96, :], cs_new[:, :, 0:64], bcsn,
                      krs_new[64:96, 0:64], bkrsn, ckv_s, kr_s)
                for q in range(16):
                    tiles = [(xw[(q * 4 + i) * 128:(q * 4 + i + 1) * 128, :], 128, None) for i in range(4)]
                    own = q >= 12
                    quadA(tiles, cosk[64:96, q * 512:(q + 1) * 512], sink[64:96, q * 512:(q + 1) * 512],
                          cT[:, :, q * 512:(q + 1) * 512], bcT[q], KT[64:96, q * 512:(q + 1) * 512], bKr[q],
                          ckv_p[(q - 12) * 512:(q - 11) * 512, :] if own else None,
                          kr_p[(q - 12) * 512:(q - 11) * 512, :] if own else None)
                rec.flush()
            if stages < 4:
                return

            def mla(esM, nkb, nk_last, QTd, QTb_, nq, diag0, vld, sg_t, sgb_):
                nacc = (nq + 511) // 512
                accs = [psum(esM, f"acc{i}", [128, 512], F32) for i in range(nacc)]
                accb = [Buf(f"acc{i}", True) for i in range(nacc)]
                sring = Ring([psum(esM, f"psS{i}", [128, 512], F32) for i in range(3)], "psS", psum=True)
                misc = sring
                V4 = sb(esM, "V4", [128, 64, 2, 128], BF16)

                bV = [Buf(f"V4_{i}") for i in range(64)]
                for j_ in range(2):
                    cp("dve" if j_ == 0 else "act", V4[:, 0:nkb, j_, 64:128], vld[:, 0:nkb].unsqueeze(2).to_broadcast([128, nkb, 64]), [CONST], bV[0:nkb])
                ptr = Ring([sb(esM, f"pt{i}", [128, 512], BF16) for i in range(6)], "pt")
                Rr = Ring([sb(esM, f"Rm{i}", [128, 512], F32) for i in range(2)], "Rm")
                tmr = Ring([sb(esM, f"tmm{i}", [128, 512], F32) for i in range(2)], "tmm")
                ngroups = (nq + 511) // 512
                nsb = (nkb + 3) // 4

                def nkeys(kb):
                    return nk_last if kb == nkb - 1 else 128

                def expandK(h, s):
                    ncol = sum(nkeys(kb) for kb in range(4 * s, min(4 * s + 4, nkb)))
                    ps, psb = misc.next()
                    for c in range(2):
                        mm(ps[0:64, 0:ncol], wukv[:, c, h * 128:h * 128 + 64], cT[:, c, s * 512:s * 512 + ncol], c == 0, c == 1, [wukvb, bcT[s]], [psb])
                    cp("dve", KT[0:64, s * 512:s * 512 + ncol], ps[0:64, 0:ncol], [psb], [bKn[s]])

                def expandV(hh):
                    wv = [wukv[:, c, :].rearrange("p (h e) -> p h e", e=128)[:, 2 * hh:2 * hh + 2, 64:128] for c in range(2)]
                    for kb in range(nkb):
                        nk = nkeys(kb)
                        ps, psb = misc.next()
                        pv = ps[0:nk, 0:128].rearrange("p (h e) -> p h e", e=64)
                        for c in range(2):
                            mm(pv, cT[:, c, kb * 128:kb * 128 + nk], wv[c], c == 0, c == 1, [wukvb, bcT[kb // 4]], [psb])
                        rec.op("dve", lambda kb=kb, nk=nk, pv=pv: nc.vector.tensor_scalar(out=V4[0:nk, kb, :, 0:64], in0=pv, scalar1=vld[0:nk, kb:kb + 1], scalar2=None, op0=ALU.mult), [psb, CONST], [bV[kb]])

                def head(h):
                    hj = h % 2
                    work = []
                    for kb in range(nkb):
                        qlo = 0 if diag0 is None else 128 * max(0, kb - diag0)
                        for g in range(ngroups):
                            g0, g1 = g * 512, min(nq, g * 512 + 512)
                            a = max(g0, qlo)
                            if a < g1:
                                work.append((kb, g, a, g1, diag0 is not None and kb >= diag0 and a == qlo))
                    state = {}
                    seen = set()

                    def emitS(i):
                        kb, g, a, b, dg = work[i]
                        nk = nkeys(kb)
                        if kb % 4 == 0 and kb not in seen and kb // 4 + 1 < nsb:
                            expandK(h, kb // 4 + 1)
                        seen.add(kb)
                        ps, psb = sring.next()
                        mm(ps[0:nk, 0:b - a], KT[0:96, kb * 128:kb * 128 + nk], QTd[0:96, h, a:b], True, True, [bKn[kb // 4], bKr[kb // 4], QTb_], [psb])
                        pt, ptb = ptr.next()
                        act(pt[0:nk, 0:b - a], ps[0:nk, 0:b - a], AF.Exp, [psb], [ptb])
                        state[i] = (pt, ptb)

                    def emitPV(i):
                        kb, g, a, b, dg = work[i]
                        nk = nkeys(kb)
                        pt, ptb = state.pop(i)
                        acc = accs[g]; g0 = g * 512
                        first = kb == 0
                        last = kb == nkb - 1
                        if dg:
                            mm(acc[0:128, a - g0:a - g0 + 64], V4[0:64, kb, hj, :], pt[0:64, 0:64], False, True, [bV[kb], ptb], [accb[g]], sgc=True)
                            mm(acc[0:128, a - g0 + 64:a - g0 + 128], V4[:, kb, hj, :], pt[:, 64:128], False, True, [bV[kb], ptb], [accb[g]], sgc=True)
                            if b > a + 128:
                                mm(acc[0:128, a - g0 + 128:b - g0], V4[:, kb, hj, :], pt[:, 128:b - a], False, False, [bV[kb], ptb], [accb[g]], sgc=True)
                        else:
                            mm(acc[0:128, a - g0:b - g0], V4[0:nk, kb, hj, :], pt[0:nk, 0:b - a], first, last, [bV[kb], ptb], [accb[g]], sgc=True)

                    c, pb = h // 2, (h % 2) * 64
                    lastkb = {}
                    for (kb_, g_, a_, b_, dg_) in work:
                        lastkb[g_] = kb_

                    def norm_group(g):
                        g0, g1 = g * 512, min(nq, g * 512 + 512)
                        n = g1 - g0
                        Rs, Rsb_ = Rr.next()
                        act(Rs[pb:pb + 64, 0:n], accs[g][64:128, 0:n], AF.Ln, [accb[g]], [Rsb_], bias=1e-30)
                        act(Rs[pb:pb + 64, 0:n], Rs[pb:pb + 64, 0:n], AF.Exp, [Rsb_], [Rsb_], scale=-1.0)
                        tm, tmb = tmr.next()
                        tt("dve", tm[pb:pb + 64, 0:n], Rs[pb:pb + 64, 0:n], sg_t[pb:pb + 64, c, g0:g1], ALU.mult, [Rsb_, sgb_], [tmb])
                        tt("dve", sg_t[pb:pb + 64, c, g0:g1], accs[g][0:64, 0:n], tm[pb:pb + 64, 0:n], ALU.mult, [accb[g], tmb], [sgb_])

                    expandK(h, 0)
                    for i in range(len(work) + 1):
                        if i < len(work):
                            emitS(i)
                        if i >= 1:
                            emitPV(i - 1)
                            kb_, g_ = work[i - 1][0], work[i - 1][1]
                            if lastkb[g_] == kb_ and (i == len(work) or work[i][0] != kb_ or work[i][1] != g_):
                                norm_group(g_)

                for hh in range(4):
                    expandV(hh)
                    for h in range(2 * hh, 2 * hh + 2):
                        head(h)

            with ExitStack() as esM:
                mla(esM, 64, 128, QT, bQT, NQC, T0, valid, sga, bsga)
                rec.flush()
            with ExitStack() as esS:
                pTs = Ring([psum(esS, "pTs0", [128, 8, 128], BF16)], "pTs", psum=True)
                cst = Ring([sb(esS, f"ccst{i}", [128, 256], F32) for i in range(2)], "ccst")
                cbf = Ring([sb(esS, f"ccbf{i}", [128, 256], BF16) for i in range(2)], "ccbf")
                kst = Ring([sb(esS, f"kst{i}", [128, 32], F32) for i in range(2)], "kst")
                kbf = Ring([sb(esS, f"kbf{i}", [128, 96], BF16) for i in range(2)], "kbf")
                for r_ in kbf.tiles:
                    memset("pool", r_[:], 0.0, kbf.bufs)
                for t in range(32):
                    st_, stb = cst.next()
                    dma("sp", st_[:], cache_ckv[t * 128:(t + 1) * 128, :], [], [stb])
                    bf, bfb = cbf.next()
                    cp(alt_eng(("dve", "pool")), bf[:], st_[:], [stb], [bfb])
                    ks, ksb = kst.next()
                    dma("sp", ks[:], cache_kr[t * 128:(t + 1) * 128, :], [], [ksb])
                    kb_, kbb = kbf.next()
                    cp("pool", kb_[:, 64:96], ks[:], [ksb], [kbb])
                    pT, pTb = pTs.next()
                    for c in range(2):
                        tr(pT[:, c, :], bf[:, c * 128:(c + 1) * 128], identb[:, :], [bfb, CONST], [pTb])
                    tr(pT[0:96, 2, :], kb_[:, 0:96], identb[:, :], [kbb, CONST], [pTb])
                    cp("act", cT[:, :, t * 128:(t + 1) * 128], pT[:, 0:2, :], [pTb], [bcT[t // 4]])
                    cp("dve", KT[64:96, t * 128:(t + 1) * 128], pT[64:96, 2, :], [pTb], [bKr[t // 4]])
                cp("dve", cT[:, :, 4096:4160], cs_new[:, :, 0:64], [bcsn], [bcT[8]])
                cp("dve", KT[64:96, 4096:4160], krs_new[64:96, 0:64], [bkrsn], [bKr[8]])
                with ExitStack() as esM:
                    mla(esM, 33, 64, QT_s, bQTs, 64, None, vone, sga_s, bsgas)
                    rec.flush()

    def make_out_proj(F, psY, tmpr):
        def out_proj(lhs_fn, lhsb, W, Wb_, rows, gb, res, resb, dst, dstb):
            ps, psb = psY.next()
            for hf in range(2):
                for kk in range(8):
                    mm(ps[0:rows, hf * 512:(hf + 1) * 512], lhs_fn(kk), W[:, kk, hf * 512:(hf + 1) * 512], kk == 0, kk == 7, lhsb + [Wb_], [psb])
            st_, stb = F["string"].next()
            tm, tmb = tmpr.next()
            act(tm[:rows, 0:512], ps[0:rows, 0:512], AF.Square, [psb], [tmb, stb], accum_out=st_[:rows, 0:1])
            act(tm[:rows, 512:1024], ps[0:rows, 512:1024], AF.Square, [psb], [tmb, stb], accum_out=st_[:rows, 3:4])
            tt("dve", st_[:rows, 0:1], st_[:rows, 0:1], st_[:rows, 3:4], ALU.add, [stb], [stb])
            act(st_[:rows, 1:2], st_[:rows, 0:1], AF.Ln, [stb], [stb], scale=1.0 / D, bias=EPS)
            act(st_[:rows, 2:3], st_[:rows, 1:2], AF.Exp, [stb], [stb], scale=-0.5)
            for hf in range(2):
                stt("dve", tm[0:rows, hf * 512:(hf + 1) * 512], ps[0:rows, hf * 512:(hf + 1) * 512], st_[:rows, 2:3], gb[0:rows, hf * 512:(hf + 1) * 512], ALU.mult, ALU.mult, [psb, stb, CONST], [tmb])
            tt("dve", dst, tm[0:rows, :], res, ALU.add, [tmb, resb], [dstb])

        return out_proj

    def stage_c0():
        with ExitStack() as es:
            wo0 = sb(es, "wo0", [128, 8, D], BF16); wo0b = Buf("wo0")
            gb0 = sb(es, "gb0", [128, D], F32)
            dma("sp", gb0[:], gb_post0[:, :], [], [CONST])
            with ExitStack() as es2:
                ring = Ring([sb(es2, f"stgw{i}", [128, 1664], F32) for i in range(4)], "stgw")
                prep_w(ring, wo0, w_out0, 8, D, None, wo0b)
                rec.flush()
            F = {}
            F["xring"] = Ring([sb(es, f"xr{i}", [128, D], F32) for i in range(4)], "xr")
            F["string"] = Ring([sb(es, f"st{i}", [128, 4], F32) for i in range(4)], "st")
            psY = Ring([psum(es, f"psY{i}", [128, 1024], F32) for i in range(3)], "psY", psum=True)
            tmpr = Ring([sb(es, f"tmpc{i}", [128, D], F32) for i in range(3)], "tmpc")
            h1r = Ring([sb(es, f"h1_{i}", [128, D], F32) for i in range(4)], "h1")
            out_proj = make_out_proj(F, psY, tmpr)
            for t in [None] + list(range(T0, 64)):
                is_sample = t is None
                rows = 64 if is_sample else 128
                oc = 0 if is_sample else (t - T0) * 128
                A_, B_ = (sga_s, mixb_s) if is_sample else (sga, mixb)
                xt, xb = F["xring"].next()
                dma("sp", xt[:rows, :], xs_in[:, :] if is_sample else xw[t * 128:(t + 1) * 128, :], [], [xb])
                h1, h1b = h1r.next()
                out_proj(lambda kk, oc=oc, rows=rows, A_=A_, B_=B_: (A_[:, kk, oc:oc + rows] if kk < 4 else B_[:, kk - 4, oc:oc + rows]),
                         [bsgas, bmixbs] if is_sample else [bsga, bmixb], wo0, wo0b, rows, gb0, xt[:rows, :], xb, h1[:rows, :], h1b)
                r0 = NQC if is_sample else oc
                dma("pool", h1d[r0:r0 + rows, :], h1[:rows, :], [h1b], [])
            rec.flush()

    def stage_c():
        with ExitStack() as es:
            wc = sb(es, "wc", [128, 8, 2304], BF16); wcb = Buf("wc")
            wo1 = sb(es, "wo1", [128, 8, D], BF16); wo1b = Buf("wo1")
            gp1 = sb(es, "gp1c", [128, 8], F32)
            gb1 = sb(es, "gb1", [128, D], F32)
            esk = sb(es, "esk", [128, 16], F32)
            dma("sp", gp1[:], g_pre1[:, :], [], [CONST])
            dma("sp", gb1[:], gb_post1[:, :], [], [CONST]); dma("sp", esk[:], sinks_d[:, :], [], [CONST])
            act(esk[:], esk[:], AF.Exp, [CONST], [CONST])
            with ExitStack() as es2:
                ring = Ring([sb(es2, f"stgw{i}", [128, 1664], F32) for i in range(4)], "stgw")
                prep_w(ring, wc, c_w_in, 8, 2304, gp1, wcb)
                prep_w(ring, wo1, c_w_out, 8, D, None, wo1b)
                rec.flush()
            F = {}
            F["xsring"] = Ring([sb(es, f"xs{i}", [128, D], BF16) for i in range(2)], "xs")
            F["string"] = Ring([sb(es, f"st{i}", [128, 4], F32) for i in range(4)], "st")
            F["pT"] = Ring([psum(es, "pT0", [128, 8, 128], BF16)], "pT", psum=True)
            psY = Ring([psum(es, f"psY{i}", [128, 1024], F32) for i in range(1)], "psY", psum=True)
            F["psring"] = Ring([psum(es, f"psc_{i}", [128, 512], F32) for i in range(5)], "psv", psum=True)
            F["t1ring"] = Ring([sb(es, f"t1_{i}", [128, 512], F32) for i in range(2)], "t1")
            F["t2ring"] = Ring([sb(es, f"t2_{i}", [128, 512], F32) for i in range(2)], "t2")
            F["qrring"] = Ring([sb(es, f"qr{i}", [128, 512], BF16) for i in range(2)], "qr")
            h1r = Ring([sb(es, f"h1_{i}", [128, D], F32) for i in range(6)], "h1")
            tmpr = Ring([sb(es, f"tmpc{i}", [128, D], F32) for i in range(2)], "tmpc")
            xn1r = Ring([sb(es, f"xn1T{i}", [128, 8, 512], BF16) for i in range(2)], "xn1T")
            q1r = Ring([sb(es, f"q1T{i}", [128, 8, 512], BF16) for i in range(2)], "q1T")
            cur = {}
            sg1r = Ring([sb(es, f"sg1_{i}", [128, 8, 512], BF16) for i in range(2)], "sg1")
            mx1 = sb(es, "mx1", [128, 8, 512], BF16); mx1b = Buf("mx1")
            tab = Ring([sb(es, f"tab{i}", [128, 4, 512], F32) for i in range(1)], "tab")
            k1f = sb(es, "k1f", [128, 512], F32); k1fb = Buf("k1f")
            k1h = sb(es, "k1h", [128, 512], BF16); k1hb = Buf("k1h")
            K2 = sb(es, "K2", [128, 2, NQC], BF16); V1 = sb(es, "V1", [128, NT, 2, 128], BF16)
            bkv = [Buf(f"kv1_{i}") for i in range(NT)]
            K2s = sb(es, "K2s", [128, 2, 192], BF16); V1s = sb(es, "V1s", [128, 2, 2, 128], BF16); bkvs = Buf("kv1s")
            ptr = Ring([sb(es, f"ptc{i}", [128, 2, 512], BF16) for i in range(3)], "ptc")
            Rr = Ring([sb(es, f"Rc{i}", [128, 512], F32) for i in range(2)], "Rc")
            tmr = Ring([sb(es, f"tmc{i}", [128, 512], F32) for i in range(1)], "tmc")
            ostg = Ring([sb(es, f"ostgc{i}", [128, 128], F32) for i in range(2)], "ostgc")
            out_proj = make_out_proj(F, psY, tmpr)

            def attn_tile(nq, qc, kprev, kown, nk_own, vprev, vown, kvbufs, eoff):
                q1T, q1b = cur["q1T"]
                sg1, sg1b = cur["sg1"]
                for g in range(2):
                    for par in range(2):
                        pb = 64 * par
                        pt, ptb = ptr.next()
                        for ki, (kfn, nk) in enumerate(((kprev, 128), (kown, nk_own))):
                            ps, psb = F["psring"].next()
                            pv = ps[0:nk, 0:4 * nq].rearrange("p (j q) -> p j q", j=4)
                            mm(pv, kfn(g, par), q1T[pb:pb + 64, 4 * g:4 * g + 4, qc:qc + nq], True, True, [kvbufs[ki], q1b], [psb])
                            act(pt[0:nk, ki, 0:4 * nq].rearrange("p (j q) -> p j q", j=4), pv, AF.Exp, [psb], [ptb])
                        p0v = pt[:, 0, 0:4 * nq].rearrange("p (j q) -> p j q", j=4)
                        p1v = pt[:, 1, 0:4 * nq].rearrange("p (j q) -> p j q", j=4)
                        acc, accb_ = F["psring"].next()
                        av = acc[0:128, 0:4 * nq].rearrange("p (j q) -> p j q", j=4)
                        if nq > 64:
                            memset("pool", p0v[0:64, :, 64:128], 0.0, [ptb])
                            memset("pool", p1v[64:128, :, 0:64], 0.0, [ptb])
                        mm(av[:, :, 0:nq], vprev(g, 0, 128), p0v[:, :, 0:nq], True, False, [kvbufs[0], ptb], [accb_])
                        mm(av[:, :, 0:nq], vown(g, 0, nk_own), p1v[0:nk_own, :, 0:nq], False, True, [kvbufs[1], ptb], [accb_])
                        Rs, Rsb_ = Rr.next()
                        Rv = Rs[0:128, 0:4 * nq].rearrange("p (j q) -> p j q", j=4)
                        for j in range(4):
                            h = 8 * g + 2 * j + par
                            act(Rv[pb:pb + 64, j, :], av[64:128, j, 0:nq], AF.Ln, [accb_, CONST], [Rsb_], bias=esk[64:128, h:h + 1])
                        act(Rv[pb:pb + 64], Rv[pb:pb + 64], AF.Exp, [Rsb_], [Rsb_], scale=-1.0)
                        tm, tmb = tmr.next()
                        tv = tm[pb:pb + 64, 0:4 * nq].rearrange("p (j q) -> p j q", j=4)
                        tt("dve", tv, Rv[pb:pb + 64], sg1[pb:pb + 64, 4 * g:4 * g + 4, qc:qc + nq], ALU.mult, [Rsb_, sg1b], [tmb])
                        tt("dve", mx1[pb:pb + 64, 4 * g:4 * g + 4, qc:qc + nq], av[0:64, :, 0:nq], tv, ALU.mult, [accb_, tmb], [mx1b])

            def group(tl, is_sample):
                N = sum(r for _, r in tl)
                h1s = []
                for ti, (t, rows) in enumerate(tl):
                    oc = 0 if is_sample else (t - T0) * 128
                    A_, B_ = (sga_s, mixb_s) if is_sample else (sga, mixb)
                    h1, h1b = h1r.next()
                    r0 = NQC if is_sample else oc
                    dma("sp", h1[:rows, :], h1d[r0:r0 + rows, :], [], [h1b])
                    h1s.append((h1, rows, h1b))
                xn1, xn1b = xn1r.next()
                q1T, q1b = q1r.next()
                cur["q1T"] = (q1T, q1b)
                sg1, sg1b = sg1r.next()
                cur["sg1"] = (sg1, sg1b)
                frontend(F, [(h1[:, :], rows, hb) for h1, rows, hb in h1s], xn1, xn1b)
                tb_, tbb = tab.next()
                c0 = 0 if is_sample else (tl[0][0] - T0) * 128
                srcs = (cos1q_s, sin1q_s, cos1k_s, sin1k_s) if is_sample else (cos1q, sin1q, cos1k, sin1k)
                for i_, s_ in enumerate(srcs):
                    dma("pool", tb_[:, i_, 0:N], s_[:, c0:c0 + N], [], [tbb])
                for c in range(8):
                    ps, psb = F["psring"].next()
                    proj(ps, wc, wcb, C1_Q + c * 128, 128, xn1, xn1b, 0, N, psb)
                    rope_fm(F, ps, psb, 128, 0, 128, N, pm128b, tb_[:, 0, 0:N], tb_[:, 1, 0:N], tbb, q1T[:, c, 0:N], q1b)
                    ps, psb = F["psring"].next()
                    proj(ps, wc, wcb, C1_G + c * 128, 128, xn1, xn1b, 0, N, psb)
                    silu_fm(F, ps, psb, N, sg1[:, c, 0:N], sg1b)
                ps, psb = F["psring"].next()
                proj(ps, wc, wcb, C1_K, 128, xn1, xn1b, 0, N, psb)
                rope_fm(F, ps, psb, 128, 0, 128, N, pm128b, tb_[:, 2, 0:N], tb_[:, 3, 0:N], tbb, k1h[:, 0:N], k1hb, out_f32=k1f[:, 0:N], outfb=k1fb)
                col = 0
                for ti, (t, rows) in enumerate(tl):
                    if is_sample:
                        Kd, kc, kb_ = K2s, 128, bkvs
                    else:
                        Kd, kc, kb_ = K2, (t - T0) * 128, bkv[t - T0]
                    for g in range(2):
                        for par in range(2):
                            cp("dve", Kd[64 * par:64 * par + 64, g, kc:kc + rows], k1h[64 * g:64 * g + 64, col:col + rows], [k1hb], [kb_])
                    ps, psb = F["psring"].next()
                    for k in range(8):
                        mm(ps[0:rows, 0:128], xn1[:, k, col:col + rows], wc[:, k, C1_V:C1_V + 128], k == 0, k == 7, [xn1b, wcb], [psb])
                    pv = ps[0:rows, 0:128].rearrange("p (g e) -> p g e", g=2)
                    if is_sample:
                        cp("dve", V1s[0:rows, 1, :, 0:64], pv, [psb], [kb_])
                    else:
                        rec.op("dve", lambda t=t, pv=pv: nc.vector.tensor_scalar(out=V1[:, t - T0, :, 0:64], in0=pv, scalar1=valid[:, t:t + 1], scalar2=None, op0=ALU.mult), [psb, CONST], [kb_])
                        cp("act", V1[:, t - T0, :, 64:128], valid[:, t:t + 1].unsqueeze(2).to_broadcast([128, 2, 64]), [CONST], [kb_])
                    if is_sample or t == 63:
                        og, ogb = ostg.next()
                        cp("act", og[0:rows, :], ps[0:rows, 0:128], [psb], [ogb])
                        dma("pool", cv_s[64:128, :] if is_sample else cv_p[:, :], og[0:rows, :], [ogb], [])
                        ps, psb = F["psring"].next()
                        mm(ps[0:rows, 0:128], k1f[:, col:col + rows], identf[:, :], True, True, [k1fb, CONST], [psb])
                        og, ogb = ostg.next()
                        cp("act", og[0:rows, :], ps[0:rows, 0:128], [psb], [ogb])
                        dma("pool", ck_s[64:128, :] if is_sample else ck_p[:, :], og[0:rows, :], [ogb], [])
                    col += rows
                col = 0
                for ti, (t, rows) in enumerate(tl):
                    if is_sample:
                        attn_tile(64, 0,
                                  lambda g, par: K2s[64 * par:64 * par + 64, g, 0:128], lambda g, par: K2s[64 * par:64 * par + 64, g, 128:192], 64,
                                  lambda g, k0, k1: V1s[k0:k1, 0, g, :], lambda g, k0, k1: V1s[k0:k1, 1, g, :], [bkvs, bkvs], 0)
                    elif t > T0:
                        j = t - T0
                        attn_tile(128, col,
                                  lambda g, par, j=j: K2[64 * par:64 * par + 64, g, (j - 1) * 128:j * 128],
                                  lambda g, par, j=j: K2[64 * par:64 * par + 64, g, j * 128:(j + 1) * 128], 128,
                                  lambda g, k0, k1, j=j: V1[k0:k1, j - 1, g, :], lambda g, k0, k1, j=j: V1[k0:k1, j, g, :],
                                  [bkv[j - 1], bkv[j]], 0)
                    col += rows
                col = 0
                for ti, (t, rows) in enumerate(tl):
                    if is_sample or t > T0:
                        h1, _, h1b = h1s[ti]
                        y, yb = tmpr.next()
                        out_proj(lambda kk, col=col, rows=rows: mx1[:, kk, col:col + rows], [mx1b], wo1, wo1b, rows, gb1, h1[:rows, :], h1b, y[:rows, :], yb)
                        dma("sp", y_s[:, :] if is_sample else y_p[(t - 48) * 128:(t - 47) * 128, :], y[:rows, :], [yb], [])
                    col += rows

            with ExitStack() as es2:
                cs_ = tmpr.tiles[0][:, 0:256].rearrange("p (a b) -> p a b", a=2); bb_ = tmpr.bufs[0]
                cb2 = F["xsring"].tiles[0][:, 0:128]; cbb = F["xsring"].bufs[0]
                dma("sp", cs_[:, 0, :], cache_ck[:, :], [], [bb_]); dma("sp", cs_[:, 1, :], cache_cv[:, :], [], [bb_])
                cp("dve", cb2, cs_[:, 0, :], [bb_], [cbb])
                pT, pTb = F["pT"].next()
                tr(pT[:, 0, :], cb2, identb[:, :], [cbb, CONST], [pTb])
                for g in range(2):
                    for par in range(2):
                        cp("dve", K2s[64 * par:64 * par + 64, g, 0:128], pT[64 * g:64 * g + 64, 0, :], [pTb], [bkvs])
                cp("dve", V1s[:, 0, :, 0:64], cs_[:, 1, :].rearrange("p (g e) -> p g e", g=2), [bb_], [bkvs])
                memset("pool", V1s[:, :, :, 64:128], 1.0, [bkvs])
                dma("pool", ck_s[0:64, :], cache_ck[64:128, :], [], [])
                dma("pool", cv_s[0:64, :], cache_cv[64:128, :], [], [])
                rec.flush()
            group([(None, 64)], True)
            group([(T0, 128)], False)
            for q in range(12, 16):
                group([(q * 4 + i, 128) for i in range(4)], False)
            rec.flush()

    with ExitStack() as esP:
        sga = sb(esP, "sga", [128, 4, NQC], BF16)
        mixb = sb(esP, "mixb", [128, 4, NQC], BF16)
        stage_b2()
        with ExitStack() as esQ:
            QT = sb(esQ, "QT", [128, 8, NQC], BF16)
            if stages >= 2:
                stage_b1()
            if stages >= 3:
                stage_am()
        if stages >= 5:
            stage_c0()
    if stages >= 5:
        stage_c()
    rec.barrier()
    top.close()
    return nc, rec


_PROG = {}


def _get_prog(stages):
    if stages not in _PROG:
        _PROG[stages] = build_program(stages)
    return _PROG[stages]


def kernel(x_prompt, x_sample, cache_a_ckv, cache_a_krope, cache_b_k, cache_b_v, cache_c_k, cache_c_v,
           ab_pre_norm, ab_post_norm, ab_w_in, ab_q_norm, ab_kv_norm, ab_w_uq, ab_w_ukv, ab_rel_bias, ab_w_out,
           c_pre_norm, c_post_norm, c_w_in, c_sinks, c_w_out):
    stages = int(os.environ.get("KSTAGES", "99"))
    f = lambda a: np.ascontiguousarray(np.asarray(a, np.float32))
    x_prompt = f(x_prompt); x_sample = f(x_sample)
    shared = {
        "w_in": _wl(ab_w_in[0], 8), "g_pre": _pk(ab_pre_norm[0], 8),
        "w_uq": _wl(ab_w_uq[0], 3), "g_q": _pk(ab_q_norm[0], 3),
        "w_ukv": _wl(ab_w_ukv[0], 2), "g_kv": _pk(ab_kv_norm[0], 2),
        "w_out0": _wl(ab_w_out[0], 8), "gb_post0": f(np.broadcast_to(np.asarray(ab_post_norm[0], np.float32)[None, :], (128, D))),
        "c_w_in": _wl(c_w_in[0], 8), "g_pre1": _pk(c_pre_norm[0], 8),
        "c_w_out": _wl(c_w_out[0], 8), "gb_post1": f(np.broadcast_to(np.asarray(c_post_norm[0], np.float32)[None, :], (128, D))),
        "pm96": _perm_lhsT(96, [(64, 32)]), "pm128": _perm_lhsT(128, [(0, 16), (64, 16)]),
        "ident": np.eye(128, dtype=np.float32),
        "sinks": f(np.broadcast_to(np.asarray(c_sinks[0], np.float32)[None, :], (128, 16))),
    }
    tbl = np.asarray(ab_rel_bias[0], np.float32)
    kk = np.arange(128)[:, None]; qq = np.arange(128)[None, :]
    bd = np.zeros((128, 2, 8, 128), np.float32)
    for d_ in range(2):
        idx = np.clip(128 * d_ + qq - kk, -128, 128) + 128
        bd[:, d_, :, :] = np.transpose(tbl[:, idx], (1, 0, 2))
    shared["bd"] = bd
    shared["cb"] = f(np.broadcast_to(tbl[None, :, 256], (128, 8)))
    pos_s = PAST + np.arange(64)
    shared["cosk_s"], shared["sink_s"] = _rope_tables_fm(pos_s, 32, 96, [64], 1.0)
    shared["cosq_s"], shared["sinq_s"] = _rope_tables_fm(pos_s, 32, 96, [64], A_SCALE)
    shared["cos1q_s"], shared["sin1q_s"] = _rope_tables_fm(pos_s, 16, 128, [0, 64], C_SCALE)
    shared["cos1k_s"], shared["sin1k_s"] = _rope_tables_fm(pos_s, 16, 128, [0, 64], 1.0)
    tabs = {}
    in_maps = []
    for core in range(8):
        b, c = core // 4, core % 4
        end = 2048 * (c + 1)
        start = end - 8192
        m = dict(shared)
        xw = np.zeros((8192, D), np.float32)
        lo = max(0, -start)
        xw[lo:] = x_prompt[b, start + lo:end]
        m["xw"] = xw
        m["xs_in"] = f(x_sample[core])
        if c not in tabs:
            pos = start + np.arange(8192)
            t = {}
            t["cosk"], t["sink"] = _rope_tables_fm(pos, 32, 96, [64], 1.0)
            t["cosq"], t["sinq"] = _rope_tables_fm(pos[5632:], 32, 96, [64], A_SCALE)
            t["cos1q"], t["sin1q"] = _rope_tables_fm(pos[6016:], 16, 128, [0, 64], C_SCALE)
            t["cos1k"], t["sin1k"] = _rope_tables_fm(pos[6016:], 16, 128, [0, 64], 1.0)
            t["valid"] = f((pos >= 0).astype(np.float32).reshape(64, 128).T)
            tabs[c] = t
        m.update(tabs[c])
        m["cache_ckv"] = f(cache_a_ckv[0, core]); m["cache_kr"] = f(cache_a_krope[0, core])
        m["cache_bk"] = f(np.asarray(cache_b_k[0, core]).reshape(512, 512)); m["cache_bv"] = f(np.asarray(cache_b_v[0, core]).reshape(512, 512))
        m["cache_ck"] = f(np.asarray(cache_c_k[0, core]).reshape(128, 128)); m["cache_cv"] = f(np.asarray(cache_c_v[0, core]).reshape(128, 128))
        in_maps.append(m)
    nc, _ = _get_prog(stages)
    res = run_bass_kernel_spmd(nc, in_maps, core_ids=list(range(8)))
    R = res.results
    y_prompt = np.zeros((2, SEQ, D), np.float32); y_sample = np.zeros((8, 64, D), np.float32)
    a_ckv_p = np.zeros((1, 2, SEQ, 256), np.float32); a_kr_p = np.zeros((1, 2, SEQ, 32), np.float32)
    b_k_p = np.zeros((1, 2, 512, 8, 64), np.float32); b_v_p = np.zeros((1, 2, 512, 8, 64), np.float32)
    c_k_p = np.zeros((1, 2, 128, 2, 64), np.float32); c_v_p = np.zeros((1, 2, 128, 2, 64), np.float32)
    a_ckv_s = np.zeros((1, 8, 64, 256), np.float32); a_kr_s = np.zeros((1, 8, 64, 32), np.float32)
    b_k_s = np.zeros((1, 8, 512, 8, 64), np.float32); b_v_s = np.zeros((1, 8, 512, 8, 64), np.float32)
    c_k_s = np.zeros((1, 8, 128, 2, 64), np.float32); c_v_s = np.zeros((1, 8, 128, 2, 64), np.float32)
    for core in range(8):
        b, c = core // 4, core % 4
        r = R[core]
        sl = slice(2048 * c, 2048 * (c + 1))
        y_prompt[b, sl] = r["y_p"]; y_sample[core] = r["y_s"]
        a_ckv_p[0, b, sl] = r["ckv_p"]; a_kr_p[0, b, sl] = r["kr_p"]
        if c == 3:
            b_k_p[0, b] = r["bk_p"].reshape(512, 8, 64); b_v_p[0, b] = r["bv_p"].reshape(512, 8, 64)
            c_k_p[0, b] = r["ck_p"].reshape(128, 2, 64); c_v_p[0, b] = r["cv_p"].reshape(128, 2, 64)
        a_ckv_s[0, core] = r["ckv_s"]; a_kr_s[0, core] = r["kr_s"]
        b_k_s[0, core] = r["bk_s"].reshape(512, 8, 64); b_v_s[0, core] = r["bv_s"].reshape(512, 8, 64)
        c_k_s[0, core] = r["ck_s"].reshape(128, 2, 64); c_v_s[0, core] = r["cv_s"].reshape(128, 2, 64)
    return (y_prompt, y_sample, a_ckv_p, a_kr_p, b_k_p, b_v_p, c_k_p, c_v_p,
            a_ckv_s, a_kr_s, b_k_s, b_v_s, c_k_s, c_v_s)
```

```python
import os
import numpy as np
import concourse.bass as bass
import concourse.mybir as mybir
from concourse.bass_utils import run_bass_kernel_spmd

F32, BF16 = mybir.dt.float32, mybir.dt.bfloat16
AF = mybir.ActivationFunctionType
ALU = mybir.AluOpType

D = 1024
SEQ = 8192
PAST = 4096
CH = 64
EPS = 1e-6
THETA = 500000.0
A_SCALE = 96 ** -0.5
B_SCALE = 64 ** -0.5
C_SCALE = 64 ** -0.5
C_QLAT, C_CKV, C_KR, C_GA, C_QB, C_KB, C_VB, C_GB = 0, 384, 640, 672, 1184, 1696, 2208, 2720
C1_Q, C1_K, C1_V, C1_G = 0, 1024, 1152, 1280
NT = 17
T0 = 47
NQC = NT * 128


class Buf:
    __slots__ = ("name", "psum")

    def __init__(self, name, psum=False):
        self.name = name
        self.psum = psum


class Rec:
    ENGS = ("pe", "act", "dve", "pool", "sp")

    def __init__(self, nc, sems, dsems):
        self.nc = nc
        self.eng = {"pe": nc.tensor, "act": nc.scalar, "dve": nc.vector, "pool": nc.gpsimd, "sp": nc.sync}
        self.sem = sems
        self.tick = {e: 0 for e in self.ENGS}
        self.known = {e: {} for e in self.ENGS}
        self.dsems = dsems
        self.dcount = {q: 0 for q in dsems}
        self.items = []
        self.n_ins = 0
        self.reorder = os.environ.get("KREORDER", "1") == "1"

    def op(self, eng, fn, r=(), w=(), cost=300.0):
        self.items.append((eng, fn, tuple(r), tuple(w), False, cost))

    def dma(self, q, fn, r=(), w=(), cost=3000.0):
        self.items.append((q, fn, tuple(r), tuple(w), True, cost))

    def _schedule(self, items, alld, W=1280):
        n = len(items)
        succ = [[] for _ in range(n)]
        left = [0] * n
        for i in range(n):
            left[i] = len(alld[i])
            for j in alld[i]:
                succ[j].append(i)
        rdy = [0.0] * n
        fin = [0.0] * n
        engfree = {}
        done = [False] * n
        order = []
        ready = [i for i in range(min(n, W)) if left[i] == 0]
        hi = min(n, W)
        lo = 0
        while len(order) < n:
            best, bt = -1, None
            for i in ready:
                t = max(rdy[i], engfree.get(items[i][0], 0.0))
                if bt is None or t < bt - 1e-9 or (abs(t - bt) <= 1e-9 and i < best):
                    best, bt = i, t
            i = best
            ready.remove(i)
            eng, isd, cost = items[i][0], items[i][4], items[i][5]
            if isd:
                engfree[eng] = bt + 60.0
                fin[i] = bt + cost
            else:
                engfree[eng] = bt + cost
                fin[i] = bt + cost
            done[i] = True
            order.append(i)
            for k in succ[i]:
                left[k] -= 1
                if fin[i] > rdy[k]:
                    rdy[k] = fin[i]
                if left[k] == 0 and k < hi:
                    ready.append(k)
            while lo < n and done[lo]:
                lo += 1
            nh = min(n, lo + W)
            while hi < nh:
                if left[hi] == 0 and not done[hi]:
                    ready.append(hi)
                hi += 1
        return order

    def _ensure(self, e, sem, val):
        k = self.known[e]
        if k.get(sem, 0) < val:
            self.eng[e].wait_ge(sem, val)
            k[sem] = val

    def flush(self):
        items = self.items
        n = len(items)
        last_w = {}
        readers = {}
        need = [None] * n
        inc = [False] * n
        alld = [None] * n
        for i, (eng, fn, R, W, isd, _c) in enumerate(items):
            d = {}
            for b in R:
                j = last_w.get(b)
                if j is not None:
                    d[j] = "raw"
                if b.psum:
                    for r_ in readers.get(b, ()):
                        if items[r_][0] != eng and r_ not in d:
                            d[r_] = "rar"
            for b in W:
                j = last_w.get(b)
                if j is not None and j not in d:
                    d[j] = "waw"
                for r_ in readers.get(b, ()):
                    if r_ != i and r_ not in d:
                        d[r_] = "war"
            lst = []
            for j, kind in d.items():
                ej, jd = items[j][0], items[j][4]
                if (not jd) and (not isd) and ej == eng:
                    if eng != "pool" and (kind != "raw" or eng == "pe"):
                        continue
                lst.append(j)
                inc[j] = True
            need[i] = lst
            alld[i] = list(d.keys())
            for b in R:
                readers.setdefault(b, []).append(i)
            for b in W:
                last_w[b] = i
                readers[b] = []
        order = self._schedule(items, alld) if (self.reorder and n > 2) else list(range(n))
        last_eng = {}
        for i in order:
            if not items[i][4]:
                last_eng[items[i][0]] = i
        for e, i in last_eng.items():
            inc[i] = True
        ev = [None] * n
        snap = [None] * n
        pos = [0] * n
        for p_, i_ in enumerate(order):
            pos[i_] = p_
        for i in order:
            eng, fn, R, W, isd, _c = items[i]
            k_ = self.known[eng]
            pend = []
            for j in sorted(need[i], key=lambda j_: -pos[j_]):
                s, v = ev[j]
                if k_.get(s, 0) >= v:
                    continue
                pend.append((s, v))
                k_[s] = v
                for s2, v2 in snap[j].items():
                    if k_.get(s2, 0) < v2:
                        k_[s2] = v2
            if isd:
                m = self.dcount[eng]
                K = len(self.dsems[eng])
                ds = self.dsems[eng][m % K]
                if m >= K and k_.get(ds, 0) < 16 * (m // K):
                    pend.append((ds, 16 * (m // K)))
                    k_[ds] = 16 * (m // K)
            for s, v in pend[:-1]:
                self.eng[eng].wait_ge(s, v)
            if isd:
                ins = fn()
                if pend:
                    ins._wait_ge(pend[-1][0], pend[-1][1])
                ins.then_inc(ds, 16)
                ev[i] = (ds, 16 * (m // K + 1))
                snap[i] = dict(k_)
                self.dcount[eng] = m + 1
            else:
                ins = fn()
                if pend:
                    ins._wait_ge(pend[-1][0], pend[-1][1])
                if inc[i]:
                    self.tick[eng] += 1
                    ins.then_inc(self.sem[eng], 1)
                    ev[i] = (self.sem[eng], self.tick[eng])
                    snap[i] = dict(k_)
            self.n_ins += 1
        self.items = []
        self.barrier()

    def barrier(self):
        for e in self.ENGS:
            for f in self.ENGS:
                if f != e and self.tick[f] > 0:
                    self._ensure(e, self.sem[f], self.tick[f])
            for q, lst in self.dsems.items():
                m = self.dcount[q]
                K = len(lst)
                for k_, s in enumerate(lst):
                    cnt = (m - k_ + K - 1) // K if m > k_ else 0
                    if cnt > 0:
                        self._ensure(e, s, 16 * cnt)


class Ring:
    def __init__(self, tiles, name, bufs=None, psum=False):
        self.tiles = tiles
        self.bufs = bufs if bufs is not None else [Buf(f"{name}{i}", psum) for i in range(len(tiles))]
        self.i = 0

    def next(self):
        k = self.i % len(self.tiles)
        self.i += 1
        return self.tiles[k], self.bufs[k]


def _rope_tables_fm(pos, rot, rows, row0s, scale):
    half = rot // 2
    inv = np.power(np.float32(THETA), -np.arange(half, dtype=np.float32) * np.float32(2.0) / np.float32(rot)).astype(np.float32)
    ang = pos.astype(np.float32)[None, :] * inv[:, None]
    cos = np.cos(ang).astype(np.float32)
    sin = np.sin(ang).astype(np.float32)
    n = len(pos)
    ct = np.full((rows, n), scale, np.float32)
    st = np.zeros((rows, n), np.float32)
    for row0 in row0s:
        ct[row0:row0 + half] = cos * scale
        ct[row0 + half:row0 + rot] = cos * scale
        st[row0:row0 + half] = sin * scale
        st[row0 + half:row0 + rot] = sin * scale
    return ct, st


def _perm_lhsT(rows, blocks):
    P = np.zeros((rows, rows), np.float32)
    for row0, rot in blocks:
        half = rot // 2
        for j in range(half):
            P[row0 + j + half, row0 + j] = -1.0
            P[row0 + j, row0 + j + half] = 1.0
    return P


def _pk(v, k):
    return np.ascontiguousarray(np.asarray(v, np.float32).reshape(k, 128).T)


def _wl(w, k):
    w = np.asarray(w, np.float32)
    return np.ascontiguousarray(w.reshape(k, 128, w.shape[1]).transpose(1, 0, 2))


def build_program(stages=99):
    from contextlib import ExitStack
    nc = bass.Bass("TRN2", target_bir_lowering=False)

    def din(name, shape):
        return nc.dram_tensor(name, list(shape), F32, kind="ExternalInput").ap()

    def dout(name, shape):
        return nc.dram_tensor(name, list(shape), F32, kind="ExternalOutput").ap()

    xw = din("xw", [8192, D]); xs_in = din("xs_in", [64, D])
    w_in = din("w_in", [128, 8, 3232]); g_pre = din("g_pre", [128, 8])
    w_uq = din("w_uq", [128, 3, 768]); g_q = din("g_q", [128, 3])
    w_ukv = din("w_ukv", [128, 2, 1024]); g_kv = din("g_kv", [128, 2])
    w_out0 = din("w_out0", [128, 8, D]); gb_post0 = din("gb_post0", [128, D])
    c_w_in = din("c_w_in", [128, 8, 2304]); g_pre1 = din("g_pre1", [128, 8])
    c_w_out = din("c_w_out", [128, 8, D]); gb_post1 = din("gb_post1", [128, D])
    cosk = din("cosk", [96, 8192]); sink = din("sink", [96, 8192])
    cosq = din("cosq", [96, 2560]); sinq = din("sinq", [96, 2560])
    cosk_s = din("cosk_s", [96, 64]); sink_s = din("sink_s", [96, 64])
    cosq_s = din("cosq_s", [96, 64]); sinq_s = din("sinq_s", [96, 64])
    cos1q = din("cos1q", [128, NQC]); sin1q = din("sin1q", [128, NQC])
    cos1k = din("cos1k", [128, NQC]); sin1k = din("sin1k", [128, NQC])
    cos1q_s = din("cos1q_s", [128, 64]); sin1q_s = din("sin1q_s", [128, 64])
    cos1k_s = din("cos1k_s", [128, 64]); sin1k_s = din("sin1k_s", [128, 64])
    pm96_d = din("pm96", [96, 96]); pm128_d = din("pm128", [128, 128]); ident_d = din("ident", [128, 128])
    valid_d = din("valid", [128, 64])
    bd_d = din("bd", [128, 2, 8, 128]); cb_d = din("cb", [128, 8])
    sinks_d = din("sinks", [128, 16])
    cache_ckv = din("cache_ckv", [PAST, 256]); cache_kr = din("cache_kr", [PAST, 32])
    cache_bk = din("cache_bk", [512, 512]); cache_bv = din("cache_bv", [512, 512])
    cache_ck = din("cache_ck", [128, 128]); cache_cv = din("cache_cv", [128, 128])

    y_p = dout("y_p", [2048, D]); y_s = dout("y_s", [64, D])
    ckv_p = dout("ckv_p", [2048, 256]); kr_p = dout("kr_p", [2048, 32])
    bk_p = dout("bk_p", [512, 512]); bv_p = dout("bv_p", [512, 512])
    ck_p = dout("ck_p", [128, 128]); cv_p = dout("cv_p", [128, 128])
    ckv_s = dout("ckv_s", [64, 256]); kr_s = dout("kr_s", [64, 32])
    bk_s = dout("bk_s", [512, 512]); bv_s = dout("bv_s", [512, 512])
    ck_s = dout("ck_s", [128, 128]); cv_s = dout("cv_s", [128, 128])

    h1d = nc.dram_tensor("h1_scratch", [NQC + 64, D], F32, kind="Internal").ap()
    top = ExitStack()
    sems = {e: top.enter_context(nc.semaphore("sem_" + e)) for e in Rec.ENGS}
    dsems = {q: [top.enter_context(nc.semaphore(f"dsem_{q}{i}")) for i in range(8)] for q in ("sp", "pool")}
    rec = Rec(nc, sems, dsems)
    E = rec.eng

    uid = {"n": 0}

    def sb(es, name, shape, dt):
        uid["n"] += 1
        return es.enter_context(nc.sbuf_tensor(f"s{uid['n']}_{name}", list(shape), dt))

    def psum(es, name, shape, dt):
        uid["n"] += 1
        return es.enter_context(nc.psum_tensor(f"p{uid['n']}_{name}", list(shape), dt))

    def fsz(ap):
        n = 1
        for d_ in list(ap.shape)[1:]:
            n *= int(d_)
        return n

    def ecost(eng, ap):
        f = fsz(ap)
        if eng == "act":
            return f / 1.2 + 200.0
        if eng == "dve":
            return f / 0.96 + 120.0
        return f / 0.45 + 150.0

    def mm(out, lhsT, rhs, start, stop, r, w, sgc=False):
        f32 = rhs.dtype == F32
        rec.op("pe", lambda: nc.tensor.matmul(out, lhsT=lhsT, rhs=rhs, start=start, stop=stop, skip_group_check=sgc), r, w,
               cost=(max(64, fsz(rhs)) / 2.4) * (4 if f32 else 1) + 70.0)

    def tr(out, in_, ident, r, w):
        rec.op("pe", lambda: nc.tensor.transpose(out, in_, ident), r, w, cost=120.0)

    def act(out, in_, func, r, w, **kw):
        rec.op("act", lambda: nc.scalar.activation(out=out, in_=in_, func=func, **kw), r, w, cost=ecost("act", out))

    def cp(eng, out, in_, r, w):
        if eng == "act":
            rec.op("act", lambda: nc.scalar.copy(out=out, in_=in_), r, w, cost=ecost("act", out))
        else:
            rec.op(eng, lambda: E[eng].tensor_copy(out=out, in_=in_), r, w, cost=ecost(eng, out))

    def tt(eng, out, in0, in1, op, r, w):
        rec.op(eng, lambda: E[eng].tensor_tensor(out=out, in0=in0, in1=in1, op=op), r, w, cost=ecost(eng, out))

    def ts(eng, out, in0, s1, op0, r, w):
        rec.op(eng, lambda: E[eng].tensor_scalar(out=out, in0=in0, scalar1=s1, scalar2=None, op0=op0), r, w, cost=ecost(eng, out))

    def stt(eng, out, in0, scalar, in1, op0, op1, r, w):
        rec.op(eng, lambda: E[eng].scalar_tensor_tensor(out=out, in0=in0, scalar=scalar, in1=in1, op0=op0, op1=op1), r, w, cost=ecost(eng, out))

    def recip(out, in_, r, w):
        rec.op("dve", lambda: nc.vector.reciprocal(out=out, in_=in_), r, w, cost=5 * ecost("dve", out))

    def memset(eng, ap, val, w):
        rec.op(eng, lambda: E[eng].memset(ap, val), [], w, cost=ecost(eng, ap))

    def dma(q, out, in_, r, w):
        nb = 1
        for d_ in list(out.shape):
            nb *= int(d_)
        rec.dma(q, lambda: E[q].dma_start(out=out, in_=in_), r, w, cost=2500.0 + nb * 4 / 120.0)

    alt = {"i": 0}

    def alt_eng(choices=("dve", "act")):
        alt["i"] += 1
        return choices[alt["i"] % len(choices)]

    identb = sb(top, "identb", [128, 128], BF16); identf = sb(top, "identf", [128, 128], F32)
    onesb = sb(top, "onesb", [128, 128], BF16); onesf = sb(top, "onesf", [128, 128], F32)
    pm96b = sb(top, "pm96b", [128, 96], BF16); pm128b = sb(top, "pm128b", [128, 128], BF16)
    valid = sb(top, "valid", [128, 64], F32); vone = sb(top, "vone", [128, 64], F32)
    QT_s = sb(top, "QT_s", [128, 8, 64], BF16)
    sga_s = sb(top, "sga_s", [128, 4, 64], BF16)
    mixb_s = sb(top, "mixb_s", [128, 4, 64], BF16)
    cs_new = sb(top, "cs_new", [128, 2, 64], BF16)
    krs_new = sb(top, "krs_new", [128, 64], BF16)
    CONST = Buf("const")
    bQT, bsga, bmixb = Buf("QT"), Buf("sga"), Buf("mixb")
    bQTs, bsgas, bmixbs, bcsn, bkrsn = Buf("QTs"), Buf("sgas"), Buf("mixbs"), Buf("csn"), Buf("krsn")

    with ExitStack() as es:
        stg = sb(es, "cstg", [128, 3, 128], F32)
        dma("sp", identf[:], ident_d[:, :], [], [CONST])
        dma("sp", stg[0:96, 0, 0:96], pm96_d[:, :], [], [CONST])
        dma("sp", stg[:, 1, :], pm128_d[:, :], [], [CONST])
        dma("sp", valid[:], valid_d[:, :], [], [CONST])
        cp("dve", identb[:], identf[:], [CONST], [CONST])
        cp("dve", pm96b[0:96, :], stg[0:96, 0, 0:96], [CONST], [CONST])
        cp("dve", pm128b[:], stg[:, 1, :], [CONST], [CONST])
        memset("dve", onesb[:], 1.0, [CONST]); memset("dve", onesf[:], 1.0, [CONST]); memset("dve", vone[:], 1.0, [CONST])
        rec.flush()

    def prep_w(es_ring, dst, src, K, C, gain, wbuf, c_lo=0, c_hi=None):
        c_hi = C if c_hi is None else c_hi
        CP = 1664
        for k in range(K):
            c = c_lo
            while c < c_hi:
                n = min(CP, c_hi - c)
                st_, sbf = es_ring.next()
                dma("sp" if alt["i"] % 4 < 2 else "pool", st_[:, :n], src[:, k, c:c + n], [], [sbf])
                e = alt_eng(("dve", "act"))
                o = dst[:, k, c - c_lo:c - c_lo + n]
                if gain is None:
                    cp(e, o, st_[:, :n], [sbf], [wbuf])
                elif e == "act":
                    act(o, st_[:, :n], AF.Copy, [sbf, CONST], [wbuf], scale=gain[:, k:k + 1])
                else:
                    ts(e, o, st_[:, :n], gain[:, k:k + 1], ALU.mult, [sbf, CONST], [wbuf])
                c += n

    def frontend(F, tiles, xnT, xnb):
        for ti, (src, rows, srcb) in enumerate(tiles):
            if srcb is None:
                xt, xb = F["xring"].next()
                dma("sp", xt[:rows, :], src, [], [xb])
            else:
                xt, xb = src, srcb
            st_, stb = F["string"].next()
            xs, xsb = F["xsring"].next()
            if F.get("junk") is None:
                act(xs[:rows, :], xt[:rows, :], AF.Square, [xb], [xsb, stb], accum_out=st_[:rows, 0:1])
            else:
                act(F["junk"][:rows, :], xt[:rows, :], AF.Square, [xb], [F["junkb"], stb], accum_out=st_[:rows, 0:1])
            act(st_[:rows, 1:2], st_[:rows, 0:1], AF.Ln, [stb], [stb], scale=1.0 / D, bias=EPS)
            act(st_[:rows, 2:3], st_[:rows, 1:2], AF.Exp, [stb], [stb], scale=-0.5)
            ts("dve", xs[:rows, :], xt[:rows, :], st_[:rows, 2:3], ALU.mult, [xb, stb], [xsb])
            pT, pTb = F["pT"].next()
            for k in range(8):
                tr(pT[:, k, :rows], xs[:rows, k * 128:(k + 1) * 128], identb[:rows, :rows], [xsb, CONST], [pTb])
            cp(alt_eng(("dve", "act")), xnT[:, :, ti * 128:ti * 128 + rows], pT[:, :, :rows], [pTb], [xnb])

    def proj(ps, W, Wb, c0, ncols, xnT, xnb, t0, n, psb):
        for k in range(8):
            mm(ps[0:ncols, 0:n], W[:, k, c0:c0 + ncols], xnT[:, k, t0:t0 + n], k == 0, k == 7, [Wb, xnb], [psb])

    def rms_fm(F, chunks, n, nfeat):
        sq, sqb = F["sqring"].next()
        for c, (ps, psb) in enumerate(chunks):
            act(sq[:, c, :n], ps[:, :n], AF.Square, [psb], [sqb])
        pss, pssb = F["psring"].next()
        for c in range(len(chunks)):
            mm(pss[:, :n], onesb[:, :], sq[:, c, :n], c == 0, c == len(chunks) - 1, [sqb, CONST], [pssb])
        rs, rsb = F["rsring"].next()
        act(rs[:, :n], pss[:, :n], AF.Ln, [pssb], [rsb], scale=1.0 / nfeat, bias=EPS)
        act(rs[:, :n], rs[:, :n], AF.Exp, [rsb], [rsb], scale=-0.5)
        return rs, rsb

    def rope_fm(F, ps, psb, R, p0, p1, n, pm, cosT, sinT, tb, out_bf, outb, out_f32=None, outfb=None):
        qr, qrb = F["qrring"].next()
        cp("act", qr[0:R, :n], ps[0:R, :n], [psb], [qrb])
        ps2, ps2b = F["psring"].next()
        mm(ps2[0:R, :n], pm[0:R, 0:R], qr[0:R, :n], True, True, [qrb, CONST], [ps2b])
        t1, t1b = F["t1ring"].next()
        t2, t2b = F["t2ring"].next()
        tt("dve", t1[p0:p1, :n], ps[p0:p1, :n], cosT, ALU.mult, [psb, tb], [t1b])
        tt("dve", t2[p0:p1, :n], ps2[p0:p1, :n], sinT, ALU.mult, [ps2b, tb], [t2b])
        if out_f32 is not None:
            tt("pool", out_f32, t1[p0:p1, :n], t2[p0:p1, :n], ALU.add, [t1b, t2b], [outfb])
            cp("pool", out_bf, out_f32, [outfb], [outb])
        else:
            tt("pool", out_bf, t1[p0:p1, :n], t2[p0:p1, :n], ALU.add, [t1b, t2b], [outb])

    def silu_fm(F, ps, psb, n, dst, dstb):
        e1, e1b = F["t1ring"].next()
        act(e1[:, :n], ps[:, :n], AF.Exp, [psb], [e1b], scale=-1.0)
        act(e1[:, :n], e1[:, :n], AF.Ln, [e1b], [e1b], bias=1.0)
        act(e1[:, :n], e1[:, :n], AF.Exp, [e1b], [e1b], scale=-1.0)
        tt("dve", dst, ps[:, :n], e1[:, :n], ALU.mult, [psb, e1b], [dstb])

    def norm_out(F, po_views, pob, nq, sg_fn, out_fn, extra_l=None):
        for hf, (po, pb_) in enumerate(zip(po_views, pob)):
            for par in range(2):
                p0 = 64 * par
                Rs, Rsb_ = F["Rring"].next()
                Rv = Rs[p0:p0 + 64, :].rearrange("p (a q) -> p a q", a=2)[:, :, :nq]
                act(Rv, po[64:128, :, par, :nq], AF.Ln, [pb_], [Rsb_], bias=1e-30)
                act(Rv, Rv, AF.Exp, [Rsb_], [Rsb_], scale=-1.0)
                sg, sgb_ = sg_fn(hf, par)
                o, ob = out_fn(hf, par)
                tm, tmb = F["tmring"].next()
                tv = tm[p0:p0 + 64, :].rearrange("p (a q) -> p a q", a=2)[:, :, :nq]
                tt("dve", tv, Rv, sg, ALU.mult, [Rsb_, sgb_], [tmb])
                tt("dve", o, po[0:64, :, par, :nq], tv, ALU.mult, [pb_, tmb], [ob])

    def stage_b2():
        with ExitStack() as es:
            Wp = sb(es, "Wb2", [128, 8, 2048], BF16); Wb = Buf("Wb2")
            gp = sb(es, "gp", [128, 8], F32)
            dma("sp", gp[:], g_pre[:, :], [], [CONST])
            bdb = sb(es, "bdb", [128, 2, 8, 128], BF16); bbd = Buf("bd")
            with ExitStack() as es2:
                ring = Ring([sb(es2, f"stgw{i}", [128, 1664], F32) for i in range(4)], "stgw")
                prep_w(ring, Wp, w_in, 8, 3232, gp, Wb, c_lo=C_QB, c_hi=3232)
                bdf = sb(es2, "bdf", [128, 2, 8, 128], F32); cbt = sb(es2, "cbt", [128, 8], F32)
                dma("sp", bdf[:], bd_d[:, :, :, :], [], [bbd]); dma("sp", cbt[:], cb_d[:, :], [], [bbd])
                rec.op("dve", lambda: nc.vector.tensor_scalar(out=cbt[:], in0=cbt[:], scalar1=-1.0, scalar2=None, op0=ALU.mult), [bbd], [bbd])
                for d_ in range(2):
                    for h in range(8):
                        act(bdb[:, d_, h, :], bdf[:, d_, h, :], AF.Exp, [bbd], [bbd], bias=cbt[:, h:h + 1])
                rec.flush()
            O_QB, O_KB, O_VB, O_GB = 0, 512, 1024, 1536
            F = {}
            F["xring"] = Ring([sb(es, f"xr{i}", [128, D], F32) for i in range(3)], "xr")
            F["xsring"] = Ring([sb(es, f"xs{i}", [128, D], BF16) for i in range(2)], "xs")
            F["junk"] = sb(es, "junk", [128, D], BF16); F["junkb"] = Buf("junk")
            F["string"] = Ring([sb(es, f"st{i}", [128, 4], F32) for i in range(4)], "st")
            F["pT"] = Ring([psum(es, "pT0", [128, 8, 128], BF16)], "pT", psum=True)
            ps2 = [psum(es, f"ps2_{i}", [128, 1024], F32) for i in range(2)]
            ps1 = [psum(es, f"ps1_{i}", [128, 512], F32) for i in range(3)]
            views = [ps2[0][:, 0:512], ps2[0][:, 512:1024], ps2[1][:, 0:512], ps2[1][:, 512:1024]] + [p[:, :] for p in ps1]
            F["psring"] = Ring(views, "psv", psum=True)
            vb_ = F["psring"].bufs
            F["prring"] = Ring([ps1[2][:, :]], "pr", [vb_[6]])
            F["t1ring"] = Ring([sb(es, f"t1_{i}", [128, 512], F32) for i in range(2)], "t1")
            F["Rring"] = Ring([sb(es, f"Rs{i}", [128, 512], F32) for i in range(2)], "Rs")
            F["tmring"] = Ring([sb(es, f"tm{i}", [128, 256], F32) for i in range(2)], "tm")
            xnr = Ring([sb(es, f"xnT{i}", [128, 8, 512], BF16) for i in range(2)], "xnT")
            sgbq = sb(es, "sgbq", [128, 4, 512], BF16); bsgbq = Buf("sgbq")
            qbq = sb(es, "qbq", [128, 4, 512], BF16); bqbq = Buf("qbq")
            kbT = sb(es, "kbT", [128, 4, 12 * 128], BF16)
            vb = sb(es, "vb", [128, 12, 8, 128], BF16)
            bslot = [Buf(f"kvslot{i}") for i in range(12)]
            ptr = Ring([sb(es, f"pTs{i}", [128, 640], BF16) for i in range(4)], "pTs")
            ostg = Ring([sb(es, f"ostg{i}", [128, 512], F32) for i in range(2)], "ostg")
            kbT_s = sb(es, "kbT_s", [128, 4, 640], BF16); vb_s = sb(es, "vb_s", [128, 5, 8, 128], BF16); bkvs = Buf("kvs")

            def b_attention(nq, qT, qTb, qc0, keyT, vaug, kbufs, nks, sg_fn, out_fn):
                po = [ps1[0], ps1[1]]
                pob = [vb_[4], vb_[5]]
                for hp in range(4):
                    for j in range(5):
                        for h in (2 * hp, 2 * hp + 1):
                            c, pb = h // 2, (h % 2) * 64
                            psS = ps2[h % 2]; psSb = [vb_[2 * (h % 2)], vb_[2 * (h % 2) + 1]]
                            nk = nks[j]
                            o = psS[0:nk, j * 128:j * 128 + nq]
                            wb_ = [psSb[0] if j < 4 else psSb[1]]
                            mm(o, keyT(j, c, pb), qT[pb:pb + 64, c, qc0:qc0 + nq], True, True, [kbufs[j], qTb], wb_)
                    for h in (2 * hp, 2 * hp + 1):
                        psS = ps2[h % 2]; psSb = [vb_[2 * (h % 2)], vb_[2 * (h % 2) + 1]]
                        pt, ptb = ptr.next()
                        act(pt[:, 0:512].rearrange("p (j q) -> p j q", j=4)[:, :, 0:nq], psS[:, 0:512].rearrange("p (j q) -> p j q", j=4)[:, :, 0:nq], AF.Exp, [psSb[0]], [ptb])
                        act(pt[0:nks[4], 512:512 + nq], psS[0:nks[4], 512:512 + nq], AF.Exp, [psSb[1]], [ptb])
                        ov = po[h // 4][0:128, :].rearrange("p (a b q) -> p a b q", a=2, b=2)[:, (h % 4) // 2, h % 2, :]
                        ob = [pob[h // 4]]
                        tt("pool", pt[:, 384:384 + nq], pt[:, 384:384 + nq], bdb[:, 1, h, 0:nq], ALU.mult, [ptb, bbd], [ptb])
                        tt("pool", pt[0:nks[4], 512:512 + nq], pt[0:nks[4], 512:512 + nq], bdb[0:nks[4], 0, h, 0:nq], ALU.mult, [ptb, bbd], [ptb])
                        if nq > 64:
                            memset("pool", pt[0:64, 64:128], 0.0, [ptb])
                            memset("pool", pt[64:128, 512:576], 0.0, [ptb])
                        for j in range(5):
                            mm(ov[:, 0:nq], vaug(j, h, 0, nks[j]), pt[0:nks[j], j * 128:j * 128 + nq], j == 0, j == 4, [kbufs[j], ptb], ob)
                pov = [p[:, :].rearrange("p (a b q) -> p a b q", a=2, b=2) for p in po]
                norm_out(F, pov, pob, nq, sg_fn, out_fn)

            def quad(tiles, wt0, full_from, is_sample):
                N = sum(r for _, r, _ in tiles)
                xnT, xnb = xnr.next()
                frontend(F, tiles, xnT, xnb)
                for c in range(4):
                    ps, psb = F["psring"].next()
                    proj(ps, Wp, Wb, O_KB + c * 128, 128, xnT, xnb, 0, N, psb)
                    if is_sample:
                        cp(alt_eng(), kbT_s[:, c, 512:512 + N], ps[:, :N], [psb], [bkvs])
                    else:
                        s0 = (wt0 - 40) % 12
                        cp(alt_eng(), kbT[:, c, s0 * 128:s0 * 128 + N], ps[:, :N], [psb], bslot[s0:s0 + 4])
                want_out = is_sample or wt0 == 60
                for ti, (_, rows, _) in enumerate(tiles):
                    ps, psb = F["psring"].next()
                    for k in range(8):
                        mm(ps[0:rows, :], xnT[:, k, ti * 128:ti * 128 + rows], Wp[:, k, O_VB:O_VB + 512], k == 0, k == 7, [xnb, Wb], [psb])
                    pv = ps[0:rows, :].rearrange("p (h e) -> p h e", h=8)
                    if is_sample:
                        cp("dve", vb_s[0:rows, 4, :, 0:64], pv, [psb], [bkvs])
                    else:
                        s = (wt0 + ti - 40) % 12
                        rec.op("dve", lambda s=s, pv=pv, t=wt0 + ti: nc.vector.tensor_scalar(out=vb[:, s, :, 0:64], in0=pv, scalar1=valid[:, t:t + 1], scalar2=None, op0=ALU.mult), [psb, CONST], [bslot[s]])
                        cp("act", vb[:, s, :, 64:128], valid[:, wt0 + ti:wt0 + ti + 1].unsqueeze(2).to_broadcast([128, 8, 64]), [CONST], [bslot[s]])
                    if want_out:
                        og, ogb = ostg.next()
                        cp("act", og[0:rows, :], ps[0:rows, :], [psb], [ogb])
                        dst = bv_s[448:512, :] if is_sample else bv_p[ti * 128:ti * 128 + 128, :]
                        dma("pool", dst, og[0:rows, :], [ogb], [])
                        ps, psb = F["psring"].next()
                        for k in range(8):
                            mm(ps[0:rows, :], xnT[:, k, ti * 128:ti * 128 + rows], Wp[:, k, O_KB:O_KB + 512], k == 0, k == 7, [xnb, Wb], [psb])
                        og, ogb = ostg.next()
                        cp("act", og[0:rows, :], ps[0:rows, :], [psb], [ogb])
                        dst = bk_s[448:512, :] if is_sample else bk_p[ti * 128:ti * 128 + 128, :]
                        dma("pool", dst, og[0:rows, :], [ogb], [])
                if full_from is None:
                    return
                f0 = full_from
                n = N - f0
                for c in range(4):
                    ps, psb = F["psring"].next()
                    proj(ps, Wp, Wb, O_QB + c * 128, 128, xnT, xnb, f0, n, psb)
                    act(qbq[:, c, 0:n], ps[:, :n], AF.Copy, [psb], [bqbq], scale=B_SCALE)
                    ps, psb = F["psring"].next()
                    proj(ps, Wp, Wb, O_GB + c * 128, 128, xnT, xnb, f0, n, psb)
                    silu_fm(F, ps, psb, n, sgbq[:, c, 0:n], bsgbq)
                if is_sample:
                    def keyT(j, c, pb):
                        return kbT_s[pb:pb + 64, c, j * 128:j * 128 + (128 if j < 4 else 64)]

                    def vaug(j, h, k0, k1):
                        return vb_s[k0:k1, j, h, :]
                    b_attention(64, qbq, bqbq, 0, keyT, vaug, [bkvs] * 5, [128, 128, 128, 128, 64],
                                lambda hf, par: (sgbq[64 * par:64 * par + 64, 2 * hf:2 * hf + 2, 0:64], bsgbq),
                                lambda hf, par: (mixb_s[64 * par:64 * par + 64, 2 * hf:2 * hf + 2, 0:64], bmixbs))
                else:
                    for ti in range(f0 // 128, len(tiles)):
                        t = wt0 + ti
                        qc = ti * 128 - f0
                        oc = (t - T0) * 128
                        slots = [(t - 4 + j - 40) % 12 for j in range(5)]

                        def keyT(j, c, pb, slots=slots):
                            return kbT[pb:pb + 64, c, slots[j] * 128:slots[j] * 128 + 128]

                        def vaug(j, h, k0, k1, slots=slots):
                            return vb[k0:k1, slots[j], h, :]
                        b_attention(128, qbq, bqbq, qc, keyT, vaug, [bslot[s] for s in slots], [128] * 5,
                                    lambda hf, par, qc=qc: (sgbq[64 * par:64 * par + 64, 2 * hf:2 * hf + 2, qc:qc + 128], bsgbq),
                                    lambda hf, par, oc=oc: (mixb[64 * par:64 * par + 64, 2 * hf:2 * hf + 2, oc:oc + 128], bmixb))

            with ExitStack() as es2:
                cst = Ring([sb(es2, f"cst{i}", [128, 512], F32) for i in range(2)], "cst")
                cbf = Ring([sb(es2, f"cbf{i}", [128, 512], BF16) for i in range(2)], "cbf")
                for j in range(4):
                    st_, stb = cst.next()
                    dma("sp", st_[:], cache_bk[j * 128:(j + 1) * 128, :], [], [stb])
                    bf, bfb = cbf.next()
                    cp("dve", bf[:], st_[:], [stb], [bfb])
                    pT, pTb = F["pT"].next()
                    for c in range(4):
                        tr(pT[:, c, :], bf[:, c * 128:(c + 1) * 128], identb[:, :], [bfb, CONST], [pTb])
                    cp("act", kbT_s[:, :, j * 128:(j + 1) * 128], pT[:, 0:4, :], [pTb], [bkvs])
                    st_, stb = cst.next()
                    dma("sp", st_[:], cache_bv[j * 128:(j + 1) * 128, :], [], [stb])
                    cp("dve", vb_s[:, j, :, 0:64], st_[:].rearrange("p (h e) -> p h e", h=8), [stb], [bkvs])
                memset("pool", vb_s[:, :, :, 64:128], 1.0, [bkvs])
                dma("pool", bk_s[0:448, :], cache_bk[64:512, :], [], [])
                dma("pool", bv_s[0:448, :], cache_bv[64:512, :], [], [])
                quad([(xs_in[:, :], 64, None)], None, 0, True)
                rec.flush()
            for q in range(10, 16):
                wt0 = q * 4
                tiles = [(xw[(wt0 + i) * 128:(wt0 + i + 1) * 128, :], 128, None) for i in range(4)]
                full_from = None if q == 10 else (384 if q == 11 else 0)
                quad(tiles, wt0, full_from, False)
            rec.flush()

    def stage_b1():
        with ExitStack() as es:
            Wp = sb(es, "Wb1", [128, 8, 896], BF16); Wb = Buf("Wb1")
            wuq = sb(es, "wuq", [128, 3, 768], BF16); wuqb = Buf("wuq")
            gp = sb(es, "gp1", [128, 8], F32); gq = sb(es, "gq1", [128, 3], F32)
            dma("sp", gp[:], g_pre[:, :], [], [CONST]); dma("sp", gq[:], g_q[:, :], [], [CONST])
            with ExitStack() as es2:
                ring = Ring([sb(es2, f"stgw{i}", [128, 1664], F32) for i in range(4)], "stgw")
                prep_w(ring, Wp[:, :, 0:384], w_in, 8, 3232, gp, Wb, c_lo=0, c_hi=384)
                prep_w(ring, Wp[:, :, 384:896], w_in, 8, 3232, gp, Wb, c_lo=C_GA, c_hi=C_GA + 512)
                prep_w(ring, wuq, w_uq, 3, 768, gq, wuqb)
                rec.flush()
            F = {}
            F["xring"] = Ring([sb(es, f"xr{i}", [128, D], F32) for i in range(4)], "xr")
            F["xsring"] = Ring([sb(es, f"xs{i}", [128, D], BF16) for i in range(3)], "xs")
            F["junk"] = sb(es, "junk", [128, D], BF16); F["junkb"] = Buf("junk")
            F["string"] = Ring([sb(es, f"st{i}", [128, 4], F32) for i in range(4)], "st")
            F["pT"] = Ring([psum(es, f"pT{i}", [128, 8, 128], BF16) for i in range(2)], "pT", psum=True)
            F["psring"] = Ring([psum(es, f"psb1_{i}", [128, 512], F32) for i in range(6)], "psv", psum=True)
            F["t1ring"] = Ring([sb(es, f"t1_{i}", [128, 512], F32) for i in range(3)], "t1")
            F["t2ring"] = Ring([sb(es, f"t2_{i}", [128, 512], F32) for i in range(2)], "t2")
            F["sqring"] = Ring([sb(es, f"sq{i}", [128, 3, 512], BF16) for i in range(2)], "sq")
            F["rsring"] = Ring([sb(es, f"rs{i}", [128, 512], F32) for i in range(2)], "rs")
            F["qrring"] = Ring([sb(es, f"qr{i}", [128, 512], BF16) for i in range(2)], "qr")
            xnr = Ring([sb(es, f"xnT{i}", [128, 8, 512], BF16) for i in range(2)], "xnT")
            qln = sb(es, "qln", [128, 3, 512], BF16); qlnb = Buf("qln")
            tab = Ring([sb(es, f"tab{i}", [128, 2, 512], F32) for i in range(2)], "tab")

            def quad(tiles, f0, cs_ap, sn_ap, QTd, QTb_, qc, sgd, sgb_):
                N = sum(r for _, r, _ in tiles)
                n = N - f0
                xnT, xnb = xnr.next()
                frontend(F, tiles, xnT, xnb)
                tb_, tbb = tab.next()
                dma("pool", tb_[0:96, 0, 0:n], cs_ap, [], [tbb]); dma("pool", tb_[0:96, 1, 0:n], sn_ap, [], [tbb])
                chunks = []
                for c in range(3):
                    ps, psb = F["psring"].next()
                    proj(ps, Wp, Wb, c * 128, 128, xnT, xnb, f0, n, psb)
                    chunks.append((ps, psb))
                rs, rsb = rms_fm(F, chunks, n, 384)
                for c, (ps, psb) in enumerate(chunks):
                    tt("dve", qln[:, c, 0:n], ps[:, :n], rs[:, :n], ALU.mult, [psb, rsb], [qlnb])
                for h in range(8):
                    ps, psb = F["psring"].next()
                    for c in range(3):
                        mm(ps[0:96, 0:n], wuq[:, c, h * 96:(h + 1) * 96], qln[:, c, 0:n], c == 0, c == 2, [wuqb, qlnb], [psb])
                    rope_fm(F, ps, psb, 96, 0, 96, n, pm96b, tb_[0:96, 0, 0:n], tb_[0:96, 1, 0:n], tbb,
                            QTd[0:96, h, qc:qc + n], QTb_)
                for c in range(4):
                    ps, psb = F["psring"].next()
                    proj(ps, Wp, Wb, 384 + c * 128, 128, xnT, xnb, f0, n, psb)
                    silu_fm(F, ps, psb, n, sgd[:, c, qc:qc + n], sgb_)

            quad([(xs_in[:, :], 64, None)], 0, cosq_s[:, :], sinq_s[:, :], QT_s, bQTs, 0, sga_s, bsgas)
            for q in range(11, 16):
                wt0 = q * 4
                tiles = [(xw[(wt0 + i) * 128:(wt0 + i + 1) * 128, :], 128, None) for i in range(4)]
                f0 = 384 if q == 11 else 0
                tc0 = (wt0 - 44) * 128 + f0
                qc = (wt0 - T0) * 128 + f0
                quad(tiles, f0, cosq[:, tc0:tc0 + 512 - f0], sinq[:, tc0:tc0 + 512 - f0], QT, bQT, qc, sga, bsga)
            rec.flush()

    def stage_am():
        with ExitStack() as es:
            cT = sb(es, "cT", [128, 2, 8192], BF16)
            KT = sb(es, "KT", [128, 8192], BF16)
            bcT = [Buf(f"cT{i}") for i in range(16)]
            bKr = [Buf(f"KTr{i}") for i in range(16)]
            bKn = [Buf(f"KTn{i}") for i in range(16)]
            wukv = sb(es, "wukv", [128, 2, 1024], BF16); wukvb = Buf("wukv")
            gkv = sb(es, "gkv", [128, 2], F32)
            dma("sp", gkv[:], g_kv[:, :], [], [CONST])
            with ExitStack() as esA:
                Wl = sb(esA, "Wl", [128, 8, 256], BF16); Wlb = Buf("Wl")
                Wk = sb(esA, "Wk", [128, 8, 96], BF16); Wkb = Buf("Wk")
                gp = sb(esA, "gpA", [128, 8], F32)
                dma("sp", gp[:], g_pre[:, :], [], [CONST])
                memset("pool", Wk[:], 0.0, [Wkb])
                with ExitStack() as es2:
                    ring = Ring([sb(es2, f"stgw{i}", [128, 1664], F32) for i in range(4)], "stgw")
                    prep_w(ring, Wl, w_in, 8, 3232, gp, Wlb, c_lo=C_CKV, c_hi=C_CKV + 256)
                    prep_w(ring, Wk[:, :, 64:96], w_in, 8, 3232, gp, Wkb, c_lo=C_KR, c_hi=C_KR + 32)
                    prep_w(ring, wukv, w_ukv, 2, 1024, None, wukvb)
                    rec.flush()
                F = {}
                F["xring"] = Ring([sb(esA, f"xr{i}", [128, D], F32) for i in range(3)], "xr")
                F["xsring"] = Ring([sb(esA, f"xs{i}", [128, D], BF16) for i in range(3)], "xs")
                F["junk"] = sb(esA, "junk", [128, D], BF16); F["junkb"] = Buf("junk")
                F["string"] = Ring([sb(esA, f"st{i}", [128, 4], F32) for i in range(4)], "st")
                F["pT"] = Ring([psum(esA, f"pT{i}", [128, 8, 128], BF16) for i in range(2)], "pT", psum=True)
                F["psring"] = Ring([psum(esA, f"psa_{i}", [128, 512], F32) for i in range(6)], "psv", psum=True)
                F["t1ring"] = Ring([sb(esA, f"t1_{i}", [128, 512], F32) for i in range(2)], "t1")
                F["t2ring"] = Ring([sb(esA, f"t2_{i}", [128, 512], F32) for i in range(2)], "t2")
                F["sqring"] = Ring([sb(esA, f"sq{i}", [128, 2, 512], BF16) for i in range(2)], "sq")
                F["rsring"] = Ring([sb(esA, f"rs{i}", [128, 512], F32) for i in range(2)], "rs")
                F["qrring"] = Ring([sb(esA, f"qr{i}", [128, 512], BF16) for i in range(2)], "qr")
                xnr = Ring([sb(esA, f"xnT{i}", [128, 8, 512], BF16) for i in range(2)], "xnT")
                tab = Ring([sb(esA, f"tab{i}", [128, 2, 512], F32) for i in range(2)], "tab")
                cf = sb(esA, "cf", [128, 2, 512], F32); cfb = Buf("cf")
                kf = sb(esA, "kf", [128, 512], F32); kfb = Buf("kf")
                ostg = Ring([sb(esA, f"ostgA{i}", [128, 288], F32) for i in range(2)], "ostgA")

                def quadA(tiles, cs_ap, sn_ap, cdst, cb_, kdst, kb_, out_c, out_k):
                    N = sum(r for _, r, _ in tiles)
                    xnT, xnb = xnr.next()
                    frontend(F, tiles, xnT, xnb)
                    tb_, tbb = tab.next()
                    dma("pool", tb_[64:96, 0, 0:N], cs_ap, [], [tbb]); dma("pool", tb_[64:96, 1, 0:N], sn_ap, [], [tbb])
                    chunks = []
                    for c in range(2):
                        ps, psb = F["psring"].next()
                        proj(ps, Wl, Wlb, c * 128, 128, xnT, xnb, 0, N, psb)
                        chunks.append((ps, psb))
                    rs, rsb = rms_fm(F, chunks, N, 256)
                    for c, (ps, psb) in enumerate(chunks):
                        if out_c is not None:
                            stt("dve", cf[:, c, 0:N], ps[:, :N], gkv[:, c:c + 1], rs[:, :N], ALU.mult, ALU.mult, [psb, rsb, CONST], [cfb])
                            cp("pool", cdst[:, c, :], cf[:, c, 0:N], [cfb], [cb_])
                        else:
                            stt("dve", cdst[:, c, :], ps[:, :N], gkv[:, c:c + 1], rs[:, :N], ALU.mult, ALU.mult, [psb, rsb, CONST], [cb_])
                    ps, psb = F["psring"].next()
                    proj(ps, Wk, Wkb, 0, 96, xnT, xnb, 0, N, psb)
                    if out_k is not None:
                        rope_fm(F, ps, psb, 96, 64, 96, N, pm96b, tb_[64:96, 0, 0:N], tb_[64:96, 1, 0:N], tbb, kdst, kb_,
                                out_f32=kf[64:96, 0:N], outfb=kfb)
                    else:
                        rope_fm(F, ps, psb, 96, 64, 96, N, pm96b, tb_[64:96, 0, 0:N], tb_[64:96, 1, 0:N], tbb, kdst, kb_)
                    if out_c is not None:
                        for ti, (_, rows, _) in enumerate(tiles):
                            pso, psob = F["psring"].next()
                            for c in range(2):
                                mm(pso[0:rows, c * 128:(c + 1) * 128], cf[:, c, ti * 128:ti * 128 + rows], identf[:, :], True, True, [cfb, CONST], [psob])
                            mm(pso[0:rows, 256:288], kf[64:96, ti * 128:ti * 128 + rows], identf[64:96, 64:96], True, True, [kfb, CONST], [psob])
                            og, ogb = ostg.next()
                            cp("act", og[0:rows, :], pso[0:rows, 0:288], [psob], [ogb])
                            dma("pool", out_c[ti * 128:ti * 128 + rows, :], og[0:rows, 0:256], [ogb], [])
                            dma("pool", out_k[ti * 128:ti * 128 + rows, :], og[0:rows, 256:288], [ogb], [])

                quadA([(xs_in[:, :], 64, None)], cosk_s[64:96, :], sink_s[64:96, :], cs_new[:, :, 0:64], bcsn,
                      krs_new[64:96, 0:64], bkrsn, ckv_s, kr_s)
                for q in range(16):
                    tiles = [(xw[(q * 4 + i) * 128:(q * 4 + i + 1) * 128, :], 128, None) for i in range(4)]
                    own = q >= 12
                    quadA(tiles, cosk[64:96, q * 512:(q + 1) * 512], sink[64:96, q * 512:(q + 1) * 512],
                          cT[:, :, q * 512:(q + 1) * 512], bcT[q], KT[64:96, q * 512:(q + 1) * 512], bKr[q],
                          ckv_p[(q - 12) * 512:(q - 11) * 512, :] if own else None,
                          kr_p[(q - 12) * 512:(q - 11) * 512, :] if own else None)
                rec.flush()
            if stages < 4:
                return

            def mla(esM, nkb, nk_last, QTd, QTb_, nq, diag0, vld, sg_t, sgb_):
                nacc = (nq + 511) // 512
                accs = [psum(esM, f"acc{i}", [128, 512], F32) for i in range(nacc)]
                accb = [Buf(f"acc{i}", True) for i in range(nacc)]
                sring = Ring([psum(esM, f"psS{i}", [128, 512], F32) for i in range(3)], "psS", psum=True)
                misc = sring
                V4 = sb(esM, "V4", [128, 64, 2, 128], BF16)

                bV = [Buf(f"V4_{i}") for i in range(64)]
                for j_ in range(2):
                    cp("dve" if j_ == 0 else "act", V4[:, 0:nkb, j_, 64:128], vld[:, 0:nkb].unsqueeze(2).to_broadcast([128, nkb, 64]), [CONST], bV[0:nkb])
                ptr = Ring([sb(esM, f"pt{i}", [128, 512], BF16) for i in range(6)], "pt")
                Rr = Ring([sb(esM, f"Rm{i}", [128, 512], F32) for i in range(2)], "Rm")
                tmr = Ring([sb(esM, f"tmm{i}", [128, 512], F32) for i in range(2)], "tmm")
                ngroups = (nq + 511) // 512
                nsb = (nkb + 3) // 4

                def nkeys(kb):
                    return nk_last if kb == nkb - 1 else 128

                def expandK(h, s):
                    ncol = sum(nkeys(kb) for kb in range(4 * s, min(4 * s + 4, nkb)))
                    ps, psb = misc.next()
                    for c in range(2):
                        mm(ps[0:64, 0:ncol], wukv[:, c, h * 128:h * 128 + 64], cT[:, c, s * 512:s * 512 + ncol], c == 0, c == 1, [wukvb, bcT[s]], [psb])
                    cp("dve", KT[0:64, s * 512:s * 512 + ncol], ps[0:64, 0:ncol], [psb], [bKn[s]])

                def expandV(hh):
                    wv = [wukv[:, c, :].rearrange("p (h e) -> p h e", e=128)[:, 2 * hh:2 * hh + 2, 64:128] for c in range(2)]
                    for kb in range(nkb):
                        nk = nkeys(kb)
                        ps, psb = misc.next()
                        pv = ps[0:nk, 0:128].rearrange("p (h e) -> p h e", e=64)
                        for c in range(2):
                            mm(pv, cT[:, c, kb * 128:kb * 128 + nk], wv[c], c == 0, c == 1, [wukvb, bcT[kb // 4]], [psb])
                        rec.op("dve", lambda kb=kb, nk=nk, pv=pv: nc.vector.tensor_scalar(out=V4[0:nk, kb, :, 0:64], in0=pv, scalar1=vld[0:nk, kb:kb + 1], scalar2=None, op0=ALU.mult), [psb, CONST], [bV[kb]])

                def head(h):
                    hj = h % 2
                    work = []
                    for kb in range(nkb):
                        qlo = 0 if diag0 is None else 128 * max(0, kb - diag0)
                        for g in range(ngroups):
                            g0, g1 = g * 512, min(nq, g * 512 + 512)
                            a = max(g0, qlo)
                            if a < g1:
                                work.append((kb, g, a, g1, diag0 is not None and kb >= diag0 and a == qlo))
                    state = {}
                    seen = set()

                    def emitS(i):
                        kb, g, a, b, dg = work[i]
                        nk = nkeys(kb)
                        if kb % 4 == 0 and kb not in seen and kb // 4 + 1 < nsb:
                            expandK(h, kb // 4 + 1)
                        seen.add(kb)
                        ps, psb = sring.next()
                        mm(ps[0:nk, 0:b - a], KT[0:96, kb * 128:kb * 128 + nk], QTd[0:96, h, a:b], True, True, [bKn[kb // 4], bKr[kb // 4], QTb_], [psb])
                        pt, ptb = ptr.next()
                        act(pt[0:nk, 0:b - a], ps[0:nk, 0:b - a], AF.Exp, [psb], [ptb])
                        state[i] = (pt, ptb)

                    def emitPV(i):
                        kb, g, a, b, dg = work[i]
                        nk = nkeys(kb)
                        pt, ptb = state.pop(i)
                        acc = accs[g]; g0 = g * 512
                        first = kb == 0
                        last = kb == nkb - 1
                        if dg:
                            mm(acc[0:128, a - g0:a - g0 + 64], V4[0:64, kb, hj, :], pt[0:64, 0:64], False, True, [bV[kb], ptb], [accb[g]], sgc=True)
                            mm(acc[0:128, a - g0 + 64:a - g0 + 128], V4[:, kb, hj, :], pt[:, 64:128], False, True, [bV[kb], ptb], [accb[g]], sgc=True)
                            if b > a + 128:
                                mm(acc[0:128, a - g0 + 128:b - g0], V4[:, kb, hj, :], pt[:, 128:b - a], False, False, [bV[kb], ptb], [accb[g]], sgc=True)
                        else:
                            mm(acc[0:128, a - g0:b - g0], V4[0:nk, kb, hj, :], pt[0:nk, 0:b - a], first, last, [bV[kb], ptb], [accb[g]], sgc=True)

                    c, pb = h // 2, (h % 2) * 64
                    lastkb = {}
                    for (kb_, g_, a_, b_, dg_) in work:
                        lastkb[g_] = kb_

                    def norm_group(g):
                        g0, g1 = g * 512, min(nq, g * 512 + 512)
                        n = g1 - g0
                        Rs, Rsb_ = Rr.next()
                        act(Rs[pb:pb + 64, 0:n], accs[g][64:128, 0:n], AF.Ln, [accb[g]], [Rsb_], bias=1e-30)
                        act(Rs[pb:pb + 64, 0:n], Rs[pb:pb + 64, 0:n], AF.Exp, [Rsb_], [Rsb_], scale=-1.0)
                        tm, tmb = tmr.next()
                        tt("dve", tm[pb:pb + 64, 0:n], Rs[pb:pb + 64, 0:n], sg_t[pb:pb + 64, c, g0:g1], ALU.mult, [Rsb_, sgb_], [tmb])
                        tt("dve", sg_t[pb:pb + 64, c, g0:g1], accs[g][0:64, 0:n], tm[pb:pb + 64, 0:n], ALU.mult, [accb[g], tmb], [sgb_])

                    expandK(h, 0)
                    for i in range(len(work) + 1):
                        if i < len(work):
                            emitS(i)
                        if i >= 1:
                            emitPV(i - 1)
                            kb_, g_ = work[i - 1][0], work[i - 1][1]
                            if lastkb[g_] == kb_ and (i == len(work) or work[i][0] != kb_ or work[i][1] != g_):
                                norm_group(g_)

                for hh in range(4):
                    expandV(hh)
                    for h in range(2 * hh, 2 * hh + 2):
                        head(h)

            with ExitStack() as esM:
                mla(esM, 64, 128, QT, bQT, NQC, T0, valid, sga, bsga)
                rec.flush()
            with ExitStack() as esS:
                pTs = Ring([psum(esS, "pTs0", [128, 8, 128], BF16)], "pTs", psum=True)
                cst = Ring([sb(esS, f"ccst{i}", [128, 256], F32) for i in range(2)], "ccst")
                cbf = Ring([sb(esS, f"ccbf{i}", [128, 256], BF16) for i in range(2)], "ccbf")
                kst = Ring([sb(esS, f"kst{i}", [128, 32], F32) for i in range(2)], "kst")
                kbf = Ring([sb(esS, f"kbf{i}", [128, 96], BF16) for i in range(2)], "kbf")
                for r_ in kbf.tiles:
                    memset("pool", r_[:], 0.0, kbf.bufs)
                for t in range(32):
                    st_, stb = cst.next()
                    dma("sp", st_[:], cache_ckv[t * 128:(t + 1) * 128, :], [], [stb])
                    bf, bfb = cbf.next()
                    cp(alt_eng(("dve", "pool")), bf[:], st_[:], [stb], [bfb])
                    ks, ksb = kst.next()
                    dma("sp", ks[:], cache_kr[t * 128:(t + 1) * 128, :], [], [ksb])
                    kb_, kbb = kbf.next()
                    cp("pool", kb_[:, 64:96], ks[:], [ksb], [kbb])
                    pT, pTb = pTs.next()
                    for c in range(2):
                        tr(pT[:, c, :], bf[:, c * 128:(c + 1) * 128], identb[:, :], [bfb, CONST], [pTb])
                    tr(pT[0:96, 2, :], kb_[:, 0:96], identb[:, :], [kbb, CONST], [pTb])
                    cp("act", cT[:, :, t * 128:(t + 1) * 128], pT[:, 0:2, :], [pTb], [bcT[t // 4]])
                    cp("dve", KT[64:96, t * 128:(t + 1) * 128], pT[64:96, 2, :], [pTb], [bKr[t // 4]])
                cp("dve", cT[:, :, 4096:4160], cs_new[:, :, 0:64], [bcsn], [bcT[8]])
                cp("dve", KT[64:96, 4096:4160], krs_new[64:96, 0:64], [bkrsn], [bKr[8]])
                with ExitStack() as esM:
                    mla(esM, 33, 64, QT_s, bQTs, 64, None, vone, sga_s, bsgas)
                    rec.flush()

    def make_out_proj(F, psY, tmpr):
        def out_proj(lhs_fn, lhsb, W, Wb_, rows, gb, res, resb, dst, dstb):
            ps, psb = psY.next()
            for hf in range(2):
                for kk in range(8):
                    mm(ps[0:rows, hf * 512:(hf + 1) * 512], lhs_fn(kk), W[:, kk, hf * 512:(hf + 1) * 512], kk == 0, kk == 7, lhsb + [Wb_], [psb])
            st_, stb = F["string"].next()
            tm, tmb = tmpr.next()
            act(tm[:rows, 0:512], ps[0:rows, 0:512], AF.Square, [psb], [tmb, stb], accum_out=st_[:rows, 0:1])
            act(tm[:rows, 512:1024], ps[0:rows, 512:1024], AF.Square, [psb], [tmb, stb], accum_out=st_[:rows, 3:4])
            tt("dve", st_[:rows, 0:1], st_[:rows, 0:1], st_[:rows, 3:4], ALU.add, [stb], [stb])
            act(st_[:rows, 1:2], st_[:rows, 0:1], AF.Ln, [stb], [stb], scale=1.0 / D, bias=EPS)
            act(st_[:rows, 2:3], st_[:rows, 1:2], AF.Exp, [stb], [stb], scale=-0.5)
            for hf in range(2):
                stt("dve", tm[0:rows, hf * 512:(hf + 1) * 512], ps[0:rows, hf * 512:(hf + 1) * 512], st_[:rows, 2:3], gb[0:rows, hf * 512:(hf + 1) * 512], ALU.mult, ALU.mult, [psb, stb, CONST], [tmb])
            tt("dve", dst, tm[0:rows, :], res, ALU.add, [tmb, resb], [dstb])

        return out_proj

    def stage_c0():
        with ExitStack() as es:
            wo0 = sb(es, "wo0", [128, 8, D], BF16); wo0b = Buf("wo0")
            gb0 = sb(es, "gb0", [128, D], F32)
            dma("sp", gb0[:], gb_post0[:, :], [], [CONST])
            with ExitStack() as es2:
                ring = Ring([sb(es2, f"stgw{i}", [128, 1664], F32) for i in range(4)], "stgw")
                prep_w(ring, wo0, w_out0, 8, D, None, wo0b)
                rec.flush()
            F = {}
            F["xring"] = Ring([sb(es, f"xr{i}", [128, D], F32) for i in range(4)], "xr")
            F["string"] = Ring([sb(es, f"st{i}", [128, 4], F32) for i in range(4)], "st")
            psY = Ring([psum(es, f"psY{i}", [128, 1024], F32) for i in range(3)], "psY", psum=True)
            tmpr = Ring([sb(es, f"tmpc{i}", [128, D], F32) for i in range(3)], "tmpc")
            h1r = Ring([sb(es, f"h1_{i}", [128, D], F32) for i in range(4)], "h1")
            out_proj = make_out_proj(F, psY, tmpr)
            for t in [None] + list(range(T0, 64)):
                is_sample = t is None
                rows = 64 if is_sample else 128
                oc = 0 if is_sample else (t - T0) * 128
                A_, B_ = (sga_s, mixb_s) if is_sample else (sga, mixb)
                xt, xb = F["xring"].next()
                dma("sp", xt[:rows, :], xs_in[:, :] if is_sample else xw[t * 128:(t + 1) * 128, :], [], [xb])
                h1, h1b = h1r.next()
                out_proj(lambda kk, oc=oc, rows=rows, A_=A_, B_=B_: (A_[:, kk, oc:oc + rows] if kk < 4 else B_[:, kk - 4, oc:oc + rows]),
                         [bsgas, bmixbs] if is_sample else [bsga, bmixb], wo0, wo0b, rows, gb0, xt[:rows, :], xb, h1[:rows, :], h1b)
                r0 = NQC if is_sample else oc
                dma("pool", h1d[r0:r0 + rows, :], h1[:rows, :], [h1b], [])
            rec.flush()

    def stage_c():
        with ExitStack() as es:
            wc = sb(es, "wc", [128, 8, 2304], BF16); wcb = Buf("wc")
            wo1 = sb(es, "wo1", [128, 8, D], BF16); wo1b = Buf("wo1")
            gp1 = sb(es, "gp1c", [128, 8], F32)
            gb1 = sb(es, "gb1", [128, D], F32)
            esk = sb(es, "esk", [128, 16], F32)
            dma("sp", gp1[:], g_pre1[:, :], [], [CONST])
            dma("sp", gb1[:], gb_post1[:, :], [], [CONST]); dma("sp", esk[:], sinks_d[:, :], [], [CONST])
            act(esk[:], esk[:], AF.Exp, [CONST], [CONST])
            with ExitStack() as es2:
                ring = Ring([sb(es2, f"stgw{i}", [128, 1664], F32) for i in range(4)], "stgw")
                prep_w(ring, wc, c_w_in, 8, 2304, gp1, wcb)
                prep_w(ring, wo1, c_w_out, 8, D, None, wo1b)
                rec.flush()
            F = {}
            F["xsring"] = Ring([sb(es, f"xs{i}", [128, D], BF16) for i in range(2)], "xs")
            F["string"] = Ring([sb(es, f"st{i}", [128, 4], F32) for i in range(4)], "st")
            F["pT"] = Ring([psum(es, "pT0", [128, 8, 128], BF16)], "pT", psum=True)
            psY = Ring([psum(es, f"psY{i}", [128, 1024], F32) for i in range(1)], "psY", psum=True)
            F["psring"] = Ring([psum(es, f"psc_{i}", [128, 512], F32) for i in range(5)], "psv", psum=True)
            F["t1ring"] = Ring([sb(es, f"t1_{i}", [128, 512], F32) for i in range(2)], "t1")
            F["t2ring"] = Ring([sb(es, f"t2_{i}", [128, 512], F32) for i in range(2)], "t2")
            F["qrring"] = Ring([sb(es, f"qr{i}", [128, 512], BF16) for i in range(2)], "qr")
            h1r = Ring([sb(es, f"h1_{i}", [128, D], F32) for i in range(6)], "h1")
            tmpr = Ring([sb(es, f"tmpc{i}", [128, D], F32) for i in range(2)], "tmpc")
            xn1r = Ring([sb(es, f"xn1T{i}", [128, 8, 512], BF16) for i in range(2)], "xn1T")
            q1r = Ring([sb(es, f"q1T{i}", [128, 8, 512], BF16) for i in range(2)], "q1T")
            cur = {}
            sg1r = Ring([sb(es, f"sg1_{i}", [128, 8, 512], BF16) for i in range(2)], "sg1")
            mx1 = sb(es, "mx1", [128, 8, 512], BF16); mx1b = Buf("mx1")
            tab = Ring([sb(es, f"tab{i}", [128, 4, 512], F32) for i in range(1)], "tab")
            k1f = sb(es, "k1f", [128, 512], F32); k1fb = Buf("k1f")
            k1h = sb(es, "k1h", [128, 512], BF16); k1hb = Buf("k1h")
            K2 = sb(es, "K2", [128, 2, NQC], BF16); V1 = sb(es, "V1", [128, NT, 2, 128], BF16)
            bkv = [Buf(f"kv1_{i}") for i in range(NT)]
            K2s = sb(es, "K2s", [128, 2, 192], BF16); V1s = sb(es, "V1s", [128, 2, 2, 128], BF16); bkvs = Buf("kv1s")
            ptr = Ring([sb(es, f"ptc{i}", [128, 2, 512], BF16) for i in range(3)], "ptc")
            Rr = Ring([sb(es, f"Rc{i}", [128, 512], F32) for i in range(2)], "Rc")
            tmr = Ring([sb(es, f"tmc{i}", [128, 512], F32) for i in range(1)], "tmc")
            ostg = Ring([sb(es, f"ostgc{i}", [128, 128], F32) for i in range(2)], "ostgc")
            out_proj = make_out_proj(F, psY, tmpr)

            def attn_tile(nq, qc, kprev, kown, nk_own, vprev, vown, kvbufs, eoff):
                q1T, q1b = cur["q1T"]
                sg1, sg1b = cur["sg1"]
                for g in range(2):
                    for par in range(2):
                        pb = 64 * par
                        pt, ptb = ptr.next()
                        for ki, (kfn, nk) in enumerate(((kprev, 128), (kown, nk_own))):
                            ps, psb = F["psring"].next()
                            pv = ps[0:nk, 0:4 * nq].rearrange("p (j q) -> p j q", j=4)
                            mm(pv, kfn(g, par), q1T[pb:pb + 64, 4 * g:4 * g + 4, qc:qc + nq], True, True, [kvbufs[ki], q1b], [psb])
                            act(pt[0:nk, ki, 0:4 * nq].rearrange("p (j q) -> p j q", j=4), pv, AF.Exp, [psb], [ptb])
                        p0v = pt[:, 0, 0:4 * nq].rearrange("p (j q) -> p j q", j=4)
                        p1v = pt[:, 1, 0:4 * nq].rearrange("p (j q) -> p j q", j=4)
                        acc, accb_ = F["psring"].next()
                        av = acc[0:128, 0:4 * nq].rearrange("p (j q) -> p j q", j=4)
                        if nq > 64:
                            memset("pool", p0v[0:64, :, 64:128], 0.0, [ptb])
                            memset("pool", p1v[64:128, :, 0:64], 0.0, [ptb])
                        mm(av[:, :, 0:nq], vprev(g, 0, 128), p0v[:, :, 0:nq], True, False, [kvbufs[0], ptb], [accb_])
                        mm(av[:, :, 0:nq], vown(g, 0, nk_own), p1v[0:nk_own, :, 0:nq], False, True, [kvbufs[1], ptb], [accb_])
                        Rs, Rsb_ = Rr.next()
                        Rv = Rs[0:128, 0:4 * nq].rearrange("p (j q) -> p j q", j=4)
                        for j in range(4):
                            h = 8 * g + 2 * j + par
                            act(Rv[pb:pb + 64, j, :], av[64:128, j, 0:nq], AF.Ln, [accb_, CONST], [Rsb_], bias=esk[64:128, h:h + 1])
                        act(Rv[pb:pb + 64], Rv[pb:pb + 64], AF.Exp, [Rsb_], [Rsb_], scale=-1.0)
                        tm, tmb = tmr.next()
                        tv = tm[pb:pb + 64, 0:4 * nq].rearrange("p (j q) -> p j q", j=4)
                        tt("dve", tv, Rv[pb:pb + 64], sg1[pb:pb + 64, 4 * g:4 * g + 4, qc:qc + nq], ALU.mult, [Rsb_, sg1b], [tmb])
                        tt("dve", mx1[pb:pb + 64, 4 * g:4 * g + 4, qc:qc + nq], av[0:64, :, 0:nq], tv, ALU.mult, [accb_, tmb], [mx1b])

            def group(tl, is_sample):
                N = sum(r for _, r in tl)
                h1s = []
                for ti, (t, rows) in enumerate(tl):
                    oc = 0 if is_sample else (t - T0) * 128
                    A_, B_ = (sga_s, mixb_s) if is_sample else (sga, mixb)
                    h1, h1b = h1r.next()
                    r0 = NQC if is_sample else oc
                    dma("sp", h1[:rows, :], h1d[r0:r0 + rows, :], [], [h1b])
                    h1s.append((h1, rows, h1b))
                xn1, xn1b = xn1r.next()
                q1T, q1b = q1r.next()
                cur["q1T"] = (q1T, q1b)
                sg1, sg1b = sg1r.next()
                cur["sg1"] = (sg1, sg1b)
                frontend(F, [(h1[:, :], rows, hb) for h1, rows, hb in h1s], xn1, xn1b)
                tb_, tbb = tab.next()
                c0 = 0 if is_sample else (tl[0][0] - T0) * 128
                srcs = (cos1q_s, sin1q_s, cos1k_s, sin1k_s) if is_sample else (cos1q, sin1q, cos1k, sin1k)
                for i_, s_ in enumerate(srcs):
                    dma("pool", tb_[:, i_, 0:N], s_[:, c0:c0 + N], [], [tbb])
                for c in range(8):
                    ps, psb = F["psring"].next()
                    proj(ps, wc, wcb, C1_Q + c * 128, 128, xn1, xn1b, 0, N, psb)
                    rope_fm(F, ps, psb, 128, 0, 128, N, pm128b, tb_[:, 0, 0:N], tb_[:, 1, 0:N], tbb, q1T[:, c, 0:N], q1b)
                    ps, psb = F["psring"].next()
                    proj(ps, wc, wcb, C1_G + c * 128, 128, xn1, xn1b, 0, N, psb)
                    silu_fm(F, ps, psb, N, sg1[:, c, 0:N], sg1b)
                ps, psb = F["psring"].next()
                proj(ps, wc, wcb, C1_K, 128, xn1, xn1b, 0, N, psb)
                rope_fm(F, ps, psb, 128, 0, 128, N, pm128b, tb_[:, 2, 0:N], tb_[:, 3, 0:N], tbb, k1h[:, 0:N], k1hb, out_f32=k1f[:, 0:N], outfb=k1fb)
                col = 0
                for ti, (t, rows) in enumerate(tl):
                    if is_sample:
                        Kd, kc, kb_ = K2s, 128, bkvs
                    else:
                        Kd, kc, kb_ = K2, (t - T0) * 128, bkv[t - T0]
                    for g in range(2):
                        for par in range(2):
                            cp("dve", Kd[64 * par:64 * par + 64, g, kc:kc + rows], k1h[64 * g:64 * g + 64, col:col + rows], [k1hb], [kb_])
                    ps, psb = F["psring"].next()
                    for k in range(8):
                        mm(ps[0:rows, 0:128], xn1[:, k, col:col + rows], wc[:, k, C1_V:C1_V + 128], k == 0, k == 7, [xn1b, wcb], [psb])
                    pv = ps[0:rows, 0:128].rearrange("p (g e) -> p g e", g=2)
                    if is_sample:
                        cp("dve", V1s[0:rows, 1, :, 0:64], pv, [psb], [kb_])
                    else:
                        rec.op("dve", lambda t=t, pv=pv: nc.vector.tensor_scalar(out=V1[:, t - T0, :, 0:64], in0=pv, scalar1=valid[:, t:t + 1], scalar2=None, op0=ALU.mult), [psb, CONST], [kb_])
                        cp("act", V1[:, t - T0, :, 64:128], valid[:, t:t + 1].unsqueeze(2).to_broadcast([128, 2, 64]), [CONST], [kb_])
                    if is_sample or t == 63:
                        og, ogb = ostg.next()
                        cp("act", og[0:rows, :], ps[0:rows, 0:128], [psb], [ogb])
                        dma("pool", cv_s[64:128, :] if is_sample else cv_p[:, :], og[0:rows, :], [ogb], [])
                        ps, psb = F["psring"].next()
                        mm(ps[0:rows, 0:128], k1f[:, col:col + rows], identf[:, :], True, True, [k1fb, CONST], [psb])
                        og, ogb = ostg.next()
                        cp("act", og[0:rows, :], ps[0:rows, 0:128], [psb], [ogb])
                        dma("pool", ck_s[64:128, :] if is_sample else ck_p[:, :], og[0:rows, :], [ogb], [])
                    col += rows
                col = 0
                for ti, (t, rows) in enumerate(tl):
                    if is_sample:
                        attn_tile(64, 0,
                                  lambda g, par: K2s[64 * par:64 * par + 64, g, 0:128], lambda g, par: K2s[64 * par:64 * par + 64, g, 128:192], 64,
                                  lambda g, k0, k1: V1s[k0:k1, 0, g, :], lambda g, k0, k1: V1s[k0:k1, 1, g, :], [bkvs, bkvs], 0)
                    elif t > T0:
                        j = t - T0
                        attn_tile(128, col,
                                  lambda g, par, j=j: K2[64 * par:64 * par + 64, g, (j - 1) * 128:j * 128],
                                  lambda g, par, j=j: K2[64 * par:64 * par + 64, g, j * 128:(j + 1) * 128], 128,
                                  lambda g, k0, k1, j=j: V1[k0:k1, j - 1, g, :], lambda g, k0, k1, j=j: V1[k0:k1, j, g, :],
                                  [bkv[j - 1], bkv[j]], 0)
                    col += rows
                col = 0
                for ti, (t, rows) in enumerate(tl):
                    if is_sample or t > T0:
                        h1, _, h1b = h1s[ti]
                        y, yb = tmpr.next()
                        out_proj(lambda kk, col=col, rows=rows: mx1[:, kk, col:col + rows], [mx1b], wo1, wo1b, rows, gb1, h1[:rows, :], h1b, y[:rows, :], yb)
                        dma("sp", y_s[:, :] if is_sample else y_p[(t - 48) * 128:(t - 47) * 128, :], y[:rows, :], [yb], [])
                    col += rows

            with ExitStack() as es2:
                cs_ = tmpr.tiles[0][:, 0:256].rearrange("p (a b) -> p a b", a=2); bb_ = tmpr.bufs[0]
                cb2 = F["xsring"].tiles[0][:, 0:128]; cbb = F["xsring"].bufs[0]
                dma("sp", cs_[:, 0, :], cache_ck[:, :], [], [bb_]); dma("sp", cs_[:, 1, :], cache_cv[:, :], [], [bb_])
                cp("dve", cb2, cs_[:, 0, :], [bb_], [cbb])
                pT, pTb = F["pT"].next()
                tr(pT[:, 0, :], cb2, identb[:, :], [cbb, CONST], [pTb])
                for g in range(2):
                    for par in range(2):
                        cp("dve", K2s[64 * par:64 * par + 64, g, 0:128], pT[64 * g:64 * g + 64, 0, :], [pTb], [bkvs])
                cp("dve", V1s[:, 0, :, 0:64], cs_[:, 1, :].rearrange("p (g e) -> p g e", g=2), [bb_], [bkvs])
                memset("pool", V1s[:, :, :, 64:128], 1.0, [bkvs])
                dma("pool", ck_s[0:64, :], cache_ck[64:128, :], [], [])
                dma("pool", cv_s[0:64, :], cache_cv[64:128, :], [], [])
                rec.flush()
            group([(None, 64)], True)
            group([(T0, 128)], False)
            for q in range(12, 16):
                group([(q * 4 + i, 128) for i in range(4)], False)
            rec.flush()

    with ExitStack() as esP:
        sga = sb(esP, "sga", [128, 4, NQC], BF16)
        mixb = sb(esP, "mixb", [128, 4, NQC], BF16)
        stage_b2()
        with ExitStack() as esQ:
            QT = sb(esQ, "QT", [128, 8, NQC], BF16)
            if stages >= 2:
                stage_b1()
            if stages >= 3:
                stage_am()
        if stages >= 5:
            stage_c0()
    if stages >= 5:
        stage_c()
    rec.barrier()
    top.close()
    return nc, rec


_PROG = {}


def _get_prog(stages):
    if stages not in _PROG:
        _PROG[stages] = build_program(stages)
    return _PROG[stages]


def kernel(x_prompt, x_sample, cache_a_ckv, cache_a_krope, cache_b_k, cache_b_v, cache_c_k, cache_c_v,
           ab_pre_norm, ab_post_norm, ab_w_in, ab_q_norm, ab_kv_norm, ab_w_uq, ab_w_ukv, ab_rel_bias, ab_w_out,
           c_pre_norm, c_post_norm, c_w_in, c_sinks, c_w_out):
    stages = int(os.environ.get("KSTAGES", "99"))
    f = lambda a: np.ascontiguousarray(np.asarray(a, np.float32))
    x_prompt = f(x_prompt); x_sample = f(x_sample)
    shared = {
        "w_in": _wl(ab_w_in[0], 8), "g_pre": _pk(ab_pre_norm[0], 8),
        "w_uq": _wl(ab_w_uq[0], 3), "g_q": _pk(ab_q_norm[0], 3),
        "w_ukv": _wl(ab_w_ukv[0], 2), "g_kv": _pk(ab_kv_norm[0], 2),
        "w_out0": _wl(ab_w_out[0], 8), "gb_post0": f(np.broadcast_to(np.asarray(ab_post_norm[0], np.float32)[None, :], (128, D))),
        "c_w_in": _wl(c_w_in[0], 8), "g_pre1": _pk(c_pre_norm[0], 8),
        "c_w_out": _wl(c_w_out[0], 8), "gb_post1": f(np.broadcast_to(np.asarray(c_post_norm[0], np.float32)[None, :], (128, D))),
        "pm96": _perm_lhsT(96, [(64, 32)]), "pm128": _perm_lhsT(128, [(0, 16), (64, 16)]),
        "ident": np.eye(128, dtype=np.float32),
        "sinks": f(np.broadcast_to(np.asarray(c_sinks[0], np.float32)[None, :], (128, 16))),
    }
    tbl = np.asarray(ab_rel_bias[0], np.float32)
    kk = np.arange(128)[:, None]; qq = np.arange(128)[None, :]
    bd = np.zeros((128, 2, 8, 128), np.float32)
    for d_ in range(2):
        idx = np.clip(128 * d_ + qq - kk, -128, 128) + 128
        bd[:, d_, :, :] = np.transpose(tbl[:, idx], (1, 0, 2))
    shared["bd"] = bd
    shared["cb"] = f(np.broadcast_to(tbl[None, :, 256], (128, 8)))
    pos_s = PAST + np.arange(64)
    shared["cosk_s"], shared["sink_s"] = _rope_tables_fm(pos_s, 32, 96, [64], 1.0)
    shared["cosq_s"], shared["sinq_s"] = _rope_tables_fm(pos_s, 32, 96, [64], A_SCALE)
    shared["cos1q_s"], shared["sin1q_s"] = _rope_tables_fm(pos_s, 16, 128, [0, 64], C_SCALE)
    shared["cos1k_s"], shared["sin1k_s"] = _rope_tables_fm(pos_s, 16, 128, [0, 64], 1.0)
    tabs = {}
    in_maps = []
    for core in range(8):
        b, c = core // 4, core % 4
        end = 2048 * (c + 1)
        start = end - 8192
        m = dict(shared)
        xw = np.zeros((8192, D), np.float32)
        lo = max(0, -start)
        xw[lo:] = x_prompt[b, start + lo:end]
        m["xw"] = xw
        m["xs_in"] = f(x_sample[core])
        if c not in tabs:
            pos = start + np.arange(8192)
            t = {}
            t["cosk"], t["sink"] = _rope_tables_fm(pos, 32, 96, [64], 1.0)
            t["cosq"], t["sinq"] = _rope_tables_fm(pos[5632:], 32, 96, [64], A_SCALE)
            t["cos1q"], t["sin1q"] = _rope_tables_fm(pos[6016:], 16, 128, [0, 64], C_SCALE)
            t["cos1k"], t["sin1k"] = _rope_tables_fm(pos[6016:], 16, 128, [0, 64], 1.0)
            t["valid"] = f((pos >= 0).astype(np.float32).reshape(64, 128).T)
            tabs[c] = t
        m.update(tabs[c])
        m["cache_ckv"] = f(cache_a_ckv[0, core]); m["cache_kr"] = f(cache_a_krope[0, core])
        m["cache_bk"] = f(np.asarray(cache_b_k[0, core]).reshape(512, 512)); m["cache_bv"] = f(np.asarray(cache_b_v[0, core]).reshape(512, 512))
        m["cache_ck"] = f(np.asarray(cache_c_k[0, core]).reshape(128, 128)); m["cache_cv"] = f(np.asarray(cache_c_v[0, core]).reshape(128, 128))
        in_maps.append(m)
    nc, _ = _get_prog(stages)
    res = run_bass_kernel_spmd(nc, in_maps, core_ids=list(range(8)))
    R = res.results
    y_prompt = np.zeros((2, SEQ, D), np.float32); y_sample = np.zeros((8, 64, D), np.float32)
    a_ckv_p = np.zeros((1, 2, SEQ, 256), np.float32); a_kr_p = np.zeros((1, 2, SEQ, 32), np.float32)
    b_k_p = np.zeros((1, 2, 512, 8, 64), np.float32); b_v_p = np.zeros((1, 2, 512, 8, 64), np.float32)
    c_k_p = np.zeros((1, 2, 128, 2, 64), np.float32); c_v_p = np.zeros((1, 2, 128, 2, 64), np.float32)
    a_ckv_s = np.zeros((1, 8, 64, 256), np.float32); a_kr_s = np.zeros((1, 8, 64, 32), np.float32)
    b_k_s = np.zeros((1, 8, 512, 8, 64), np.float32); b_v_s = np.zeros((1, 8, 512, 8, 64), np.float32)
    c_k_s = np.zeros((1, 8, 128, 2, 64), np.float32); c_v_s = np.zeros((1, 8, 128, 2, 64), np.float32)
    for core in range(8):
        b, c = core // 4, core % 4
        r = R[core]
        sl = slice(2048 * c, 2048 * (c + 1))
        y_prompt[b, sl] = r["y_p"]; y_sample[core] = r["y_s"]
        a_ckv_p[0, b, sl] = r["ckv_p"]; a_kr_p[0, b, sl] = r["kr_p"]
        if c == 3:
            b_k_p[0, b] = r["bk_p"].reshape(512, 8, 64); b_v_p[0, b] = r["bv_p"].reshape(512, 8, 64)
            c_k_p[0, b] = r["ck_p"].reshape(128, 2, 64); c_v_p[0, b] = r["cv_p"].reshape(128, 2, 64)
        a_ckv_s[0, core] = r["ckv_s"]; a_kr_s[0, core] = r["kr_s"]
        b_k_s[0, core] = r["bk_s"].reshape(512, 8, 64); b_v_s[0, core] = r["bv_s"].reshape(512, 8, 64)
        c_k_s[0, core] = r["ck_s"].reshape(128, 2, 64); c_v_s[0, core] = r["cv_s"].reshape(128, 2, 64)
    return (y_prompt, y_sample, a_ckv_p, a_kr_p, b_k_p, b_v_p, c_k_p, c_v_p,
            a_ckv_s, a_kr_s, b_k_s, b_v_s, c_k_s, c_v_s)
```
